# Optimizing a Trainium2 kernel written in Bass

```python
import jax, jax.numpy as jnp
from jax import lax
import numpy as np

D_MODEL = 1024
BATCH = 32
SEQ = 256
DEPTH = 2
DEC_BATCH = 8
DEC_SEQ = 2048
PAST_LEN = 256

GRID_W = 64
D_MIX = 1024
EPS = 1e-6
ROPE_THETA = 10000.0
NEG_INF = -1e30
N_MOD = 9
D_FF = 2816
QBLOCK = 128
H_A = 4
KV_A = 2
G_A = H_A // KV_A
HD_A = 64
WINDOW_A = 128
BLOCK_A = 128
H_B = 4
DK_B = 64
DV_B = 64
CHUNK_B = 16
H_C = 4
Q_LORA_C = 256
KV_LORA_C = 128
NOPE_C = 64
ROPE_C = 32
V_C = 64
POOL_WINDOWS = (2, 4, 8, 16)
POOL_GROUPS = 4
POOL_CH = 64

SPLIT_SIZES = (H_A * HD_A, KV_A * HD_A, KV_A * HD_A,
               H_B * DK_B, H_B * DV_B, H_B * DK_B, H_B * DK_B, H_B * DV_B,
               Q_LORA_C, KV_LORA_C, ROPE_C,
               POOL_GROUPS * POOL_CH)
SPLIT_POINTS = tuple(int(s) for s in np.cumsum(SPLIT_SIZES)[:-1])
D_IN = int(sum(SPLIT_SIZES))

kernel_name = "hybrid_prefix_diffusion_trunk_step"


def rms_norm(x, g):
    xf = x.astype(jnp.float32)
    y = xf * lax.rsqrt(jnp.mean(xf * xf, axis=-1, keepdims=True) + EPS)
    return (y * g.astype(jnp.float32)).astype(x.dtype)


def swiglu(h, wg, wu, wd):
    return (jax.nn.silu(h @ wg) * (h @ wu)) @ wd


def modulation(cvec, w_ada, b_ada):
    m = jax.nn.silu(cvec) @ w_ada + b_ada
    return m.reshape(cvec.shape[0], 1, N_MOD, D_MODEL)


def axial_rope_tables(rows, dim):
    n_freq = dim // 4
    inv = ROPE_THETA ** (-jnp.arange(n_freq, dtype=jnp.float32) / n_freq)
    row_id = jnp.repeat(jnp.arange(rows, dtype=jnp.float32), GRID_W)
    col_id = jnp.tile(jnp.arange(GRID_W, dtype=jnp.float32), rows)
    ang = jnp.concatenate([row_id[:, None] * inv, col_id[:, None] * inv], axis=-1)
    return jnp.cos(ang), jnp.sin(ang)


def apply_rope(x, cos, sin):
    half = x.shape[-1] // 2
    xf = x.astype(jnp.float32)
    x1, x2 = xf[..., :half], xf[..., half:]
    return jnp.concatenate([x1 * cos - x2 * sin, x1 * sin + x2 * cos], axis=-1).astype(x.dtype)


def sink_softmax(logits, sink):
    full = jnp.concatenate([logits, jnp.broadcast_to(sink, logits.shape[:-1] + (1,))], axis=-1)
    return jax.nn.softmax(full, axis=-1)[..., :-1]


def block_attention(q, k, v, sink=None):
    bq, hk, g, t, d = q.shape
    nb = t // QBLOCK
    scale = d ** -0.5
    qb = jnp.moveaxis(q.reshape(bq, hk, g, nb, QBLOCK, d), 3, 0)

    def attend(qblk):
        s = jnp.einsum('bkgqd,bksd->bkgqs', qblk, k).astype(jnp.float32) * scale
        p = jax.nn.softmax(s, axis=-1) if sink is None else sink_softmax(s, sink[None, :, :, None, None])
        return jnp.einsum('bkgqs,bksd->bkgqd', p.astype(v.dtype), v)

    o = lax.map(attend, qb)
    return jnp.moveaxis(o, 0, 3).reshape(bq, hk, g, t, v.shape[-1])


def window_attention(q, k, v, kc, vc, sink):
    bq, t = q.shape[:2]
    nb = t // BLOCK_A
    scale = HD_A ** -0.5
    qb = q.reshape(bq, nb, BLOCK_A, KV_A, G_A, HD_A)

    def band(x):
        xp = jnp.pad(x, ((0, 0), (BLOCK_A, BLOCK_A), (0, 0), (0, 0))).reshape(bq, nb + 2, BLOCK_A, KV_A, HD_A)
        return jnp.concatenate([xp[:, :-2], xp[:, 1:-1], xp[:, 2:]], axis=2)

    kb, vb = band(k), band(v)
    s_loc = jnp.einsum('bnqkgd,bnskd->bkgnqs', qb, kb).astype(jnp.float32) * scale
    s_ctx = jnp.einsum('bnqkgd,bksd->bkgnqs', qb, kc).astype(jnp.float32) * scale
    q_pos = jnp.arange(nb)[:, None, None] * BLOCK_A + jnp.arange(BLOCK_A)[None, :, None]
    k_pos = (jnp.arange(nb)[:, None, None] - 1) * BLOCK_A + jnp.arange(3 * BLOCK_A)[None, None, :]
    valid = (jnp.abs(k_pos - q_pos) <= WINDOW_A) & (k_pos >= 0) & (k_pos < t)
    s_loc = jnp.where(valid, s_loc, NEG_INF)
    p = sink_softmax(jnp.concatenate([s_loc, s_ctx], axis=-1), sink[None, :, :, None, None, None])
    p_loc, p_ctx = p[..., :3 * BLOCK_A].astype(v.dtype), p[..., 3 * BLOCK_A:].astype(v.dtype)
    o = (jnp.einsum('bkgnqs,bnskd->bnqkgd', p_loc, vb)
         + jnp.einsum('bkgnqs,bksd->bnqkgd', p_ctx, vc))
    return o.reshape(bq, t, H_A * HD_A)


def hgrn_lower_bounds(lb_logits):
    p = jax.nn.softmax(lb_logits.astype(jnp.float32), axis=0)
    return jnp.cumsum(p, axis=0) - p[0:1]


def gla_chunk_scan(q, k, v, log_f, s0):
    bq, h, t, _ = q.shape
    n = t // CHUNK_B

    def chunks(a):
        return a.astype(jnp.float32).reshape(bq, h, n, CHUNK_B, a.shape[-1])

    qc, kc, vc = chunks(q), chunks(k), chunks(v)
    b = jnp.cumsum(chunks(log_f), axis=3)
    b_last = b[:, :, :, -1]
    causal = jnp.tril(jnp.ones((CHUNK_B, CHUNK_B), dtype=bool))[:, :, None]
    decay = jnp.exp(jnp.where(causal, b[:, :, :, :, None, :] - b[:, :, :, None, :, :], -jnp.inf))
    scores = jnp.einsum('bhntd,bhnsd,bhntsd->bhnts', qc, kc, decay)
    o_intra = jnp.einsum('bhnts,bhnsv->bhntv', scores, vc)
    u = jnp.einsum('bhnsd,bhnsv->bhndv', kc * jnp.exp(b_last[:, :, :, None, :] - b), vc)

    def step(s, xs):
        chunk_decay, chunk_u = xs
        return chunk_decay[..., None] * s + chunk_u, s

    s_final, s_prev = lax.scan(step, s0, (jnp.moveaxis(jnp.exp(b_last), 2, 0), jnp.moveaxis(u, 2, 0)))
    o_inter = jnp.einsum('bhntd,nbhdv->bhntv', qc * jnp.exp(b), s_prev)
    return (o_intra + o_inter).reshape(bq, h, t, v.shape[-1]), s_final


def hgrn_mixer(q, i, f_fwd, f_bwd, g, lb_fwd, lb_bwd, norm_g, s0_fwd, s0_bwd):
    bq, t, _ = q.shape

    def heads(a, d):
        return a.reshape(bq, t, H_B, d).transpose(0, 2, 1, 3)

    qh, vh = heads(q, DK_B), heads(i, DV_B)

    def scan_dir(f_pre, lb, s0, reverse):
        f = lb + (1.0 - lb) * jax.nn.sigmoid(f_pre.astype(jnp.float32))
        args = (qh, heads(1.0 - f, DK_B), vh, heads(jnp.log(f), DK_B))
        if reverse:
            args = tuple(jnp.flip(a, axis=2) for a in args)
        o, s_fin = gla_chunk_scan(*args, s0.astype(jnp.float32))
        return (jnp.flip(o, axis=2) if reverse else o), s_fin

    o_f, s_f = scan_dir(f_fwd, lb_fwd, s0_fwd, False)
    o_b, s_b = scan_dir(f_bwd, lb_bwd, s0_bwd, True)
    o = (o_f + o_b).transpose(0, 2, 1, 3)
    o = rms_norm(o, norm_g) * jax.nn.silu(g.reshape(bq, t, H_B, DV_B).astype(jnp.float32))
    return o.reshape(bq, t, H_B * DV_B).astype(q.dtype), jnp.stack([s_f, s_b], axis=1).astype(q.dtype)


def mla_project(c_q, c_kv, q_norm, kv_norm, w_uq):
    bq, t, _ = c_q.shape
    qf = (rms_norm(c_q, q_norm) @ w_uq).reshape(bq, t, H_C, NOPE_C + ROPE_C)
    return qf[..., :NOPE_C], qf[..., NOPE_C:], rms_norm(c_kv, kv_norm)


def mla_keys_values(ckv, k_rope, w_ukv):
    bq, s, _ = ckv.shape
    kv = (ckv @ w_ukv).reshape(bq, s, H_C, NOPE_C + V_C)
    k = jnp.concatenate([kv[..., :NOPE_C], jnp.broadcast_to(k_rope[:, :, None, :], (bq, s, H_C, ROPE_C))], axis=-1)
    return k, kv[..., NOPE_C:]


def mla_attend(q, k, v):
    bq, t = q.shape[:2]
    o = block_attention(q.transpose(0, 2, 1, 3)[:, :, None], k.transpose(0, 2, 1, 3), v.transpose(0, 2, 1, 3))
    return o[:, :, 0].transpose(0, 2, 1, 3).reshape(bq, t, H_C * V_C)


def pool_mixer(h, w_pool, scale):
    bq, t, _ = h.shape
    hg = h.reshape(bq, t, POOL_GROUPS, POOL_CH)
    cs = jnp.concatenate([jnp.zeros((bq, 1, POOL_GROUPS, POOL_CH), jnp.float32),
                          jnp.cumsum(hg.astype(jnp.float32), axis=1)], axis=1)
    pos = jnp.arange(t)
    outs = []
    for gi, w in enumerate(POOL_WINDOWS):
        lo = jnp.clip(pos - w // 2, 0, t)
        hi = jnp.clip(pos - w // 2 + w, 0, t)
        cs_g = cs[:, :, gi]
        mean = (cs_g[:, hi] - cs_g[:, lo]) / (hi - lo).astype(jnp.float32)[:, None]
        outs.append((mean.astype(h.dtype) - hg[:, :, gi]) @ w_pool[gi])
    return jnp.concatenate(outs, axis=-1) * scale


def ffn_sublayer(x, m, j, norm_g, wg, wu, wd):
    h = rms_norm(x, norm_g) * (1.0 + m[:, :, 3 * j + 1]) + m[:, :, 3 * j]
    return x + 0.5 * m[:, :, 3 * j + 2] * swiglu(h, wg, wu, wd)


def mixer_inputs(x, m, norm_g, w_in):
    h = rms_norm(x, norm_g) * (1.0 + m[:, :, 4]) + m[:, :, 3]
    return jnp.split(h @ w_in, SPLIT_POINTS, axis=-1)


def context_mixers(parts, lw):
    a_q, a_k, a_v, b_q, b_i, b_ff, b_fb, b_g, c_q, c_kv, c_kr, d_h = parts
    bq, t, _ = a_q.shape
    qa = a_q.reshape(bq, t, KV_A, G_A, HD_A).transpose(0, 2, 3, 1, 4)
    ka = a_k.reshape(bq, t, KV_A, HD_A).transpose(0, 2, 1, 3)
    va = a_v.reshape(bq, t, KV_A, HD_A).transpose(0, 2, 1, 3)
    oa = block_attention(qa, ka, va, lw['sink']).transpose(0, 3, 1, 2, 4).reshape(bq, t, H_A * HD_A)
    s0 = jnp.zeros((bq, H_B, DK_B, DV_B), jnp.float32)
    ob, st = hgrn_mixer(b_q, b_i, b_ff, b_fb, b_g, lw['lb_fwd'], lw['lb_bwd'], lw['hgrn_norm'], s0, s0)
    qn, qr, ckv = mla_project(c_q, c_kv, lw['q_norm'], lw['kv_norm'], lw['w_uq'])
    kc, vc = mla_keys_values(ckv, c_kr, lw['w_ukv'])
    oc = mla_attend(jnp.concatenate([qn, qr], axis=-1), kc, vc)
    od = pool_mixer(d_h, lw['pool_w'], lw['pool_scale'])
    return jnp.concatenate([oa, ob, oc, od], axis=-1), (ka, va, st, ckv, c_kr)


def latent_mixers(parts, lw, kc_a, vc_a, st, ckv_c, kr_c, rope_a, rope_c):
    a_q, a_k, a_v, b_q, b_i, b_ff, b_fb, b_g, c_q, c_kv, c_kr, d_h = parts
    bq, t, _ = a_q.shape
    cos_a, sin_a = rope_a
    cos_c, sin_c = rope_c
    qa = apply_rope(a_q.reshape(bq, t, H_A, HD_A), cos_a[:, None, :], sin_a[:, None, :])
    ka = apply_rope(a_k.reshape(bq, t, KV_A, HD_A), cos_a[:, None, :], sin_a[:, None, :])
    oa = window_attention(qa, ka, a_v.reshape(bq, t, KV_A, HD_A), kc_a, vc_a, lw['sink'])
    ob, _ = hgrn_mixer(b_q, b_i, b_ff, b_fb, b_g, lw['lb_fwd'], lw['lb_bwd'], lw['hgrn_norm'], st[:, 0], st[:, 1])
    qn, qr, ckv = mla_project(c_q, c_kv, lw['q_norm'], lw['kv_norm'], lw['w_uq'])
    qr = apply_rope(qr, cos_c[:, None, :], sin_c[:, None, :])
    k_lat, v_lat = mla_keys_values(ckv, apply_rope(c_kr, cos_c, sin_c), lw['w_ukv'])
    k_ctx, v_ctx = mla_keys_values(ckv_c, kr_c, lw['w_ukv'])
    oc = mla_attend(jnp.concatenate([qn, qr], axis=-1),
                    jnp.concatenate([k_lat, k_ctx], axis=1), jnp.concatenate([v_lat, v_ctx], axis=1))
    od = pool_mixer(d_h, lw['pool_w'], lw['pool_scale'])
    return jnp.concatenate([oa, ob, oc, od], axis=-1)


def setup_inputs(seed: int = 0) -> dict:
    key = jax.random.key(seed)
    keys = jax.random.split(key, 32)

    def nrm(i, shape, scale=1.0):
        return jax.random.normal(keys[i], shape, jnp.float32) * scale

    return {
        "x_prompt": nrm(0, (BATCH, SEQ, D_MODEL)),
        "x_sample": nrm(1, (DEC_BATCH, DEC_SEQ, D_MODEL)),
        "c": nrm(2, (DEC_BATCH, D_MODEL)),
        "c_ctx": nrm(3, (D_MODEL,)),
        "cache_attn_k": nrm(4, (DEC_BATCH, DEPTH, KV_A, PAST_LEN, HD_A)),
        "cache_attn_v": nrm(5, (DEC_BATCH, DEPTH, KV_A, PAST_LEN, HD_A)),
        "state_hgrn": nrm(6, (DEC_BATCH, DEPTH, 2, H_B, DK_B, DV_B), 0.5),
        "cache_mla_ckv": nrm(7, (DEC_BATCH, DEPTH, PAST_LEN, KV_LORA_C)),
        "cache_mla_krope": nrm(8, (DEC_BATCH, DEPTH, PAST_LEN, ROPE_C)),
        "w_ada": nrm(9, (DEPTH, D_MODEL, N_MOD * D_MODEL), 0.5 * D_MODEL ** -0.5),
        "b_ada": nrm(10, (DEPTH, N_MOD * D_MODEL), 0.02),
        "norm_sub": 1.0 + nrm(11, (DEPTH, 3, D_MODEL), 0.1),
        "w_ffn_gate": nrm(12, (DEPTH, 2, D_MODEL, D_FF), D_MODEL ** -0.5),
        "w_ffn_up": nrm(13, (DEPTH, 2, D_MODEL, D_FF), D_MODEL ** -0.5),
        "w_ffn_down": nrm(14, (DEPTH, 2, D_FF, D_MODEL), D_FF ** -0.5),
        "w_in": nrm(15, (DEPTH, D_MODEL, D_IN), D_MODEL ** -0.5),
        "w_out": nrm(16, (DEPTH, D_MIX, D_MODEL), D_MIX ** -0.5),
        "attn_sink": nrm(17, (DEPTH, H_A), 0.5),
        "hgrn_lb_logits": nrm(18, (DEPTH, 2, H_B * DK_B)),
        "hgrn_out_norm": 1.0 + nrm(19, (DEPTH, DV_B), 0.1),
        "mla_q_norm": 1.0 + nrm(20, (DEPTH, Q_LORA_C), 0.1),
        "mla_kv_norm": 1.0 + nrm(21, (DEPTH, KV_LORA_C), 0.1),
        "mla_w_uq": nrm(22, (DEPTH, Q_LORA_C, H_C * (NOPE_C + ROPE_C)), Q_LORA_C ** -0.5),
        "mla_w_ukv": nrm(23, (DEPTH, KV_LORA_C, H_C * (NOPE_C + V_C)), KV_LORA_C ** -0.5),
        "pool_w": nrm(24, (DEPTH, POOL_GROUPS, POOL_CH, POOL_CH), POOL_CH ** -0.5),
        "pool_scale": 1.0 + nrm(25, (DEPTH, POOL_GROUPS * POOL_CH), 0.1),
        "final_norm": 1.0 + nrm(26, (D_MODEL,), 0.1),
    }


def reference(x_prompt, x_sample, c, c_ctx, cache_attn_k, cache_attn_v, state_hgrn, cache_mla_ckv,
              cache_mla_krope, w_ada, b_ada, norm_sub, w_ffn_gate, w_ffn_up, w_ffn_down, w_in, w_out,
              attn_sink, hgrn_lb_logits, hgrn_out_norm, mla_q_norm, mla_kv_norm, mla_w_uq, mla_w_ukv,
              pool_w, pool_scale, final_norm):
    lb = hgrn_lower_bounds(hgrn_lb_logits)
    rows = x_sample.shape[1] // GRID_W
    rope_a = axial_rope_tables(rows, HD_A)
    rope_c = axial_rope_tables(rows, ROPE_C)
    xp, xs = x_prompt, x_sample
    new_k, new_v, new_st, new_ckv, new_kr = [], [], [], [], []
    for l in range(DEPTH):
        lw = {"sink": attn_sink[l].reshape(KV_A, G_A).astype(jnp.float32), "lb_fwd": lb[l, 0],
              "lb_bwd": lb[l, 1], "hgrn_norm": hgrn_out_norm[l], "q_norm": mla_q_norm[l],
              "kv_norm": mla_kv_norm[l], "w_uq": mla_w_uq[l], "w_ukv": mla_w_ukv[l],
              "pool_w": pool_w[l], "pool_scale": pool_scale[l]}
        mp = modulation(c_ctx[None, :], w_ada[l], b_ada[l])
        ms = modulation(c, w_ada[l], b_ada[l])
        xp = ffn_sublayer(xp, mp, 0, norm_sub[l, 0], w_ffn_gate[l, 0], w_ffn_up[l, 0], w_ffn_down[l, 0])
        xs = ffn_sublayer(xs, ms, 0, norm_sub[l, 0], w_ffn_gate[l, 0], w_ffn_up[l, 0], w_ffn_down[l, 0])
        mix_p, (ka, va, st, ckv, kr) = context_mixers(mixer_inputs(xp, mp, norm_sub[l, 1], w_in[l]), lw)
        xp = xp + mp[:, :, 5] * (mix_p @ w_out[l])
        new_k.append(ka)
        new_v.append(va)
        new_st.append(st)
        new_ckv.append(ckv)
        new_kr.append(kr)
        mix_s = latent_mixers(mixer_inputs(xs, ms, norm_sub[l, 1], w_in[l]), lw,
                              cache_attn_k[:, l], cache_attn_v[:, l], state_hgrn[:, l],
                              cache_mla_ckv[:, l], cache_mla_krope[:, l], rope_a, rope_c)
        xs = xs + ms[:, :, 5] * (mix_s @ w_out[l])
        xp = ffn_sublayer(xp, mp, 2, norm_sub[l, 2], w_ffn_gate[l, 1], w_ffn_up[l, 1], w_ffn_down[l, 1])
        xs = ffn_sublayer(xs, ms, 2, norm_sub[l, 2], w_ffn_gate[l, 1], w_ffn_up[l, 1], w_ffn_down[l, 1])
    y_prompt = rms_norm(xp, final_norm)
    y_sample = rms_norm(xs, final_norm)
    new_attn_k = jnp.stack(new_k, axis=1)
    new_attn_v = jnp.stack(new_v, axis=1)
    new_state_hgrn = jnp.stack(new_st, axis=1)
    new_mla_ckv = jnp.stack(new_ckv, axis=1)
    new_mla_krope = jnp.stack(new_kr, axis=1)
    return (y_prompt, y_sample, new_attn_k, new_attn_v, new_state_hgrn, new_mla_ckv, new_mla_krope)
```

```python
from contextlib import ExitStack
import numpy as np
import concourse.bass as bass
import concourse.mybir as mybir
from concourse.bass_utils import run_bass_kernel_spmd

F32 = mybir.dt.float32
BF16 = mybir.dt.bfloat16
AF = mybir.ActivationFunctionType
ALU = mybir.AluOpType
ENGS = ['tensor', 'vector', 'scalar', 'gpsimd', 'sync']

D = 1024
NK = 8
DFF = 2816
NF = 22
FG = 2
DIN = 2464
HC = 32
EPS = 1e-6
POOL_WINDOWS = (2, 4, 8, 16)


class Buf:
    def __init__(self, name):
        self.name = name
        self.lw = None
        self.rd = {}


class Sched:
    def __init__(self, nc, es):
        self.engh = {'tensor': nc.tensor, 'vector': nc.vector, 'scalar': nc.scalar, 'gpsimd': nc.gpsimd, 'sync': nc.sync}
        self.nc = nc
        self.es = es
        self.sems = {}
        self.val = {}
        self.seen = {e: {} for e in ENGS}
        self.pend_r = {e: [] for e in ENGS}
        self.pend_w = {e: [] for e in ENGS}
        for e in ENGS:
            self._mksem(e)
        self.ninst = 0
        self.nops = {e: 0 for e in ENGS}
        self.marks = []

    def mark(self, name):
        self.marks.append((name, dict(self.nops)))

    def _mksem(self, key):
        h = self.es.enter_context(self.nc.semaphore("s%d" % len(self.sems)))
        self.sems[key] = h
        self.val[key] = 0
        return h

    def _wait(self, eng, deps):
        best = {}
        for d in deps:
            if d is None:
                continue
            k, v = d
            if v > best.get(k, 0):
                best[k] = v
        for k, v in best.items():
            if self.seen[eng].get(k, 0) < v:
                self.seen[eng][k] = v
                self.engh[eng].wait_ge(self.sems[k], v)
                self.ninst += 1

    def _deps(self, r, w):
        deps = []
        for b in r:
            deps.append(b.lw)
        for b in w:
            deps.append(b.lw)
            deps.extend(b.rd.items())
        return deps

    def op(self, eng, fn, r=(), w=(), inc=True):
        self._wait(eng, self._deps(r, w))
        self.pend_r[eng].extend(r)
        self.pend_w[eng].extend(w)
        self.ninst += 1
        self.nops[eng] += 1
        if inc:
            self.val[eng] += 1
            v = self.val[eng]
            fn(self.engh[eng]).then_inc(self.sems[eng], 1)
            for b in self.pend_w[eng]:
                b.lw = (eng, v)
                b.rd = {}
            for b in self.pend_r[eng]:
                if b.lw == (eng, v):
                    continue
                b.rd[eng] = v
            self.pend_r[eng] = []
            self.pend_w[eng] = []
        else:
            fn(self.engh[eng])

    def dma(self, eng, fn, r=(), w=()):
        self._wait(eng, self._deps(r, w))
        owner = (list(w) + list(r))[0]
        key = ('dma', eng, owner.name)
        if key not in self.sems:
            self._mksem(key)
        self.val[key] += 16
        v = self.val[key]
        fn(self.engh[eng]).then_inc(self.sems[key], 16)
        self.ninst += 1
        for b in w:
            b.lw = (key, v)
            b.rd = {}
        for b in r:
            b.rd[key] = v

    def barrier(self):
        deps = [(k, v) for k, v in self.val.items() if v > 0]
        for e in ENGS:
            self._wait(e, deps)

    def finish(self):
        self.barrier()


class TT:
    def __init__(self, t, name):
        self.t = t
        self.b = Buf(name)

    def __getitem__(self, k):
        return self.t[k]


def build(flags=None):
    flags = flags or {}
    MIX = flags.get('mix', 'ABCD')
    nc = bass.Bass("TRN2", target_bir_lowering=False)
    es = ExitStack()
    S = Sched(nc, es)

    def din(name, shape):
        return nc.dram_tensor(name, list(shape), F32, kind="ExternalInput").ap()

    def dout(name, shape):
        return nc.dram_tensor(name, list(shape), F32, kind="ExternalOutput").ap()

    xs_d = din("xs", [2048, D])
    xp_d = din("xp", [1024, D])
    vecA_d = din("vecA", [120, 128])
    vecB_d = din("vecB", [114, 128])
    hnorm_d = din("hnorm", [2, 64])
    sink_d = din("sink", [2, 4])
    ck_d = din("cache_k", [2, 2, 256, 64])
    cv_d = din("cache_v", [2, 2, 256, 64])
    st_d = din("state", [2, 2, 4, 64, 64])
    cckv_d = din("cache_ckv", [2, 256, 128])
    ckr_d = din("cache_kr", [2, 256, 32])
    wada_d = din("w_ada", [2, D, 9 * D])
    wg_d = din("w_gate", [2, 2, D, DFF])
    wu_d = din("w_up", [2, 2, D, DFF])
    wd_d = din("w_down", [2, 2, DFF, D])
    win_d = din("w_in", [2, D, DIN])
    wout_d = din("w_out", [2, D, D])
    wuq_d = din("w_uq", [2, 256, 384])
    wukv_d = din("w_ukv", [2, 128, 512])
    poolw_d = din("pool_w", [2, 4, 64, 64])
    ident_d = din("ident", [128, 128])
    ropeA_c_d = din("ropeA_c", [128, 2048])
    ropeA_s_d = din("ropeA_s", [128, 2048])
    ropeC_c_d = din("ropeC_c", [128, 2048])
    ropeC_s_d = din("ropeC_s", [128, 2048])
    winmask_d = din("winmask", [128, 384])
    hmask_d = din("hmask", [2, 128, 128])
    scanmask_d = din("scanmask", [128, 512])
    cmask_d = din("cmask", [128, 4])
    invS_d = din("invcntS", [128, 2, 2048])
    invP_d = din("invcntP", [128, 2, 256])
    perm_d = din("permm", [4, 128, 128])

    ys_d = dout("y_s", [2048, D])
    yp_d = dout("y_p", [1024, D])
    nk_d = dout("new_k", [4, 2, 2, 256, 64])
    nv_d = dout("new_v", [4, 2, 2, 256, 64])
    nst_d = dout("new_st", [4, 2, 2, 4, 64, 64])
    nckv_d = dout("new_ckv", [4, 2, 256, 128])
    nkr_d = dout("new_kr", [4, 2, 256, 32])

    cnt = [0]
    live_names = {}

    def sb(shape, dt, stack=None, name=None):
        cnt[0] += 1
        nm = "%s_%d" % (name or "t", cnt[0])
        t = (stack or es).enter_context(nc.sbuf_tensor(nm, list(shape), dt))
        key = (id(stack or es), name or nm)
        n_ = live_names.get(key, 0)
        live_names[key] = n_ + 1
        return TT(t, (name or nm) + ("#%d" % n_ if n_ else ""))

    def op(eng, fn, r=(), w=(), inc=True):
        S.op(eng, fn, r=[x.b for x in r], w=[x.b for x in w], inc=inc)

    def dma(eng, out, in_, r=(), w=(), slow=False):
        if slow:
            S.dma(eng, lambda e: e.dma_start(out=out, in_=in_, allow_slow_non_contiguous=True),
                  r=[x.b for x in r], w=[x.b for x in w])
        else:
            S.dma(eng, lambda e: e.dma_start(out=out, in_=in_), r=[x.b for x in r], w=[x.b for x in w])

    def mm(out, lhsT, rhs, start, stop, r, w, inc=True, tp=None):
        if tp is None:
            op('tensor', lambda e: e.matmul(out, lhsT=lhsT, rhs=rhs, start=start, stop=stop), r=r, w=w, inc=inc)
        else:
            op('tensor', lambda e: e.matmul(out, lhsT=lhsT, rhs=rhs, start=start, stop=stop, tile_position=tp),
               r=r, w=w, inc=inc)

    def tr(out, in_, r, w, n=128):
        op('tensor', lambda e: e.transpose(out=out, in_=in_, identity=ident_f[0:n, 0:n]), r=list(r) + [ident_f], w=w)

    def act(out, in_, func, r, w, bias=None, scale=None):
        kw = {}
        if bias is not None:
            kw['bias'] = bias
        if scale is not None:
            kw['scale'] = scale
        op('scalar', lambda e: e.activation(out=out, in_=in_, func=func, **kw), r=r, w=w)

    def tt(out, in0, in1, alu, r, w, eng='vector'):
        op(eng, lambda e: e.tensor_tensor(out=out, in0=in0, in1=in1, op=alu), r=r, w=w)

    def ts(out, in0, s1, s2, op0, op1, r, w, eng='vector'):
        if op1 is None:
            op(eng, lambda e: e.tensor_scalar(out=out, in0=in0, scalar1=s1, scalar2=None, op0=op0), r=r, w=w)
        else:
            op(eng, lambda e: e.tensor_scalar(out=out, in0=in0, scalar1=s1, scalar2=s2, op0=op0, op1=op1), r=r, w=w)

    def rstd_of(out, in_, r, w):
        act(out, in_, AF.Ln, list(r) + [epsT], w, bias=epsT[:, 0:1], scale=1.0)
        act(out, out, AF.Exp, w, w, scale=-0.5)

    def lnexp_tables():
        pass

    def stt(out, in0, scalar, in1, op0, op1, r, w, eng='vector'):
        op(eng, lambda e: e.scalar_tensor_tensor(out=out, in0=in0, scalar=scalar, in1=in1, op0=op0, op1=op1), r=r, w=w)

    def cp(out, in_, r, w, eng='vector'):
        if eng == 'scalar':
            act(out, in_, AF.Copy, r, w)
        else:
            op(eng, lambda e: e.tensor_copy(out=out, in_=in_), r=r, w=w)

    def memset(t, ap, val, eng='vector'):
        op(eng, lambda e: e.memset(ap, val), r=(), w=[t])

    PSB = [TT(es.enter_context(nc.psum_tensor("ps%d" % i, [128, 512], F32)), "ps%d" % i) for i in range(8)]
    xT = sb([128, NK, 2048], F32, name="xT")
    hT = sb([128, NK, 2048], BF16, name="hT")
    hTk = []
    for _k in range(NK):
        _v = TT(hT.t, "hT%d" % _k)
        hTk.append(_v)
    ident_f = sb([128, 128], F32, name="identf")
    ident_b = sb([128, 128], BF16, name="identb")
    onesD = sb([128, 128], BF16, name="onesD")
    ones256 = sb([128, 128], BF16, name="ones256")
    ones128 = sb([128, 128], BF16, name="ones128")
    bd64 = sb([128, 128], BF16, name="bd64")
    epsT = sb([128, 1], F32, name="eps")
    oneT = sb([128, 1], F32, name="one")
    vecTA = sb([128, 120], F32, name="vecTA")
    vecTB = sb([128, 114], F32, name="vecTB")
    modT = sb([128, 2, 72, 2], F32, name="modT")
    coefA = sb([128, 2, 3, 2, NK], F32, name="coefA")
    gateC = sb([128, 2, 3, 2, NK], F32, name="gateC")
    shiftC = sb([128, 2, 3, 2, NK], F32, name="shiftC")
    scT = sb([128, NK, 2], BF16, name="scT")
    hnT = sb([128, 2], F32, name="hnT")
    esink = sb([128, 8], F32, name="esink")
    lbv = sb([128, 2, 2, 2], F32, name="lbv")
    oml = sb([128, 2, 2, 2], F32, name="oml")
    wslots = [sb([128, NK, 512], BF16, name="wslot%d" % i) for i in range(3)]
    wsl_i = [0]

    def next_wslot():
        w = wslots[wsl_i[0] % 3]
        wsl_i[0] += 1
        return w

    psi = [0]
    held = set()

    def ps(hold=False):
        while True:
            i = psi[0] % 8
            psi[0] += 1
            if i not in held:
                break
        if hold:
            held.add(i)
        return PSB[i]

    def release(p):
        held.discard(PSB.index(p))

    dma('sync', ident_f[:], ident_d, w=[ident_f])
    dma('gpsimd', ident_b[:], ident_d, w=[ident_b])
    memset(onesD, onesD[:], 1.0 / 1024)
    memset(ones256, ones256[:], 1.0 / 256)
    memset(ones128, ones128[:], 1.0 / 128)
    memset(bd64, bd64[:], 0.0)
    memset(bd64, bd64[0:64, 0:64], 1.0 / 64)
    memset(bd64, bd64[64:128, 64:128], 1.0 / 64)
    memset(epsT, epsT[:], EPS)
    memset(oneT, oneT[:], 1.0)
    with ExitStack() as ph:
        stg = sb([128, 128], F32, ph, "stg")
        for (src, n, dst) in ((vecA_d, 120, vecTA), (vecB_d, 114, vecTB)):
            dma('sync', stg[0:n, :], src, w=[stg])
            p = ps()
            tr(p[:, 0:n], stg[0:n, :], [stg], [p], n=n)
            cp(dst[:, 0:n], p[:, 0:n], [p], [dst])
        dma('sync', hnT[0:64, :], hnorm_d.rearrange("l v -> v l"), w=[hnT], slow=True)
        dma('sync', hnT[64:128, :], hnorm_d.rearrange("l v -> v l"), w=[hnT], slow=True)
        sk = sb([128, 8], F32, ph, "sk")
        dma('sync', sk[:], sink_d.rearrange("l h -> (l h)").partition_broadcast(128), w=[sk], slow=True)
        act(esink[:], sk[:], AF.Exp, [sk], [esink])
        lbl = vecTB[:, 98:106].rearrange("p (l r) -> p l r", l=2)
        ex = sb([128, 2, 4], F32, ph, "ex")
        act(ex[:], lbl, AF.Exp, [vecTB], [ex])
        sm = sb([128, 4], F32, ph, "sm")
        tt(sm[:], ex[:, 0, :], ex[:, 1, :], ALU.add, [ex], [sm])
        op('vector', lambda e: e.reciprocal(out=sm[:], in_=sm[:]), r=[sm], w=[sm])
        lbf = lbv[:].rearrange("p l r q -> p l (r q)")
        memset(lbv, lbf[:, 0, :], 0.0)
        tt(lbf[:, 1, :], ex[:, 1, :], sm[:], ALU.mult, [ex, sm], [lbv])
        ts(oml[:].rearrange("p l r q -> p (l r q)"), lbv[:].rearrange("p l r q -> p (l r q)"), -1.0, 1.0, ALU.mult, ALU.add, [lbv], [oml])
        act(scT[:].rearrange("p k g -> p g k"), vecTB[:, 72:88].rearrange("p (g k) -> p g k", g=2), AF.Silu, [vecTB], [scT])
        pass
    S.barrier()
    mblk = sb([2, 512], F32, name="mblk")
    mod_pending = [(l_, cb_) for l_ in range(2) for cb_ in range(18)]
    mod_done = [0]

    def mod_coefs(l, j):
        for g in range(2):
            ng = vecTA[:, 72 + (l * 3 + j) * 8: 72 + (l * 3 + j) * 8 + 8]
            sc_ = modT[:, l, (3 * j + 1) * 8:(3 * j + 2) * 8, g]
            stt(coefA[:, l, j, g, :], sc_, 1.0, ng, ALU.add, ALU.mult, [modT, vecTA], [coefA])
            cp(shiftC[:, l, j, g, :], modT[:, l, (3 * j) * 8:(3 * j + 1) * 8, g], [modT], [shiftC])
            ts(gateC[:, l, j, g, :], modT[:, l, (3 * j + 2) * 8:(3 * j + 3) * 8, g], 0.5 if j != 1 else 1.0, None,
               ALU.mult, None, [modT], [gateC])

    mod_inflight = []

    def mod_issue(n):
        for _ in range(n):
            if not mod_pending:
                return
            l, cb = mod_pending.pop(0)
            wsl = next_wslot()
            dma('gpsimd', wsl[:], wada_d[l, :, cb * 512:(cb + 1) * 512].rearrange("(k p) n -> p k n", p=128), w=[wsl])
            mod_inflight.append((l, cb, wsl))

    def mod_step(n):
        mod_issue(n)
        mod_consume()

    def mod_consume(keep=0):
        while len(mod_inflight) > keep:
            l, cb, wsl = mod_inflight.pop(0)
            p = ps()
            for k in range(NK):
                mm(p[0:2, :], scT[:, k, :], wsl[:, k, :], k == 0, k == NK - 1, [scT, wsl], [p], inc=(k == NK - 1))
            cp(mblk[:], p[0:2, :], [p], [mblk], eng='scalar')
            p2 = ps()
            for c4 in range(4):
                tr(p2[:, c4 * 2:c4 * 2 + 2], mblk[0:2, c4 * 128:(c4 + 1) * 128], [mblk], [p2], n=2)
            bsrc = (vecTA[:, 0:72] if l == 0 else vecTB[:, 0:72])
            bt = vecTA if l == 0 else vecTB
            tt(modT[:, l, cb * 4:(cb + 1) * 4, :], p2[:, 0:8].rearrange("p (c g) -> p c g", g=2),
               bsrc[:, cb * 4:(cb + 1) * 4].unsqueeze(2).broadcast_to([128, 4, 2]), ALU.add, [p2, bt], [modT])
            mod_done[0] += 1
            if cb % 6 == 5:
                mod_coefs(l, cb // 6)

    mod_issue(2)

    def mod_prologue():
        for _ in range(4):
            mod_consume(keep=1)
            mod_issue(1)
        mod_consume()
    S.barrier()

    def load_xT(x_d, T):
        S.mark("load")
        with ExitStack() as ph:
            stg = [sb([128, D], F32, ph, "xstg%d" % i) for i in range(2)]
            for bi in range(T // 128):
                s_ = stg[bi % 2]
                dma('sync', s_[:], x_d[bi * 128:(bi + 1) * 128, :], w=[s_])
                for half in range(2):
                    p = ps()
                    for q in range(4):
                        k = half * 4 + q
                        tr(p[:, q * 128:(q + 1) * 128], s_[:, k * 128:(k + 1) * 128], [s_], [p])
                    cp(xT[:, half * 4:half * 4 + 4, bi * 128:(bi + 1) * 128], p[:].rearrange("p (q t) -> p q t", q=4), [p], [xT],
                       eng='vector')
        S.barrier()

    def normmod(T, A, B, out, out_b, ph):
        sqs = [[sb([128, 512], BF16, ph, "sq%d_%d" % (i, k)) for k in range(NK)] for i in range(2)]
        rstds = [sb([128, 512], F32, ph, "rstd%d" % i) for i in range(2)]
        tmps = [sb([128, 512], F32, ph, "nm_tmp%d" % i) for i in range(4)]
        tiles = list(range(0, T, 512))
        lnexp_tables()

        pss = {}

        def s1a(i):
            t0 = tiles[i]
            sq = sqs[i % 2]
            for k in range(NK):
                e_ = ('gpsimd', 'scalar', 'vector', 'scalar', 'gpsimd', 'vector', 'scalar', 'vector')[k]
                if e_ == 'scalar':
                    act(sq[k][:], xT[:, k, t0:t0 + 512], AF.Square, [xT], [sq[k]])
                else:
                    tt(sq[k][:], xT[:, k, t0:t0 + 512], xT[:, k, t0:t0 + 512], ALU.mult, [xT], [sq[k]], eng=e_)
            p = ps(hold=True)
            pss[i] = p
            for k in range(NK):
                mm(p[:], onesD[:], sq[k][:], k == 0, k == NK - 1, [onesD, sq[k]], [p], inc=(k == NK - 1))

        def s1b(i):
            p = pss.pop(i)
            rstd_of(rstds[i % 2][:], p[:], [p], [rstds[i % 2]])
            release(p)

        def s2(i):
            t0 = tiles[i]
            rstd = rstds[i % 2]
            for k in range(NK):
                tm = tmps[k % 4]
                tt(tm[:], xT[:, k, t0:t0 + 512], rstd[:], ALU.mult, [xT, rstd], [tm])
                if B is not None:
                    if k % 4 == 3:
                        ts(out(k, t0), tm[:], A[0](k), B[0](k), ALU.mult, ALU.add, [tm] + A[1] + B[1], [out_b[k]], eng='gpsimd')
                    else:
                        act(out(k, t0), tm[:], AF.Identity, [tm] + A[1] + B[1], [out_b[k]], bias=B[0](k), scale=A[0](k))
                else:
                    act(out(k, t0), tm[:], AF.Identity, [tm] + A[1], [out_b[k]], scale=A[0](k))

        s1a(0)
        s1b(0)
        for step in range(len(tiles)):
            if step + 1 < len(tiles):
                s1a(step + 1)
            s2(step)
            if step + 1 < len(tiles):
                s1b(step + 1)

    def ffn(T, l, j, grp, fi):
        S.mark("ffn l%d j%d g%d" % (l, j, grp))
        if mod_done[0] == 0:
            mod_prologue()
        with ExitStack() as ph:
            wgu = [sb([128, NK, 2, FG * 128], BF16, ph, "wgu%d" % i) for i in range(2)]
            wdn = [sb([128, FG, D], BF16, ph, "wdn%d" % i) for i in range(2)]
            acts = [sb([128, FG, 512], BF16, ph, "act%d" % i) for i in range(2)]
            sgs = [sb([128, 512], BF16, ph, "sg%d" % i) for i in range(2)]
            ngrp = NF // FG
            ntile = T // 512

            def load(fg):
                f0 = fg * FG * 128
                dma('gpsimd', wgu[fg % 2][:, :, 0, :], wg_d[l, fi, :, f0:f0 + FG * 128].rearrange("(k p) n -> p k n", p=128), w=[wgu[fg % 2]])
                dma('gpsimd', wgu[fg % 2][:, :, 1, :], wu_d[l, fi, :, f0:f0 + FG * 128].rearrange("(k p) n -> p k n", p=128), w=[wgu[fg % 2]])
                dma('gpsimd', wdn[fg % 2][:], wd_d[l, fi, f0:f0 + FG * 128, :].rearrange("(c p) n -> p c n", p=128), w=[wdn[fg % 2]])

            load(0)
            normmod(T, (lambda k: coefA[:, l, j, grp, k:k + 1], [coefA]), (lambda k: shiftC[:, l, j, grp, k:k + 1], [shiftC]),
                    lambda k, t0: hT[:, k, t0:t0 + 512], hTk, ph)
            sgi = 0
            for fg in range(ngrp):
                if fg + 1 < ngrp:
                    load(fg + 1)
                mod_consume()
                if mod_done[0] + len(mod_inflight) < (18 if (l == 0 and j == 0) else 36):
                    mod_issue(2)
                W = wgu[fg % 2]
                Wd = wdn[fg % 2]

                def gateup(ti):
                    nonlocal sgi
                    A_ = acts[ti % 2]
                    t0 = ti * 512
                    for fc in range(FG):
                        pg = ps()
                        pu = ps()
                        for k in range(NK):
                            mm(pg[:], W[:, k, 0, fc * 128:(fc + 1) * 128], hT[:, k, t0:t0 + 512], k == 0, k == NK - 1, [W, hTk[k]], [pg], inc=(k == NK - 1))
                        for k in range(NK):
                            mm(pu[:], W[:, k, 1, fc * 128:(fc + 1) * 128], hT[:, k, t0:t0 + 512], k == 0, k == NK - 1, [W, hTk[k]], [pu], inc=(k == NK - 1))
                        sg = sgs[sgi % 2]
                        sgi += 1
                        act(sg[:], pg[:], AF.Silu, [pg], [sg])
                        tt(A_[:, fc, :], pu[:], sg[:], ALU.mult, [pu, sg], [A_])

                def down(ti):
                    A_ = acts[ti % 2]
                    t0 = ti * 512
                    for dc in range(NK):
                        pd = ps()
                        for fc in range(FG):
                            mm(pd[:], Wd[:, fc, dc * 128:(dc + 1) * 128], A_[:, fc, :], fc == 0, fc == FG - 1, [Wd, A_], [pd], inc=(fc == FG - 1))
                        stt(xT[:, dc, t0:t0 + 512], pd[:], gateC[:, l, j, grp, dc:dc + 1], xT[:, dc, t0:t0 + 512], ALU.mult, ALU.add,
                            [pd, gateC, xT], [xT])

                for step in range(ntile + 1):
                    if step < ntile:
                        gateup(step)
                    if step >= 1:
                        down(step - 1)
            mod_consume()
            if j == 0 and 'B' in MIX:
                prefetch(('BX', l), [(512, 1024)], l)
                prefetch(('BY', l), [(1024, 1536)], l)
        S.barrier()

    def final_out(T, y_d):
        S.mark("final")
        with ExitStack() as ph:
            zT = [sb([128, NK, 512], F32, ph, "zT%d" % i) for i in range(2)]
            ost = [sb([128, D], F32, ph, "ost%d" % i) for i in range(2)]
            sqs = [[sb([128, 512], BF16, ph, "sq%d_%d" % (i, k)) for k in range(NK)] for i in range(2)]
            rstds = [sb([128, 512], F32, ph, "rstd%d" % i) for i in range(2)]
            tmps = [sb([128, 512], F32, ph, "nm_tmp%d" % i) for i in range(4)]
            tiles = list(range(0, T, 512))
            lnexp_tables()
            oi = [0]

            pss = {}

            def s1a(i):
                t0 = tiles[i]
                sq = sqs[i % 2]
                for k in range(NK):
                    e_ = ('gpsimd', 'scalar', 'vector', 'scalar', 'gpsimd', 'vector', 'scalar', 'vector')[k]
                    if e_ == 'scalar':
                        act(sq[k][:], xT[:, k, t0:t0 + 512], AF.Square, [xT], [sq[k]])
                    else:
                        tt(sq[k][:], xT[:, k, t0:t0 + 512], xT[:, k, t0:t0 + 512], ALU.mult, [xT], [sq[k]], eng=e_)
                p = ps(hold=True)
                pss[i] = p
                for k in range(NK):
                    mm(p[:], onesD[:], sq[k][:], k == 0, k == NK - 1, [onesD, sq[k]], [p], inc=(k == NK - 1))

            def s1b(i):
                p = pss.pop(i)
                rstd_of(rstds[i % 2][:], p[:], [p], [rstds[i % 2]])
                release(p)

            def s2(i):
                t0 = tiles[i]
                rstd = rstds[i % 2]
                z = zT[i % 2]
                for k in range(NK):
                    tm = tmps[k % 4]
                    tt(tm[:], xT[:, k, t0:t0 + 512], rstd[:], ALU.mult, [xT, rstd], [tm])
                    act(z[:, k, :], tm[:], AF.Identity, [tm, vecTB], [z], scale=vecTB[:, 106 + k:107 + k])
                for bi in range(4):
                    o_ = ost[oi[0] % 2]
                    oi[0] += 1
                    for half in range(2):
                        p = ps()
                        for q in range(4):
                            k = half * 4 + q
                            tr(p[:, q * 128:(q + 1) * 128], z[:, k, bi * 128:(bi + 1) * 128], [z], [p])
                        cp(o_[:, half * 512:(half + 1) * 512], p[:], [p], [o_], eng='vector')
                    dma('sync', y_d[t0 + bi * 128:t0 + (bi + 1) * 128, :], o_[:], r=[o_])

            s1a(0)
            s1b(0)
            for step in range(len(tiles)):
                if step + 1 < len(tiles):
                    s1a(step + 1)
                s2(step)
                if step + 1 < len(tiles):
                    s1b(step + 1)
        S.barrier()

    def proj_fm(ph_r, W, wcols, M, t0, n, out_ps, rows0=0):
        for k in range(NK):
            mm(out_ps[rows0:rows0 + M, 0:n], W[:, k, wcols[0]:wcols[1]], hT[:, k, t0:t0 + n], k == 0, k == NK - 1, [W, hTk[k]], [out_ps],
               inc=(k == NK - 1))

    def load_w(cols_list, l, src=None):
        wsl = next_wslot()
        o = 0
        for (c0, c1) in cols_list:
            dma('gpsimd', wsl[:, :, o:o + (c1 - c0)], win_d[l, :, c0:c1].rearrange("(k p) n -> p k n", p=128), w=[wsl])
            o += c1 - c0
        return wsl

    stash = {}

    def prefetch(key, cols, l):
        stash[key] = load_w(cols, l)

    def get_w(key, cols, l):
        if key in stash:
            return stash.pop(key)
        return load_w(cols, l)

    def load_wout(l, c0, tile=None):
        wsl = tile if tile is not None else next_wslot()
        wv = wsl[:].rearrange("p k n -> p (k n)")[:, 0:2048].rearrange("p (c n) -> p c n", c=2)
        dma('gpsimd', wv, wout_d[l, c0 * 128:(c0 + 2) * 128, :].rearrange("(c p) n -> p c n", p=128), w=[wsl])
        return wsl

    def outproj(l, grp, tb, L, mixT, c0, wsl=None):
        if wsl is None:
            wsl = load_wout(l, c0)
        wv = wsl[:].rearrange("p k n -> p (k n)")[:, 0:2048].rearrange("p (c n) -> p c n", c=2)
        TL = min(512, L)
        for t0 in range(0, L, TL):
            for dc in range(NK):
                p = ps()
                for c in range(2):
                    mm(p[:, 0:TL], wv[:, c, dc * 128:(dc + 1) * 128], mixT[:, c, t0:t0 + TL], c == 0, c == 1, [wsl, mixT], [p], inc=(c == 1))
                stt(xT[:, dc, tb + t0:tb + t0 + TL], p[:, 0:TL], gateC[:, l, 1, grp, dc:dc + 1], xT[:, dc, tb + t0:tb + t0 + TL],
                    ALU.mult, ALU.add, [p, gateC, xT], [xT])

    def softmax_norm(pacc, n, g, mixT, c, t0, sinkcol, ph_tiles):
        rc = ph_tiles
        nr = slice(g * 64, (g + 1) * 64)
        dr = slice((1 - g) * 64, (2 - g) * 64)
        if sinkcol is not None:
            act(rc[dr, 0:n], pacc[dr, 0:n], AF.Ln, [pacc, esink], [rc], bias=esink[dr, sinkcol:sinkcol + 1], scale=1.0)
        else:
            act(rc[dr, 0:n], pacc[dr, 0:n], AF.Ln, [pacc], [rc])
        act(rc[dr, 0:n], rc[dr, 0:n], AF.Exp, [rc], [rc], scale=-1.0)
        tt(mixT[nr, c, t0:t0 + n], pacc[nr, 0:n], rc[dr, 0:n], ALU.mult, [pacc, rc], [mixT])

    def build_kdup(WA, W2):
        kview = WA[:, :, 256:384].rearrange("p k (c d) -> p k c d", c=2)
        w2k = W2[:, :, 0:256].rearrange("p k (c u d) -> p k c u d", c=2, u=2)
        for u in range(2):
            cp(w2k[:, :, :, u, :], kview, [WA], [W2], eng=('vector' if u else 'gpsimd'))

    def mixer_A(l, grp, tb, L, latent, seq, ph, W=None):
        lnexp_tables()
        if latent:
            wm = sb([128, 384], BF16, ph, "winmask")
            dma('gpsimd', wm[:], winmask_d, w=[wm])
        nb = L // 128
        TL = min(512, L)
        mixT = sb([128, 2, L], BF16, ph, "mixA")
        qT = sb([128, 2, L], BF16, ph, "qT")
        kT = sb([128, 2, L + (256 if latent else 0)], BF16, ph, "kTdup")
        vaug = sb([128, nb + (2 if latent else 0), 2, 192], BF16, ph, "vaugA")
        rc = sb([128, TL], F32, ph, "rcA")
        memset(vaug, vaug[:, :, :, 64:128], 1.0)
        if W is not None:
            WA, W2, wo = W
        else:
            WA = get_w(('A', l), [(0, 512)], l)
            W2 = next_wslot()
            build_kdup(WA, W2)
            wo = load_wout(l, 0)
        if latent:
            W3 = sb([128, NK, 256], BF16, ph, "W3A")
            qv = WA[:, :, 0:256].rearrange("p k (h two d) -> p k h two d", two=2, d=32)
            w2q = W2[:, :, 256:512].rearrange("p k (h two d) -> p k h two d", two=2, d=32)
            for half in range(2):
                cp(w2q[:, :, :, half, :], qv[:, :, :, 1 - half, :], [WA], [W2], eng=('vector' if half else 'gpsimd'))
            kv2 = WA[:, :, 256:384].rearrange("p k (c two d) -> p k c two d", two=2, d=32)
            w3v = W3[:].rearrange("p k (c u two d) -> p k c u two d", c=2, u=2, two=2)
            for u in range(2):
                for half in range(2):
                    cp(w3v[:, :, :, u, half, :], kv2[:, :, :, 1 - half, :], [WA], [W3], eng=('vector' if half else 'gpsimd'))
            rc_t = sb([128, TL], F32, ph, "ropec")
            rs_t = sb([128, TL], F32, ph, "ropes")
            t1 = sb([128, TL], F32, ph, "ropet1")
            t2 = rc
        for t0 in range(0, L, TL):
            if latent:
                dma('sync', rc_t[:, 0:TL], ropeA_c_d[:, t0:t0 + TL], w=[rc_t])
                dma('sync', rs_t[:, 0:TL], ropeA_s_d[:, t0:t0 + TL], w=[rs_t])
            for ci in range(4):
                p = ps()
                if ci < 2:
                    proj_fm(ph, WA, (ci * 128, ci * 128 + 128), 128, tb + t0, TL, p)
                else:
                    proj_fm(ph, W2, ((ci - 2) * 128, (ci - 2) * 128 + 128), 128, tb + t0, TL, p)
                dst = (qT[:, ci, t0:t0 + TL] if ci < 2 else kT[:, ci - 2, t0:t0 + TL])
                dstT = qT if ci < 2 else kT
                if latent:
                    p2 = ps()
                    if ci < 2:
                        proj_fm(ph, W2, (256 + ci * 128, 256 + ci * 128 + 128), 128, tb + t0, TL, p2)
                    else:
                        proj_fm(ph, W3, ((ci - 2) * 128, (ci - 2) * 128 + 128), 128, tb + t0, TL, p2)
                    tt(t1[:, 0:TL], p[:, 0:TL], rc_t[:, 0:TL], ALU.mult, [p, rc_t], [t1])
                    tt(t2[:, 0:TL], p2[:, 0:TL], rs_t[:, 0:TL], ALU.mult, [p2, rs_t], [t2])
                    tt(dst, t1[:, 0:TL], t2[:, 0:TL], ALU.add, [t1, t2], [dstT], eng='gpsimd')
                else:
                    cp(dst, p[:, 0:TL], [p], [dstT], eng='scalar')
        if not latent:
            kvst = sb([128, nb, 256], F32, ph, "kvst")
        for bi in range(nb):
            p = ps()
            for k in range(NK):
                mm(p[:, 0:256], hT[:, k, tb + bi * 128:tb + (bi + 1) * 128], WA[:, k, 256:512], k == 0, k == NK - 1, [hTk[k], WA], [p], inc=(k == NK - 1))
            for c in range(2):
                cp(vaug[:, bi, c, 0:64], p[:, 128 + c * 64:128 + (c + 1) * 64], [p], [vaug], eng='vector')
                cp(vaug[:, bi, c, 128:192], p[:, 128 + c * 64:128 + (c + 1) * 64], [p], [vaug], eng='vector')
            if not latent:
                cp(kvst[:, bi, :], p[:, 0:256], [p], [kvst], eng='vector')
        if not latent:
            for c in range(2):
                dma('sync', nk_d[seq, l, c].rearrange("(b p) d -> p b d", p=128), kvst[:, :, c * 64:(c + 1) * 64], r=[kvst])
                dma('sync', nv_d[seq, l, c].rearrange("(b p) d -> p b d", p=128), kvst[:, :, 128 + c * 64:128 + (c + 1) * 64], r=[kvst])
        nctx = 0
        if latent:
            nctx = 2
            kc = sb([128, 2, 2, 2, 64], F32, ph, "kctok")
            vc = sb([128, 2, 2, 64], F32, ph, "vctok")
            for bi in range(2):
                for dup in range(2):
                    dma('sync', kc[:, bi, :, dup, :], ck_d[l, :, bi * 128:(bi + 1) * 128, :].rearrange("c p d -> p c d"), w=[kc])
                dma('sync', vc[:, bi, :, :], cv_d[l, :, bi * 128:(bi + 1) * 128, :].rearrange("c p d -> p c d"), w=[vc])
            for bi in range(2):
                for c in range(2):
                    p = ps()
                    tr(p[:, 0:128], kc[:, bi, c, :, :].rearrange("p u d -> p (u d)"), [kc], [p])
                    cp(kT[:, c, L + bi * 128:L + (bi + 1) * 128], p[:, 0:128], [p], [kT], eng='scalar')
                    cp(vaug[:, nb + bi, c, 0:64], vc[:, bi, c, :], [vc], [vaug])
                    cp(vaug[:, nb + bi, c, 128:192], vc[:, bi, c, :], [vc], [vaug])
        scale = 0.125
        PT = [sb([128, TL], BF16, ph, "PT%d" % i) for i in range(8 if latent else 2)]
        pti = 0
        for c in range(2):
            for g in range(2):
                h = 2 * c + g
                rows = slice(g * 64, (g + 1) * 64)
                if latent:
                    for q0 in range(0, nb, 4):
                        pacc = ps(hold=True)
                        kbs = [j for j in range(q0 - 1, q0 + 5) if 0 <= j < nb]
                        info = {}
                        for j in kbs:
                            qa = max(j - 1, q0)
                            qb = min(j + 1, q0 + 3)
                            n = (qb - qa + 1) * 128
                            moff = (qa - (j - 1)) * 128
                            p = ps()
                            mm(p[:, 0:n], kT[rows, c, j * 128:(j + 1) * 128], qT[rows, c, qa * 128:qa * 128 + n], True, False, [kT, qT], [p], inc=False)
                            mm(p[:, 0:n], ident_b[:], wm[:, moff:moff + n], False, True, [ident_b, wm], [p])
                            P_ = PT[j - (q0 - 1)]
                            act(P_[:, 0:n], p[:, 0:n], AF.Exp, [p], [P_], scale=scale)
                            info[j] = (P_, qa)
                        for bi in range(2):
                            p = ps()
                            mm(p[:, 0:512], kT[rows, c, L + bi * 128:L + (bi + 1) * 128], qT[rows, c, q0 * 128:q0 * 128 + 512], True, True, [kT, qT], [p])
                            P_ = PT[6 + bi]
                            act(P_[:, 0:512], p[:, 0:512], AF.Exp, [p], [P_], scale=scale)
                        for qi in range(q0, q0 + 4):
                            o = (qi - q0) * 128
                            srcs = []
                            for j in (qi - 1, qi, qi + 1):
                                if j in info:
                                    P_, qa = info[j]
                                    srcs.append((vaug[:, j, c, g * 64:g * 64 + 128], P_[:, (qi - qa) * 128:(qi - qa + 1) * 128], P_))
                            for bi in range(2):
                                srcs.append((vaug[:, nb + bi, c, g * 64:g * 64 + 128], PT[6 + bi][:, o:o + 128], PT[6 + bi]))
                            for si_, (lh, rh, Pt_) in enumerate(srcs):
                                mm(pacc[:, o:o + 128], lh, rh, si_ == 0, si_ == len(srcs) - 1, [vaug, Pt_], [pacc], inc=(si_ == len(srcs) - 1))
                        softmax_norm(pacc, 512, g, mixT, c, q0 * 128, l * 4 + h, rc)
                        release(pacc)
                else:
                    pacc = ps(hold=True)
                    for j in range(nb):
                        p = ps()
                        mm(p[:, 0:L], kT[rows, c, j * 128:(j + 1) * 128], qT[rows, c, 0:L], True, True, [kT, qT], [p])
                        P_ = PT[pti % 2]
                        pti += 1
                        act(P_[:, 0:L], p[:, 0:L], AF.Exp, [p], [P_], scale=scale)
                        mm(pacc[:, 0:L], vaug[:, j, c, g * 64:g * 64 + 128], P_[:, 0:L], j == 0, j == nb - 1, [vaug, P_], [pacc], inc=(j == nb - 1))
                    softmax_norm(pacc, L, g, mixT, c, 0, l * 4 + h, rc)
                    release(pacc)
        outproj(l, grp, tb, L, mixT, 0, wo)
        if W is None and 'C' in MIX:
            prefetch(('C', l), [(1792, 2208)], l)

    def load_pw(l, stack):
        pw = sb([128, 2, 64], BF16, stack, "poolw")
        for b2 in range(2):
            dma('gpsimd', pw[b2 * 64:b2 * 64 + 64, :, :], poolw_d[l].rearrange("(a b) i o -> b i a o", b=2)[b2], w=[pw])
        return pw

    def mixer_D(l, grp, tb, L, latent, seq, ph, W=None):
        TL = min(512, L)
        mixT = sb([128, 2, L], BF16, ph, "mixD")
        if W is not None:
            wd_, pw, wo = W
        else:
            wd_ = get_w(('D', l), [(2208, 2464)], l)
            pw = load_pw(l, ph)
            wo = load_wout(l, 6)
        dpad = sb([128, 2, L + 32], F32, ph, "dpad")
        f2 = sb([128, 2, L + 32], F32, ph, "f2")
        f4 = sb([128, 2, L + 32], F32, ph, "f4")
        inv = sb([128, L], F32, ph, "invc")
        memset(dpad, dpad[:], 0.0)
        memset(f2, f2[:], 0.0)
        memset(f4, f4[:], 0.0)
        for t0 in range(0, L, TL):
            for c in range(2):
                p = ps()
                proj_fm(ph, wd_, (c * 128, c * 128 + 128), 128, tb + t0, TL, p)
                cp(dpad[:, c, 16 + t0:16 + t0 + TL], p[:, 0:TL], [p], [dpad], eng='scalar')
        n = L + 16
        diff = sb([128, 2, L], BF16, ph, "pdiff")
        for c in range(2):
            dma('sync', inv[:], (invS_d if latent else invP_d)[:, c, :], w=[inv])
            tt(f2[:, c, 0:n], dpad[:, c, 0:n], dpad[:, c, 1:n + 1], ALU.add, [dpad], [f2])
            tt(f4[:, c, 0:n], f2[:, c, 0:n], f2[:, c, 2:n + 2], ALU.add, [f2], [f4])
            if c == 0:
                srcs = [(f2, 2, slice(0, 64)), (f4, 4, slice(64, 128))]
            else:
                tt(f2[:, c, 0:n], f4[:, c, 0:n], f4[:, c, 4:n + 4], ALU.add, [f4], [f2])
                tt(f4[:, c, 0:n - 8], f2[:, c, 0:n - 8], f2[:, c, 8:n], ALU.add, [f2], [f4])
                srcs = [(f2, 8, slice(0, 64)), (f4, 16, slice(64, 128))]
            for (ft, w_, rs) in srcs:
                o = 16 - w_ // 2
                tt(inv[rs, 0:L], ft[rs, c, o:o + L], inv[rs, 0:L], ALU.mult, [ft, inv], [inv])
                tt(diff[rs, c, 0:L], inv[rs, 0:L], dpad[rs, c, 16:16 + L], ALU.subtract, [inv, dpad], [diff])
        for t0 in range(0, L, TL):
            for c in range(2):
                p = ps()
                for g in range(2):
                    rs = slice(g * 64, g * 64 + 64)
                    mm(p[rs, 0:TL], pw[rs, c, :], diff[rs, c, t0:t0 + TL], True, True, [pw, diff], [p], inc=(g == 1), tp=(g * 64, g * 64))
                act(mixT[:, c, t0:t0 + TL], p[:, 0:TL], AF.Identity, [p, vecTB], [mixT], scale=vecTB[:, 94 + l * 2 + c:95 + l * 2 + c])
        outproj(l, grp, tb, L, mixT, 6, wo)

    def load_wu(l, stack):
        wuq = sb([128, 2, 384], BF16, stack, "wuq")
        dma('gpsimd', wuq[:], wuq_d[l].rearrange("(k p) n -> p k n", p=128), w=[wuq])
        wukv = sb([128, 512], BF16, stack, "wukv")
        dma('gpsimd', wukv[:], wukv_d[l], w=[wukv])
        return wuq, wukv

    def mixer_C(l, grp, tb, L, latent, seq, ph, W=None):
        lnexp_tables()
        TL = min(512, L)
        nb = L // 128
        nkb = nb + (2 if latent else 0)
        LK = nkb * 128
        mixT = sb([128, 2, L], BF16, ph, "mixC")
        cqn = sb([128, 2, L], BF16, ph, "cqn")
        ckvT = sb([128, LK], BF16, ph, "ckvT")
        KhT = sb([128, LK], BF16, ph, "KhT")
        krs = sb([32, TL], BF16, ph, "krs")
        memset(KhT, KhT[64:128, :], 0.0)
        rstd = sb([128, TL], F32, ph, "rstdC")
        tmp = sb([128, TL], F32, ph, "tmpC")
        sq = sb([128, 2, TL], BF16, ph, "sqC")
        if W is not None:
            wc, wuq, wukv, wo = W
        else:
            wc = get_w(('C', l), [(1792, 2208)], l)
            wuq, wukv = load_wu(l, ph)
            wo = load_wout(l, 4)
        if latent:
            wkrs = sb([128, NK, 32], BF16, ph, "wkrs")
            cp(wkrs[:, :, 0:16], wc[:, :, 400:416], [wc], [wkrs], eng='gpsimd')
            cp(wkrs[:, :, 16:32], wc[:, :, 384:400], [wc], [wkrs], eng='vector')
            wuqs = sb([128, 2, 4, 96], BF16, ph, "wuqs")
            wuqv = wuq[:].rearrange("p k (h e) -> p k h e", e=96)
            cp(wuqs[:, :, :, 0:64], wuqv[:, :, :, 0:64], [wuq], [wuqs], eng='gpsimd')
            cp(wuqs[:, :, :, 64:80], wuqv[:, :, :, 80:96], [wuq], [wuqs], eng='vector')
            cp(wuqs[:, :, :, 80:96], wuqv[:, :, :, 64:80], [wuq], [wuqs], eng='vector')
            rcCs = [sb([128, TL], F32, ph, "ropeCc%d" % i) for i in range(2)]
            rsCs = [sb([128, TL], F32, ph, "ropeCs%d" % i) for i in range(2)]
            ropei = [0]

            def rope_tiles(t0_):
                i_ = ropei[0] % 2
                ropei[0] += 1
                dma('sync', rcCs[i_][:], ropeC_c_d[:, t0_:t0_ + TL], w=[rcCs[i_]])
                dma('sync', rsCs[i_][:], ropeC_s_d[:, t0_:t0_ + TL], w=[rsCs[i_]])
                return rcCs[i_], rsCs[i_]
            t1 = rstd
            t2 = tmp
        else:
            ckvF = sb([128, L], F32, ph, "ckvF")
            krF = sb([32, L], F32, ph, "krF")
        for t0 in range(0, L, TL):
            pq = [ps(), ps()]
            for c in range(2):
                proj_fm(ph, wc, (c * 128, c * 128 + 128), 128, tb + t0, TL, pq[c])
                act(sq[:, c, 0:TL], pq[c][:, 0:TL], AF.Square, [pq[c]], [sq])
            p = ps()
            for c in range(2):
                mm(p[:, 0:TL], ones256[:], sq[:, c, 0:TL], c == 0, c == 1, [ones256, sq], [p], inc=(c == 1))
            rstd_of(rstd[:, 0:TL], p[:, 0:TL], [p], [rstd])
            for c in range(2):
                tt(tmp[:, 0:TL], pq[c][:, 0:TL], rstd[:, 0:TL], ALU.mult, [pq[c], rstd], [tmp])
                act(cqn[:, c, t0:t0 + TL], tmp[:, 0:TL], AF.Identity, [tmp, vecTB], [cqn], scale=vecTB[:, 88 + l * 2 + c:89 + l * 2 + c])
            pk = ps()
            proj_fm(ph, wc, (256, 384), 128, tb + t0, TL, pk)
            act(sq[:, 0, 0:TL], pk[:, 0:TL], AF.Square, [pk], [sq])
            p = ps()
            mm(p[:, 0:TL], ones128[:], sq[:, 0, 0:TL], True, True, [ones128, sq], [p])
            rstd_of(rstd[:, 0:TL], p[:, 0:TL], [p], [rstd])
            tt(tmp[:, 0:TL], pk[:, 0:TL], rstd[:, 0:TL], ALU.mult, [pk, rstd], [tmp])
            if latent:
                act(ckvT[:, t0:t0 + TL], tmp[:, 0:TL], AF.Identity, [tmp, vecTB], [ckvT], scale=vecTB[:, 92 + l:93 + l])
            else:
                act(ckvF[:, t0:t0 + TL], tmp[:, 0:TL], AF.Identity, [tmp, vecTB], [ckvF], scale=vecTB[:, 92 + l:93 + l])
                cp(ckvT[:, t0:t0 + TL], ckvF[:, t0:t0 + TL], [ckvF], [ckvT])
            pr = ps()
            proj_fm(ph, wc, (384, 416), 32, tb + t0, TL, pr)
            if latent:
                pr2 = ps()
                proj_fm(ph, wkrs, (0, 32), 32, tb + t0, TL, pr2)
                rcC, rsC = rope_tiles(t0)
                tt(t1[0:32, 0:TL], pr[0:32, 0:TL], rcC[0:32, 0:TL], ALU.mult, [pr, rcC], [t1])
                tt(t2[0:32, 0:TL], pr2[0:32, 0:TL], rsC[0:32, 0:TL], ALU.mult, [pr2, rsC], [t2])
                tt(krs[:, 0:TL], t1[0:32, 0:TL], t2[0:32, 0:TL], ALU.add, [t1, t2], [krs])
                cp(KhT[64:96, t0:t0 + TL], krs[:, 0:TL], [krs], [KhT])
            else:
                cp(krF[:, t0:t0 + TL], pr[0:32, 0:TL], [pr], [krF], eng='scalar')
                cp(KhT[64:96, t0:t0 + TL], krF[:, t0:t0 + TL], [krF], [KhT])
        if latent:
            cst = sb([128, 2, 160], F32, ph, "cstg")
            dma('sync', cst[:, :, 0:128], cckv_d[l].rearrange("(b p) d -> p b d", p=128), w=[cst])
            dma('sync', cst[:, :, 128:160], ckr_d[l].rearrange("(b p) d -> p b d", p=128), w=[cst])
            for bi in range(2):
                p = ps()
                tr(p[:, 0:128], cst[:, bi, 0:128], [cst], [p])
                cp(ckvT[:, L + bi * 128:L + (bi + 1) * 128], p[:, 0:128], [p], [ckvT], eng='scalar')
                p = ps()
                tr(p[0:32, 0:128], cst[:, bi, 128:160], [cst], [p])
                cp(krs[:, 0:128], p[0:32, 0:128], [p], [krs], eng='scalar')
                cp(KhT[64:96, L + bi * 128:L + (bi + 1) * 128], krs[:, 0:128], [krs], [KhT])
        else:
            ost = sb([128, nb, 160], F32, ph, "ostC")
            for bi in range(nb):
                p = ps()
                tr(p[:, 0:128], ckvF[:, bi * 128:(bi + 1) * 128], [ckvF], [p])
                cp(ost[:, bi, 0:128], p[:, 0:128], [p], [ost])
                p = ps()
                tr(p[:, 0:32], krF[0:32, bi * 128:(bi + 1) * 128], [krF], [p], n=32)
                cp(ost[:, bi, 128:160], p[:, 0:32], [p], [ost])
            dma('sync', nckv_d[seq, l].rearrange("(b p) d -> p b d", p=128), ost[:, :, 0:128], r=[ost])
            dma('sync', nkr_d[seq, l].rearrange("(b p) d -> p b d", p=128), ost[:, :, 128:160], r=[ost])
        vaug = sb([128, nkb, 2, 192], BF16, ph, "vaugC")
        memset(vaug, vaug[:, :, :, 64:128], 1.0)
        wv4 = wukv[:].rearrange("p (h e) -> p h e", e=128)
        for j in range(nkb):
            p = ps()
            for h in range(4):
                mm(p[:, h * 64:(h + 1) * 64], ckvT[:, j * 128:(j + 1) * 128], wv4[:, h, 64:128], True, True, [ckvT, wukv], [p], inc=(h == 3))
            for pr_ in range(2):
                cp(vaug[:, j, pr_, 0:64], p[:, pr_ * 128:pr_ * 128 + 64], [p], [vaug], eng='vector')
                cp(vaug[:, j, pr_, 128:192], p[:, pr_ * 128 + 64:pr_ * 128 + 128], [p], [vaug], eng='vector')
        qhT = sb([128, L], BF16, ph, "qhT")
        memset(qhT, qhT[64:128, :], 0.0)
        rc = tmp
        PT = [sb([128, TL], BF16, ph, "PTC%d" % i) for i in range(3 if latent else 2)]
        pti = 0
        scale = 96 ** -0.5
        for h in range(4):
            c, g = h // 2, h % 2
            for k0 in range(0, LK, 512):
                n = min(512, LK - k0)
                p = ps()
                mm(p[0:64, 0:n], wv4[:, h, 0:64], ckvT[:, k0:k0 + n], True, True, [wukv, ckvT], [p])
                cp(KhT[0:64, k0:k0 + n], p[0:64, 0:n], [p], [KhT], eng='scalar')
            for t0 in range(0, L, TL):
                p = ps()
                for kc_ in range(2):
                    mm(p[0:96, 0:TL], wuq[:, kc_, h * 96:(h + 1) * 96], cqn[:, kc_, t0:t0 + TL], kc_ == 0, kc_ == 1, [wuq, cqn], [p], inc=(kc_ == 1))
                if latent:
                    p2 = ps()
                    for kc_ in range(2):
                        mm(p2[0:96, 0:TL], wuqs[:, kc_, h, :], cqn[:, kc_, t0:t0 + TL], kc_ == 0, kc_ == 1, [wuqs, cqn], [p2], inc=(kc_ == 1))
                    cp(qhT[0:64, t0:t0 + TL], p[0:64, 0:TL], [p], [qhT], eng='scalar')
                    rcC, rsC = rope_tiles(t0)
                    tt(t1[64:96, 0:TL], p[64:96, 0:TL], rcC[64:96, 0:TL], ALU.mult, [p, rcC], [t1])
                    tt(t2[64:96, 0:TL], p2[64:96, 0:TL], rsC[64:96, 0:TL], ALU.mult, [p2, rsC], [t2])
                    tt(qhT[64:96, t0:t0 + TL], t1[64:96, 0:TL], t2[64:96, 0:TL], ALU.add, [t1, t2], [qhT])
                else:
                    cp(qhT[0:96, t0:t0 + TL], p[0:96, 0:TL], [p], [qhT], eng='scalar')
            for t0 in range(0, L, TL):
                pacc = ps(hold=True)
                pend = None
                for j in range(nkb):
                    p = ps()
                    mm(p[:, 0:TL], KhT[:, j * 128:(j + 1) * 128], qhT[:, t0:t0 + TL], True, True, [KhT, qhT], [p])
                    P_ = PT[pti % len(PT)]
                    pti += 1
                    act(P_[:, 0:TL], p[:, 0:TL], AF.Exp, [p], [P_], scale=scale)
                    if pend is not None:
                        mm(pacc[:, 0:TL], vaug[:, pend[0], c, g * 64:g * 64 + 128], pend[1][:, 0:TL], pend[0] == 0, False, [vaug, pend[1]], [pacc], inc=False)
                    pend = (j, P_)
                mm(pacc[:, 0:TL], vaug[:, pend[0], c, g * 64:g * 64 + 128], pend[1][:, 0:TL], pend[0] == 0, True, [vaug, pend[1]], [pacc], inc=True)
                softmax_norm(pacc, TL, g, mixT, c, t0, None, rc)
                release(pacc)
        outproj(l, grp, tb, L, mixT, 4, wo)
        if W is None and 'D' in MIX:
            prefetch(('D', l), [(2208, 2464)], l)

    def mixer_B(l, grp, tb, L, latent, nseq, ph):
        TL = 512
        nt = L // TL
        nb = L // 128
        cps = (L // nseq) // HC
        cpt = TL // HC
        cpb = 128 // HC
        mixT = sb([128, 2, L], BF16, ph, "mixB")
        vtok = sb([128, nb, 256], BF16, ph, "vtok")
        WX = get_w(('BX', l), [(512, 1024)], l)
        WY = get_w(('BY', l), [(1024, 1536)], l)
        wo = load_wout(l, 2)
        WG = sb([128, NK, 256], BF16, ph, "wgB")
        dma('gpsimd', WG[:], win_d[l, :, 1536:1792].rearrange("(k p) n -> p k n", p=128), w=[WG])
        for bi in range(nb):
            p = ps()
            for k in range(NK):
                mm(p[:, 0:256], hT[:, k, tb + bi * 128:tb + (bi + 1) * 128], WX[:, k, 256:512], k == 0, k == NK - 1, [hTk[k], WX], [p], inc=(k == NK - 1))
            cp(vtok[:, bi, :], p[:, 0:256], [p], [vtok], eng='scalar')
        hm = sb([128, 2, 128], BF16, ph, "hmask")
        dma('gpsimd', hm[:], hmask_d.rearrange("r s t -> s r t"), w=[hm])
        scm = sb([128, 512], F32, ph, "scanm")
        dma('sync', scm[:], scanmask_d, w=[scm])
        cm = sb([128, 4], F32, ph, "cmask")
        dma('sync', cm[:], cmask_d, w=[cm])
        oacc = sb([128, L], F32, ph, "oacc")
        Szero = sb([128, 64], F32, ph, "Szero")
        memset(Szero, Szero[:], 0.0)

        class X_:
            pass
        st = []
        for dr in range(2):
            X = X_()
            for nm in ("f_", "lf", "bb", "kk", "e1", "e2"):
                setattr(X, nm, sb([128, 512], F32, ph, "h%s%d" % (nm, dr)))
            X.qq = sb([128, 512], BF16, ph, "qq%d" % dr)
            X.kd = sb([128, 512], BF16, ph, "kd%d" % dr)
            X.kutok = sb([128, 4, 128], BF16, ph, "kutok%d" % dr)
            X.Sprev = sb([128, cpt, 64], BF16, ph, "Sprev%d" % dr)
            X.gam = sb([128, cpt], F32, ph, "gam%d" % dr)
            X.vexp = sb([128, 2, cpb, 64], BF16, ph, "vexp%d" % dr)
            X.At = [sb([128, 128], BF16, ph, "At%d_%d" % (dr, i)) for i in range(2)]
            X.Sall = [sb([128, cpt + 1, 64], F32, ph, "Sall%d" % dr)]
            st.append(X)

        for pr_ in range(2):
            touched = set()

            def stream(dr):
                X = st[dr]
                f_, lf, bb, kk, e1, e2 = X.f_, X.lf, X.bb, X.kk, X.e1, X.e2
                lb_ = lbv[:, l, dr, pr_:pr_ + 1]
                om_ = oml[:, l, dr, pr_:pr_ + 1]
                if latent:
                    dma('sync', X.Sall[0][:, 0, :], st_d[l, dr, 2 * pr_:2 * pr_ + 2].rearrange("h d v -> (h d) v"), w=[X.Sall[0]])
                tiles = list(range(nt)) if dr == 0 else list(range(nt - 1, -1, -1))
                for tidx, ti in enumerate(tiles):
                    SA = X.Sall[0]
                    if tidx > 0:
                        cp(SA[:, 0, :], SA[:, cpt, :], [SA], [SA])
                    jj = 0
                    t0 = ti * TL
                    pf = ps(hold=True)
                    proj_fm(ph, WY, (dr * 256 + pr_ * 128, dr * 256 + pr_ * 128 + 128), 128, tb + t0, TL, pf)
                    pq = ps(hold=True)
                    proj_fm(ph, WX, (pr_ * 128, pr_ * 128 + 128), 128, tb + t0, TL, pq)
                    yield
                    act(f_[:], pf[:], AF.Exp, [pf], [f_], scale=-1.0)
                    release(pf)
                    act(lf[:], f_[:], AF.Ln, [f_, oneT], [lf], bias=oneT[:, 0:1], scale=1.0)
                    act(f_[:], lf[:], AF.Exp, [lf], [f_], scale=-1.0)
                    yield
                    ts(f_[:], f_[:], om_, lb_, ALU.mult, ALU.add, [f_, oml, lbv], [f_])
                    act(lf[:], f_[:], AF.Ln, [f_], [lf])
                    ts(kk[:], f_[:], -1.0, 1.0, ALU.mult, ALU.add, [f_], [kk], eng='gpsimd')
                    yield
                    op('vector', lambda e: e.tensor_tensor_scan(out=bb[:], data0=scm[:], data1=lf[:], initial=0.0,
                                                                op0=ALU.mult, op1=ALU.add), r=[scm, lf], w=[bb])
                    b3 = bb[:].rearrange("p (n c) -> p n c", c=HC)
                    tot = b3[:, :, HC - 1:HC]
                    act(X.gam[:], b3[:, :, HC - 1], AF.Exp, [bb], [X.gam])
                    if dr == 1:
                        tt(e1[:].rearrange("p (n c) -> p n c", c=HC), tot.broadcast_to([128, cpt, HC]), b3, ALU.subtract, [bb], [e1])
                        tt(e2[:], bb[:], lf[:], ALU.subtract, [bb, lf], [e2])
                        yield
                        tt(bb[:], e1[:], lf[:], ALU.add, [e1, lf], [bb])
                    else:
                        tt(e2[:].rearrange("p (n c) -> p n c", c=HC), tot.broadcast_to([128, cpt, HC]), b3, ALU.subtract, [bb], [e2])
                    yield
                    act(e1[:], bb[:], AF.Exp, [bb], [e1])
                    tt(X.qq[:], pq[:], e1[:], ALU.mult, [pq, e1], [X.qq])
                    release(pq)
                    yield
                    act(f_[:], bb[:], AF.Exp, [bb], [f_], scale=-1.0)
                    tt(X.kd[:], kk[:], f_[:], ALU.mult, [kk, f_], [X.kd])
                    yield
                    act(e2[:], e2[:], AF.Exp, [e2], [e2])
                    tt(e2[:], kk[:], e2[:], ALU.mult, [kk, e2], [e2])
                    yield
                    for q in range(4):
                        p = ps()
                        tr(p[:, 0:128], e2[:, q * 128:(q + 1) * 128], [e2], [p])
                        cp(X.kutok[:, q, :], p[:, 0:128], [p], [X.kutok], eng='scalar')
                    yield
                    bqs = list(range(4)) if dr == 0 else list(range(3, -1, -1))
                    for bq in bqs:
                        bi = ti * 4 + bq
                        tt(X.vexp[:], vtok[:, bi, pr_ * 128:(pr_ + 1) * 128].rearrange("p (h v) -> p h v", h=2).unsqueeze(2).broadcast_to([128, 2, cpb, 64]),
                           cm[:, 0:cpb].unsqueeze(1).unsqueeze(3).broadcast_to([128, 2, cpb, 64]), ALU.mult, [vtok, cm], [X.vexp], eng='gpsimd')
                        pU = ps(hold=True)
                        for hh in range(2):
                            mm(pU[hh * 64:(hh + 1) * 64, 0:cpb * 64], X.kutok[:, bq, hh * 64:(hh + 1) * 64],
                               X.vexp[:, hh, :, :].rearrange("p n v -> p (n v)"), True, True, [X.kutok, X.vexp], [pU], inc=(hh == 1), tp=(0, hh * 64))
                        yield
                        chs = list(range(cpb)) if dr == 0 else list(range(cpb - 1, -1, -1))
                        for cj in chs:
                            nl = bq * cpb + cj
                            ng = ti * cpt + nl
                            first = (ng % cps == 0) if dr == 0 else (ng % cps == cps - 1)
                            last = (ng % cps == cps - 1) if dr == 0 else (ng % cps == 0)
                            if first and not latent:
                                cp(SA[:, jj, :], Szero[:], [Szero], [SA])
                            stt(SA[:, jj + 1, :], SA[:, jj, :], X.gam[:, nl:nl + 1], pU[:, cj * 64:(cj + 1) * 64], ALU.mult, ALU.add, [SA, X.gam, pU], [SA])
                            jj += 1
                            if last and not latent:
                                dma('sync', nst_d[ng // cps, l, dr, 2 * pr_:2 * pr_ + 2].rearrange("h d v -> (h d) v"), SA[:, jj, :], r=[SA])
                        yield
                        release(pU)
                    cp(X.Sprev[:], SA[:, 0:cpt, :], [SA], [X.Sprev], eng='scalar')
                    po = ps(hold=True)
                    for q in range(4):
                        bi = ti * 4 + q
                        cs = slice(q * 128, (q + 1) * 128)
                        for hh in range(2):
                            rs = slice(hh * 64, (hh + 1) * 64)
                            pA = ps()
                            mm(pA[:, 0:128], X.kd[rs, cs], X.qq[rs, cs], True, True, [X.kd, X.qq], [pA], tp=(hh * 64, 0))
                            A_ = X.At[hh]
                            tt(A_[:], pA[:, 0:128], hm[:, dr, :], ALU.mult, [pA, hm], [A_])
                            mm(po[rs, cs], vtok[:, bi, pr_ * 128 + hh * 64:pr_ * 128 + (hh + 1) * 64], A_[:], True, False,
                               [vtok, A_], [po], inc=False, tp=(0, hh * 64))
                            for cj in range(cpb):
                                nl = q * cpb + cj
                                lastm = (q == 3 and hh == 1 and cj == cpb - 1)
                                mm(po[rs, q * 128 + cj * HC:q * 128 + (cj + 1) * HC], X.Sprev[rs, (nl if dr == 0 else cpt - 1 - nl), :], X.qq[rs, q * 128 + cj * HC:q * 128 + (cj + 1) * HC],
                                   False, cj == cpb - 1, [X.Sprev, X.qq], [po], inc=(lastm or cj == cpb - 1), tp=(hh * 64, hh * 64))
                        yield
                    if ti not in touched:
                        touched.add(ti)
                        cp(oacc[:, t0:t0 + TL], po[:], [po], [oacc], eng='scalar')
                    else:
                        tt(oacc[:, t0:t0 + TL], oacc[:, t0:t0 + TL], po[:], ALU.add, [oacc, po], [oacc])
                    release(po)
                    yield

            gens = [stream(0), stream(1)]
            alive = [True, True]
            while any(alive):
                for gi in range(2):
                    if alive[gi]:
                        try:
                            next(gens[gi])
                        except StopIteration:
                            alive[gi] = False
            X = st[0]
            Y = st[1]
            for t0 in range(0, L, TL):
                Z = X if (t0 // TL) % 2 == 0 else Y
                act(Z.e1[:], oacc[:, t0:t0 + TL], AF.Square, [oacc], [Z.e1])
                cp(Z.kd[:], Z.e1[:], [Z.e1], [Z.kd])
                p = ps()
                mm(p[:], bd64[:], Z.kd[:], True, True, [bd64, Z.kd], [p])
                rstd_of(Z.e2[:], p[:], [p], [Z.e2])
                tt(Z.e1[:], oacc[:, t0:t0 + TL], Z.e2[:], ALU.mult, [oacc, Z.e2], [Z.e1])
                pg = ps()
                proj_fm(ph, WG, (pr_ * 128, pr_ * 128 + 128), 128, tb + t0, TL, pg)
                act(Z.f_[:], pg[:], AF.Exp, [pg], [Z.f_], scale=-1.0)
                act(Z.lf[:], Z.f_[:], AF.Ln, [Z.f_, oneT], [Z.lf], bias=oneT[:, 0:1], scale=1.0)
                act(Z.lf[:], Z.lf[:], AF.Exp, [Z.lf], [Z.lf], scale=-1.0)
                tt(Z.f_[:], pg[:], Z.lf[:], ALU.mult, [pg, Z.lf], [Z.f_])
                stt(mixT[:, pr_, t0:t0 + TL], Z.e1[:], hnT[:, l:l + 1], Z.f_[:], ALU.mult, ALU.mult, [Z.e1, hnT, Z.f_], [mixT])
        outproj(l, grp, tb, L, mixT, 2, wo)
        if latent and 'A' in MIX:
            prefetch(('A', l), [(0, 512)], l)

    def mixer_layer(T, l, grp, latent, nseq):
        L = T // nseq
        mod_consume()
        if l == 0:
            if mod_done[0] < 18:
                for _ in range(18 - mod_done[0]):
                    mod_step(1)
        else:
            for _ in range(36 - mod_done[0]):
                mod_step(1)
        S.mark("mixnorm l%d g%d" % (l, grp))
        with ExitStack() as ph:
            normmod(T, (lambda k: coefA[:, l, 1, grp, k:k + 1], [coefA]), (lambda k: shiftC[:, l, 1, grp, k:k + 1], [shiftC]),
                    lambda k, t0: hT[:, k, t0:t0 + 512], hTk, ph)
        S.barrier()
        if 'B' in MIX:
            S.mark("mixB l%d g%d s0" % (l, grp))
            with ExitStack() as ph:
                mixer_B(l, grp, 0, T, latent, nseq, ph)
            S.barrier()
        with ExitStack() as wph:
            Ws = {'A': None, 'C': None, 'D': None}
            if nseq > 1:
                def wtile(c0, c1, nm):
                    t = sb([128, NK, c1 - c0], BF16, wph, nm)
                    dma('gpsimd', t[:], win_d[l, :, c0:c1].rearrange("(k p) n -> p k n", p=128), w=[t])
                    return t

                def wo_tile(c0, nm):
                    return load_wout(l, c0)
                _wa = wtile(0, 512, "WAp")
                _w2 = sb([128, NK, 256], BF16, wph, "W2p")
                build_kdup(_wa, _w2)
                Ws['A'] = (_wa, _w2, wo_tile(0, "WoA"))
                Ws['C'] = (wtile(1792, 2208, "WCp"),) + load_wu(l, wph) + (wo_tile(4, "WoC"),)
                Ws['D'] = (wtile(2208, 2464, "WDp"), load_pw(l, wph), wo_tile(6, "WoD"))
            for name, fn in (('A', mixer_A), ('C', mixer_C), ('D', mixer_D)):
                if name not in MIX:
                    continue
                S.mark("mix%s l%d g%d s0" % (name, l, grp))
                with ExitStack() as ph:
                    for seq in range(nseq):
                        fn(l, grp, seq * L, L, latent, seq, ph, Ws[name])
                S.barrier()

    def run_pass(x_d, y_d, T, grp, latent, nseq):
        load_xT(x_d, T)
        for l in range(2):
            ffn(T, l, 0, grp, 0)
            mixer_layer(T, l, grp, latent, nseq)
            ffn(T, l, 2, grp, 1)
        final_out(T, y_d)

    if flags.get('sample', True):
        run_pass(xs_d, ys_d, 2048, 1, True, 1)
    if flags.get('prompt', True):
        run_pass(xp_d, yp_d, 1024, 0, False, 4)
    S.mark("end")
    S.finish()
    es.close()
    return nc, S


def _rope_tables(dim, nrows_tab, row_slices):
    n_freq = dim // 4
    half = dim // 2
    inv = (10000.0 ** (-np.arange(n_freq, dtype=np.float32) / n_freq)).astype(np.float32)
    t = np.arange(2048)
    row_id = (t // 64).astype(np.float32)
    col_id = (t % 64).astype(np.float32)
    ang = np.concatenate([row_id[:, None] * inv, col_id[:, None] * inv], axis=-1).astype(np.float32)
    cos = np.cos(ang).astype(np.float32)
    sin = np.sin(ang).astype(np.float32)
    C = np.zeros((128, 2048), np.float32)
    Sg = np.zeros((128, 2048), np.float32)
    for (r0, n) in row_slices:
        for r in range(n):
            d = r % dim
            j = d % half
            C[r0 + r] = cos[:, j]
            Sg[r0 + r] = sin[:, j] * (-1.0 if d < half else 1.0)
    return C, Sg


_CONST = {}


def _consts():
    if _CONST:
        return _CONST
    c = {}
    c['ident'] = np.eye(128, dtype=np.float32)
    c['ropeA_c'], c['ropeA_s'] = _rope_tables(64, 128, [(0, 128)])
    c['ropeC_c'], c['ropeC_s'] = _rope_tables(32, 128, [(0, 32), (64, 32)])
    b = np.arange(128)[:, None]
    a = np.arange(128)[None, :]
    m = np.zeros((128, 384), np.float32)
    m[:, 0:128] = np.where(b <= a, 0.0, -240000.0)
    m[:, 256:384] = np.where(a <= b, 0.0, -240000.0)
    c['winmask'] = m
    s = np.arange(128)[:, None]
    t = np.arange(128)[None, :]
    same = (s // HC) == (t // HC)
    c['hmask'] = np.stack([(same & (s <= t)), (same & (s >= t))]).astype(np.float32)
    sc = np.ones((128, 512), np.float32)
    sc[:, ::HC] = 0.0
    c['scanmask'] = sc
    c['cmask'] = (np.arange(128)[:, None] // HC == np.arange(4)[None, :]).astype(np.float32)
    pm = np.zeros((4, 128, 128), np.float32)
    for m in range(128):
        pm[0, m + 32 if (m % 64) < 32 else m - 32, m] = 1.0
        if m < 32 or 64 <= m < 96:
            pm[1, m + 16 if (m % 32) < 16 else m - 16, m] = 1.0
        else:
            pm[1, m, m] = 1.0
        pm[2, m % 64, m] = 1.0
        pm[3, 64 + m % 64, m] = 1.0
    c['permm'] = pm
    for nm, L in (('invcntS', 2048), ('invcntP', 256)):
        inv = np.zeros((128, 2, L), np.float32)
        pos = np.arange(L)
        for g, w in enumerate(POOL_WINDOWS):
            lo = np.clip(pos - w // 2, 0, L)
            hi = np.clip(pos - w // 2 + w, 0, L)
            inv[(g % 2) * 64:(g % 2) * 64 + 64, g // 2, :] = (1.0 / (hi - lo).astype(np.float32))[None, :]
        c[nm] = inv
    _CONST.update(c)
    return _CONST


_NC = {}


def kernel(x_prompt, x_sample, c, c_ctx, cache_attn_k, cache_attn_v, state_hgrn, cache_mla_ckv,
           cache_mla_krope, w_ada, b_ada, norm_sub, w_ffn_gate, w_ffn_up, w_ffn_down, w_in, w_out,
           attn_sink, hgrn_lb_logits, hgrn_out_norm, mla_q_norm, mla_kv_norm, mla_w_uq, mla_w_ukv,
           pool_w, pool_scale, final_norm, _flags=None):
    f32 = lambda a: np.ascontiguousarray(np.asarray(a, dtype=np.float32))
    key = repr(_flags)
    if key not in _NC:
        _NC[key] = build(_flags)[0]
    nc = _NC[key]
    cs = _consts()
    x_prompt, x_sample, c, c_ctx = f32(x_prompt), f32(x_sample), f32(c), f32(c_ctx)
    b_ada, norm_sub = f32(b_ada), f32(norm_sub)
    shared = {
        "hnorm": f32(hgrn_out_norm), "sink": f32(attn_sink), "w_ada": f32(w_ada), "w_gate": f32(w_ffn_gate), "w_up": f32(w_ffn_up),
        "w_down": f32(w_ffn_down), "w_in": f32(w_in), "w_out": f32(w_out), "w_uq": f32(mla_w_uq), "w_ukv": f32(mla_w_ukv),
        "pool_w": f32(pool_w),
    }
    shared.update(cs)
    vecA = np.concatenate([b_ada[0].reshape(72, 128), norm_sub.reshape(48, 128)], axis=0)
    in_maps = []
    for b in range(8):
        vecB = np.concatenate([b_ada[1].reshape(72, 128), c_ctx.reshape(8, 128), c[b].reshape(8, 128),
                               f32(mla_q_norm).reshape(4, 128), f32(mla_kv_norm).reshape(2, 128), f32(pool_scale).reshape(4, 128),
                               f32(hgrn_lb_logits).reshape(8, 128), f32(final_norm).reshape(8, 128)], axis=0)
        m = dict(shared)
        m.update({
            "xs": x_sample[b], "xp": x_prompt[4 * b:4 * b + 4].reshape(1024, 1024), "vecA": vecA, "vecB": np.ascontiguousarray(vecB),
            "cache_k": f32(cache_attn_k[b]), "cache_v": f32(cache_attn_v[b]), "state": f32(state_hgrn[b]),
            "cache_ckv": f32(cache_mla_ckv[b]), "cache_kr": f32(cache_mla_krope[b]),
        })
        in_maps.append(m)
    res = run_bass_kernel_spmd(nc, in_maps, core_ids=list(range(8)))
    R = res.results
    y_p = np.concatenate([r["y_p"].reshape(4, 256, 1024) for r in R], axis=0)
    y_s = np.stack([r["y_s"] for r in R], axis=0)
    nk = np.concatenate([r["new_k"] for r in R], axis=0)
    nv = np.concatenate([r["new_v"] for r in R], axis=0)
    nst = np.concatenate([r["new_st"] for r in R], axis=0)
    nckv = np.concatenate([r["new_ckv"] for r in R], axis=0)
    nkr = np.concatenate([r["new_kr"] for r in R], axis=0)
    return (y_p.astype(np.float32), y_s.astype(np.float32), nk.astype(np.float32), nv.astype(np.float32),
            nst.astype(np.float32), nckv.astype(np.float32), nkr.astype(np.float32))
```

```python
from contextlib import ExitStack
import numpy as np
import concourse.bass as bass
import concourse.mybir as mybir
from concourse.bass_utils import run_bass_kernel_spmd

F32 = mybir.dt.float32
BF16 = mybir.dt.bfloat16
AF = mybir.ActivationFunctionType
ALU = mybir.AluOpType
ENGS = ['tensor', 'vector', 'scalar', 'gpsimd', 'sync']

D = 1024
NK = 8
DFF = 2816
NF = 22
FG = 2
DIN = 2464
HC = 32
EPS = 1e-6
POOL_WINDOWS = (2, 4, 8, 16)


class Buf:
    def __init__(self, name):
        self.name = name
        self.lw = None
        self.rd = {}


class Sched:
    def __init__(self, nc, es):
        self.engh = {'tensor': nc.tensor, 'vector': nc.vector, 'scalar': nc.scalar, 'gpsimd': nc.gpsimd, 'sync': nc.sync}
        self.nc = nc
        self.es = es
        self.sems = {}
        self.val = {}
        self.seen = {e: {} for e in ENGS}
        self.pend_r = {e: [] for e in ENGS}
        self.pend_w = {e: [] for e in ENGS}
        for e in ENGS:
            self._mksem(e)
        self.ninst = 0
        self.nops = {e: 0 for e in ENGS}
        self.marks = []

    def mark(self, name):
        self.marks.append((name, dict(self.nops)))

    def _mksem(self, key):
        h = self.es.enter_context(self.nc.semaphore("s%d" % len(self.sems)))
        self.sems[key] = h
        self.val[key] = 0
        return h

    def _wait(self, eng, deps):
        best = {}
        for d in deps:
            if d is None:
                continue
            k, v = d
            if v > best.get(k, 0):
                best[k] = v
        for k, v in best.items():
            if self.seen[eng].get(k, 0) < v:
                self.seen[eng][k] = v
                self.engh[eng].wait_ge(self.sems[k], v)
                self.ninst += 1

    def _deps(self, r, w):
        deps = []
        for b in r:
            deps.append(b.lw)
        for b in w:
            deps.append(b.lw)
            deps.extend(b.rd.items())
        return deps

    def op(self, eng, fn, r=(), w=(), inc=True):
        self._wait(eng, self._deps(r, w))
        self.pend_r[eng].extend(r)
        self.pend_w[eng].extend(w)
        self.ninst += 1
        self.nops[eng] += 1
        if inc:
            self.val[eng] += 1
            v = self.val[eng]
            fn(self.engh[eng]).then_inc(self.sems[eng], 1)
            for b in self.pend_w[eng]:
                b.lw = (eng, v)
                b.rd = {}
            for b in self.pend_r[eng]:
                if b.lw == (eng, v):
                    continue
                b.rd[eng] = v
            self.pend_r[eng] = []
            self.pend_w[eng] = []
        else:
            fn(self.engh[eng])

    def dma(self, eng, fn, r=(), w=()):
        self._wait(eng, self._deps(r, w))
        owner = (list(w) + list(r))[0]
        key = ('dma', eng, owner.name)
        if key not in self.sems:
            self._mksem(key)
        self.val[key] += 16
        v = self.val[key]
        fn(self.engh[eng]).then_inc(self.sems[key], 16)
        self.ninst += 1
        for b in w:
            b.lw = (key, v)
            b.rd = {}
        for b in r:
            b.rd[key] = v

    def barrier(self):
        deps = [(k, v) for k, v in self.val.items() if v > 0]
        for e in ENGS:
            self._wait(e, deps)

    def finish(self):
        self.barrier()


class TT:
    def __init__(self, t, name):
        self.t = t
        self.b = Buf(name)

    def __getitem__(self, k):
        return self.t[k]


def build(flags=None):
    flags = flags or {}
    MIX = flags.get('mix', 'ABCD')
    nc = bass.Bass("TRN2", target_bir_lowering=False)
    es = ExitStack()
    S = Sched(nc, es)

    def din(name, shape):
        return nc.dram_tensor(name, list(shape), F32, kind="ExternalInput").ap()

    def dout(name, shape):
        return nc.dram_tensor(name, list(shape), F32, kind="ExternalOutput").ap()

    xs_d = din("xs", [2048, D])
    xp_d = din("xp", [1024, D])
    vecA_d = din("vecA", [120, 128])
    vecB_d = din("vecB", [114, 128])
    hnorm_d = din("hnorm", [2, 64])
    sink_d = din("sink", [2, 4])
    ck_d = din("cache_k", [2, 2, 256, 64])
    cv_d = din("cache_v", [2, 2, 256, 64])
    st_d = din("state", [2, 2, 4, 64, 64])
    cckv_d = din("cache_ckv", [2, 256, 128])
    ckr_d = din("cache_kr", [2, 256, 32])
    wada_d = din("w_ada", [2, D, 9 * D])
    wg_d = din("w_gate", [2, 2, D, DFF])
    wu_d = din("w_up", [2, 2, D, DFF])
    wd_d = din("w_down", [2, 2, DFF, D])
    win_d = din("w_in", [2, D, DIN])
    wout_d = din("w_out", [2, D, D])
    wuq_d = din("w_uq", [2, 256, 384])
    wukv_d = din("w_ukv", [2, 128, 512])
    poolw_d = din("pool_w", [2, 4, 64, 64])
    ident_d = din("ident", [128, 128])
    ropeA_c_d = din("ropeA_c", [128, 2048])
    ropeA_s_d = din("ropeA_s", [128, 2048])
    ropeC_c_d = din("ropeC_c", [128, 2048])
    ropeC_s_d = din("ropeC_s", [128, 2048])
    winmask_d = din("winmask", [128, 384])
    hmask_d = din("hmask", [2, 128, 128])
    scanmask_d = din("scanmask", [128, 512])
    cmask_d = din("cmask", [128, 4])
    invS_d = din("invcntS", [128, 2, 2048])
    invP_d = din("invcntP", [128, 2, 256])
    perm_d = din("permm", [4, 128, 128])

    ys_d = dout("y_s", [2048, D])
    yp_d = dout("y_p", [1024, D])
    nk_d = dout("new_k", [4, 2, 2, 256, 64])
    nv_d = dout("new_v", [4, 2, 2, 256, 64])
    nst_d = dout("new_st", [4, 2, 2, 4, 64, 64])
    nckv_d = dout("new_ckv", [4, 2, 256, 128])
    nkr_d = dout("new_kr", [4, 2, 256, 32])

    cnt = [0]
    live_names = {}

    def sb(shape, dt, stack=None, name=None):
        cnt[0] += 1
        nm = "%s_%d" % (name or "t", cnt[0])
        t = (stack or es).enter_context(nc.sbuf_tensor(nm, list(shape), dt))
        key = (id(stack or es), name or nm)
        n_ = live_names.get(key, 0)
        live_names[key] = n_ + 1
        return TT(t, (name or nm) + ("#%d" % n_ if n_ else ""))

    def op(eng, fn, r=(), w=(), inc=True):
        S.op(eng, fn, r=[x.b for x in r], w=[x.b for x in w], inc=inc)

    def dma(eng, out, in_, r=(), w=(), slow=False):
        if slow:
            S.dma(eng, lambda e: e.dma_start(out=out, in_=in_, allow_slow_non_contiguous=True),
                  r=[x.b for x in r], w=[x.b for x in w])
        else:
            S.dma(eng, lambda e: e.dma_start(out=out, in_=in_), r=[x.b for x in r], w=[x.b for x in w])

    def mm(out, lhsT, rhs, start, stop, r, w, inc=True, tp=None):
        if tp is None:
            op('tensor', lambda e: e.matmul(out, lhsT=lhsT, rhs=rhs, start=start, stop=stop), r=r, w=w, inc=inc)
        else:
            op('tensor', lambda e: e.matmul(out, lhsT=lhsT, rhs=rhs, start=start, stop=stop, tile_position=tp),
               r=r, w=w, inc=inc)

    def tr(out, in_, r, w, n=128):
        op('tensor', lambda e: e.transpose(out=out, in_=in_, identity=ident_f[0:n, 0:n]), r=list(r) + [ident_f], w=w)

    def act(out, in_, func, r, w, bias=None, scale=None):
        kw = {}
        if bias is not None:
            kw['bias'] = bias
        if scale is not None:
            kw['scale'] = scale
        op('scalar', lambda e: e.activation(out=out, in_=in_, func=func, **kw), r=r, w=w)

    def tt(out, in0, in1, alu, r, w, eng='vector'):
        op(eng, lambda e: e.tensor_tensor(out=out, in0=in0, in1=in1, op=alu), r=r, w=w)

    def ts(out, in0, s1, s2, op0, op1, r, w, eng='vector'):
        if op1 is None:
            op(eng, lambda e: e.tensor_scalar(out=out, in0=in0, scalar1=s1, scalar2=None, op0=op0), r=r, w=w)
        else:
            op(eng, lambda e: e.tensor_scalar(out=out, in0=in0, scalar1=s1, scalar2=s2, op0=op0, op1=op1), r=r, w=w)

    def rstd_of(out, in_, r, w):
        act(out, in_, AF.Ln, list(r) + [epsT], w, bias=epsT[:, 0:1], scale=1.0)
        act(out, out, AF.Exp, w, w, scale=-0.5)

    def lnexp_tables():
        pass

    def stt(out, in0, scalar, in1, op0, op1, r, w, eng='vector'):
        op(eng, lambda e: e.scalar_tensor_tensor(out=out, in0=in0, scalar=scalar, in1=in1, op0=op0, op1=op1), r=r, w=w)

    def cp(out, in_, r, w, eng='vector'):
        if eng == 'scalar':
            act(out, in_, AF.Copy, r, w)
        else:
            op(eng, lambda e: e.tensor_copy(out=out, in_=in_), r=r, w=w)

    def memset(t, ap, val, eng='vector'):
        op(eng, lambda e: e.memset(ap, val), r=(), w=[t])

    PSB = [TT(es.enter_context(nc.psum_tensor("ps%d" % i, [128, 512], F32)), "ps%d" % i) for i in range(8)]
    xT = sb([128, NK, 2048], F32, name="xT")
    hT = sb([128, NK, 2048], BF16, name="hT")
    hTk = []
    for _k in range(NK):
        _v = TT(hT.t, "hT%d" % _k)
        hTk.append(_v)
    ident_f = sb([128, 128], F32, name="identf")
    ident_b = sb([128, 128], BF16, name="identb")
    onesD = sb([128, 128], BF16, name="onesD")
    ones256 = sb([128, 128], BF16, name="ones256")
    ones128 = sb([128, 128], BF16, name="ones128")
    bd64 = sb([128, 128], BF16, name="bd64")
    epsT = sb([128, 1], F32, name="eps")
    oneT = sb([128, 1], F32, name="one")
    vecTA = sb([128, 120], F32, name="vecTA")
    vecTB = sb([128, 114], F32, name="vecTB")
    modT = sb([128, 2, 72, 2], F32, name="modT")
    coefA = sb([128, 2, 3, 2, NK], F32, name="coefA")
    gateC = sb([128, 2, 3, 2, NK], F32, name="gateC")
    shiftC = sb([128, 2, 3, 2, NK], F32, name="shiftC")
    scT = sb([128, NK, 2], BF16, name="scT")
    hnT = sb([128, 2], F32, name="hnT")
    esink = sb([128, 8], F32, name="esink")
    lbv = sb([128, 2, 2, 2], F32, name="lbv")
    oml = sb([128, 2, 2, 2], F32, name="oml")
    wslots = [sb([128, NK, 512], BF16, name="wslot%d" % i) for i in range(3)]
    wsl_i = [0]

    def next_wslot():
        w = wslots[wsl_i[0] % 3]
        wsl_i[0] += 1
        return w

    psi = [0]
    held = set()

    def ps(hold=False):
        while True:
            i = psi[0] % 8
            psi[0] += 1
            if i not in held:
                break
        if hold:
            held.add(i)
        return PSB[i]

    def release(p):
        held.discard(PSB.index(p))

    dma('sync', ident_f[:], ident_d, w=[ident_f])
    dma('gpsimd', ident_b[:], ident_d, w=[ident_b])
    memset(onesD, onesD[:], 1.0 / 1024)
    memset(ones256, ones256[:], 1.0 / 256)
    memset(ones128, ones128[:], 1.0 / 128)
    memset(bd64, bd64[:], 0.0)
    memset(bd64, bd64[0:64, 0:64], 1.0 / 64)
    memset(bd64, bd64[64:128, 64:128], 1.0 / 64)
    memset(epsT, epsT[:], EPS)
    memset(oneT, oneT[:], 1.0)
    with ExitStack() as ph:
        stg = sb([128, 128], F32, ph, "stg")
        for (src, n, dst) in ((vecA_d, 120, vecTA), (vecB_d, 114, vecTB)):
            dma('sync', stg[0:n, :], src, w=[stg])
            p = ps()
            tr(p[:, 0:n], stg[0:n, :], [stg], [p], n=n)
            cp(dst[:, 0:n], p[:, 0:n], [p], [dst])
        dma('sync', hnT[0:64, :], hnorm_d.rearrange("l v -> v l"), w=[hnT], slow=True)
        dma('sync', hnT[64:128, :], hnorm_d.rearrange("l v -> v l"), w=[hnT], slow=True)
        sk = sb([128, 8], F32, ph, "sk")
        dma('sync', sk[:], sink_d.rearrange("l h -> (l h)").partition_broadcast(128), w=[sk], slow=True)
        act(esink[:], sk[:], AF.Exp, [sk], [esink])
        lbl = vecTB[:, 98:106].rearrange("p (l r) -> p l r", l=2)
        ex = sb([128, 2, 4], F32, ph, "ex")
        act(ex[:], lbl, AF.Exp, [vecTB], [ex])
        sm = sb([128, 4], F32, ph, "sm")
        tt(sm[:], ex[:, 0, :], ex[:, 1, :], ALU.add, [ex], [sm])
        op('vector', lambda e: e.reciprocal(out=sm[:], in_=sm[:]), r=[sm], w=[sm])
        lbf = lbv[:].rearrange("p l r q -> p l (r q)")
        memset(lbv, lbf[:, 0, :], 0.0)
        tt(lbf[:, 1, :], ex[:, 1, :], sm[:], ALU.mult, [ex, sm], [lbv])
        ts(oml[:].rearrange("p l r q -> p (l r q)"), lbv[:].rearrange("p l r q -> p (l r q)"), -1.0, 1.0, ALU.mult, ALU.add, [lbv], [oml])
        act(scT[:].rearrange("p k g -> p g k"), vecTB[:, 72:88].rearrange("p (g k) -> p g k", g=2), AF.Silu, [vecTB], [scT])
        pass
    S.barrier()
    mblk = sb([2, 512], F32, name="mblk")
    mod_pending = [(l_, cb_) for l_ in range(2) for cb_ in range(18)]
    mod_done = [0]

    def mod_coefs(l, j):
        for g in range(2):
            ng = vecTA[:, 72 + (l * 3 + j) * 8: 72 + (l * 3 + j) * 8 + 8]
            sc_ = modT[:, l, (3 * j + 1) * 8:(3 * j + 2) * 8, g]
            stt(coefA[:, l, j, g, :], sc_, 1.0, ng, ALU.add, ALU.mult, [modT, vecTA], [coefA])
            cp(shiftC[:, l, j, g, :], modT[:, l, (3 * j) * 8:(3 * j + 1) * 8, g], [modT], [shiftC])
            ts(gateC[:, l, j, g, :], modT[:, l, (3 * j + 2) * 8:(3 * j + 3) * 8, g], 0.5 if j != 1 else 1.0, None,
               ALU.mult, None, [modT], [gateC])

    mod_inflight = []

    def mod_issue(n):
        for _ in range(n):
            if not mod_pending:
                return
            l, cb = mod_pending.pop(0)
            wsl = next_wslot()
            dma('gpsimd', wsl[:], wada_d[l, :, cb * 512:(cb + 1) * 512].rearrange("(k p) n -> p k n", p=128), w=[wsl])
            mod_inflight.append((l, cb, wsl))

    def mod_step(n):
        mod_issue(n)
        mod_consume()

    def mod_consume(keep=0):
        while len(mod_inflight) > keep:
            l, cb, wsl = mod_inflight.pop(0)
            p = ps()
            for k in range(NK):
                mm(p[0:2, :], scT[:, k, :], wsl[:, k, :], k == 0, k == NK - 1, [scT, wsl], [p], inc=(k == NK - 1))
            cp(mblk[:], p[0:2, :], [p], [mblk], eng='scalar')
            p2 = ps()
            for c4 in range(4):
                tr(p2[:, c4 * 2:c4 * 2 + 2], mblk[0:2, c4 * 128:(c4 + 1) * 128], [mblk], [p2], n=2)
            bsrc = (vecTA[:, 0:72] if l == 0 else vecTB[:, 0:72])
            bt = vecTA if l == 0 else vecTB
            tt(modT[:, l, cb * 4:(cb + 1) * 4, :], p2[:, 0:8].rearrange("p (c g) -> p c g", g=2),
               bsrc[:, cb * 4:(cb + 1) * 4].unsqueeze(2).broadcast_to([128, 4, 2]), ALU.add, [p2, bt], [modT])
            mod_done[0] += 1
            if cb % 6 == 5:
                mod_coefs(l, cb // 6)

    mod_issue(2)

    def mod_prologue():
        for _ in range(4):
            mod_consume(keep=1)
            mod_issue(1)
        mod_consume()
    S.barrier()

    def load_xT(x_d, T):
        S.mark("load")
        with ExitStack() as ph:
            stg = [sb([128, D], F32, ph, "xstg%d" % i) for i in range(2)]
            for bi in range(T // 128):
                s_ = stg[bi % 2]
                dma('sync', s_[:], x_d[bi * 128:(bi + 1) * 128, :], w=[s_])
                for half in range(2):
                    p = ps()
                    for q in range(4):
                        k = half * 4 + q
                        tr(p[:, q * 128:(q + 1) * 128], s_[:, k * 128:(k + 1) * 128], [s_], [p])
                    cp(xT[:, half * 4:half * 4 + 4, bi * 128:(bi + 1) * 128], p[:].rearrange("p (q t) -> p q t", q=4), [p], [xT],
                       eng='vector')
        S.barrier()

    def normmod(T, A, B, out, out_b, ph):
        sqs = [[sb([128, 512], BF16, ph, "sq%d_%d" % (i, k)) for k in range(NK)] for i in range(2)]
        rstds = [sb([128, 512], F32, ph, "rstd%d" % i) for i in range(2)]
        tmps = [sb([128, 512], F32, ph, "nm_tmp%d" % i) for i in range(4)]
        tiles = list(range(0, T, 512))
        lnexp_tables()

        pss = {}

        def s1a(i):
            t0 = tiles[i]
            sq = sqs[i % 2]
            for k in range(NK):
                e_ = ('gpsimd', 'scalar', 'vector', 'scalar', 'gpsimd', 'vector', 'scalar', 'vector')[k]
                if e_ == 'scalar':
                    act(sq[k][:], xT[:, k, t0:t0 + 512], AF.Square, [xT], [sq[k]])
                else:
                    tt(sq[k][:], xT[:, k, t0:t0 + 512], xT[:, k, t0:t0 + 512], ALU.mult, [xT], [sq[k]], eng=e_)
            p = ps(hold=True)
            pss[i] = p
            for k in range(NK):
                mm(p[:], onesD[:], sq[k][:], k == 0, k == NK - 1, [onesD, sq[k]], [p], inc=(k == NK - 1))

        def s1b(i):
            p = pss.pop(i)
            rstd_of(rstds[i % 2][:], p[:], [p], [rstds[i % 2]])
            release(p)

        def s2(i):
            t0 = tiles[i]
            rstd = rstds[i % 2]
            for k in range(NK):
                tm = tmps[k % 4]
                tt(tm[:], xT[:, k, t0:t0 + 512], rstd[:], ALU.mult, [xT, rstd], [tm])
                if B is not None:
                    if k % 4 == 3:
                        ts(out(k, t0), tm[:], A[0](k), B[0](k), ALU.mult, ALU.add, [tm] + A[1] + B[1], [out_b[k]], eng='gpsimd')
                    else:
                        act(out(k, t0), tm[:], AF.Identity, [tm] + A[1] + B[1], [out_b[k]], bias=B[0](k), scale=A[0](k))
                else:
                    act(out(k, t0), tm[:], AF.Identity, [tm] + A[1], [out_b[k]], scale=A[0](k))

        s1a(0)
        s1b(0)
        for step in range(len(tiles)):
            if step + 1 < len(tiles):
                s1a(step + 1)
            s2(step)
            if step + 1 < len(tiles):
                s1b(step + 1)

    def ffn(T, l, j, grp, fi):
        S.mark("ffn l%d j%d g%d" % (l, j, grp))
        if mod_done[0] == 0:
            mod_prologue()
        with ExitStack() as ph:
            wgu = [sb([128, NK, 2, FG * 128], BF16, ph, "wgu%d" % i) for i in range(2)]
            wdn = [sb([128, FG, D], BF16, ph, "wdn%d" % i) for i in range(2)]
            acts = [sb([128, FG, 512], BF16, ph, "act%d" % i) for i in range(2)]
            sgs = [sb([128, 512], BF16, ph, "sg%d" % i) for i in range(2)]
            ngrp = NF // FG
            ntile = T // 512

            def load(fg):
                f0 = fg * FG * 128
                dma('gpsimd', wgu[fg % 2][:, :, 0, :], wg_d[l, fi, :, f0:f0 + FG * 128].rearrange("(k p) n -> p k n", p=128), w=[wgu[fg % 2]])
                dma('gpsimd', wgu[fg % 2][:, :, 1, :], wu_d[l, fi, :, f0:f0 + FG * 128].rearrange("(k p) n -> p k n", p=128), w=[wgu[fg % 2]])
                dma('gpsimd', wdn[fg % 2][:], wd_d[l, fi, f0:f0 + FG * 128, :].rearrange("(c p) n -> p c n", p=128), w=[wdn[fg % 2]])

            load(0)
            normmod(T, (lambda k: coefA[:, l, j, grp, k:k + 1], [coefA]), (lambda k: shiftC[:, l, j, grp, k:k + 1], [shiftC]),
                    lambda k, t0: hT[:, k, t0:t0 + 512], hTk, ph)
            sgi = 0
            for fg in range(ngrp):
                if fg + 1 < ngrp:
                    load(fg + 1)
                mod_consume()
                if mod_done[0] + len(mod_inflight) < (18 if (l == 0 and j == 0) else 36):
                    mod_issue(2)
                W = wgu[fg % 2]
                Wd = wdn[fg % 2]

                def gateup(ti):
                    nonlocal sgi
                    A_ = acts[ti % 2]
                    t0 = ti * 512
                    for fc in range(FG):
                        pg = ps()
                        pu = ps()
                        for k in range(NK):
                            mm(pg[:], W[:, k, 0, fc * 128:(fc + 1) * 128], hT[:, k, t0:t0 + 512], k == 0, k == NK - 1, [W, hTk[k]], [pg], inc=(k == NK - 1))
                        for k in range(NK):
                            mm(pu[:], W[:, k, 1, fc * 128:(fc + 1) * 128], hT[:, k, t0:t0 + 512], k == 0, k == NK - 1, [W, hTk[k]], [pu], inc=(k == NK - 1))
                        sg = sgs[sgi % 2]
                        sgi += 1
                        act(sg[:], pg[:], AF.Silu, [pg], [sg])
                        tt(A_[:, fc, :], pu[:], sg[:], ALU.mult, [pu, sg], [A_])

                def down(ti):
                    A_ = acts[ti % 2]
                    t0 = ti * 512
                    for dc in range(NK):
                        pd = ps()
                        for fc in range(FG):
                            mm(pd[:], Wd[:, fc, dc * 128:(dc + 1) * 128], A_[:, fc, :], fc == 0, fc == FG - 1, [Wd, A_], [pd], inc=(fc == FG - 1))
                        stt(xT[:, dc, t0:t0 + 512], pd[:], gateC[:, l, j, grp, dc:dc + 1], xT[:, dc, t0:t0 + 512], ALU.mult, ALU.add,
                            [pd, gateC, xT], [xT])

                for step in range(ntile + 1):
                    if step < ntile:
                        gateup(step)
                    if step >= 1:
                        down(step - 1)
            mod_consume()
            if j == 0 and 'B' in MIX:
                prefetch(('BX', l), [(512, 1024)], l)
                prefetch(('BY', l), [(1024, 1536)], l)
        S.barrier()

    def final_out(T, y_d):
        S.mark("final")
        with ExitStack() as ph:
            zT = [sb([128, NK, 512], F32, ph, "zT%d" % i) for i in range(2)]
            ost = [sb([128, D], F32, ph, "ost%d" % i) for i in range(2)]
            sqs = [[sb([128, 512], BF16, ph, "sq%d_%d" % (i, k)) for k in range(NK)] for i in range(2)]
            rstds = [sb([128, 512], F32, ph, "rstd%d" % i) for i in range(2)]
            tmps = [sb([128, 512], F32, ph, "nm_tmp%d" % i) for i in range(4)]
            tiles = list(range(0, T, 512))
            lnexp_tables()
            oi = [0]

            pss = {}

            def s1a(i):
                t0 = tiles[i]
                sq = sqs[i % 2]
                for k in range(NK):
                    e_ = ('gpsimd', 'scalar', 'vector', 'scalar', 'gpsimd', 'vector', 'scalar', 'vector')[k]
                    if e_ == 'scalar':
                        act(sq[k][:], xT[:, k, t0:t0 + 512], AF.Square, [xT], [sq[k]])
                    else:
                        tt(sq[k][:], xT[:, k, t0:t0 + 512], xT[:, k, t0:t0 + 512], ALU.mult, [xT], [sq[k]], eng=e_)
                p = ps(hold=True)
                pss[i] = p
                for k in range(NK):
                    mm(p[:], onesD[:], sq[k][:], k == 0, k == NK - 1, [onesD, sq[k]], [p], inc=(k == NK - 1))

            def s1b(i):
                p = pss.pop(i)
                rstd_of(rstds[i % 2][:], p[:], [p], [rstds[i % 2]])
                release(p)

            def s2(i):
                t0 = tiles[i]
                rstd = rstds[i % 2]
                z = zT[i % 2]
                for k in range(NK):
                    tm = tmps[k % 4]
                    tt(tm[:], xT[:, k, t0:t0 + 512], rstd[:], ALU.mult, [xT, rstd], [tm])
                    act(z[:, k, :], tm[:], AF.Identity, [tm, vecTB], [z], scale=vecTB[:, 106 + k:107 + k])
                for bi in range(4):
                    o_ = ost[oi[0] % 2]
                    oi[0] += 1
                    for half in range(2):
                        p = ps()
                        for q in range(4):
                            k = half * 4 + q
                            tr(p[:, q * 128:(q + 1) * 128], z[:, k, bi * 128:(bi + 1) * 128], [z], [p])
                        cp(o_[:, half * 512:(half + 1) * 512], p[:], [p], [o_], eng='vector')
                    dma('sync', y_d[t0 + bi * 128:t0 + (bi + 1) * 128, :], o_[:], r=[o_])

            s1a(0)
            s1b(0)
            for step in range(len(tiles)):
                if step + 1 < len(tiles):
                    s1a(step + 1)
                s2(step)
                if step + 1 < len(tiles):
                    s1b(step + 1)
        S.barrier()

    def proj_fm(ph_r, W, wcols, M, t0, n, out_ps, rows0=0):
        for k in range(NK):
            mm(out_ps[rows0:rows0 + M, 0:n], W[:, k, wcols[0]:wcols[1]], hT[:, k, t0:t0 + n], k == 0, k == NK - 1, [W, hTk[k]], [out_ps],
               inc=(k == NK - 1))

    def load_w(cols_list, l, src=None):
        wsl = next_wslot()
        o = 0
        for (c0, c1) in cols_list:
            dma('gpsimd', wsl[:, :, o:o + (c1 - c0)], win_d[l, :, c0:c1].rearrange("(k p) n -> p k n", p=128), w=[wsl])
            o += c1 - c0
        return wsl

    stash = {}

    def prefetch(key, cols, l):
        stash[key] = load_w(cols, l)

    def get_w(key, cols, l):
        if key in stash:
            return stash.pop(key)
        return load_w(cols, l)

    def load_wout(l, c0, tile=None):
        wsl = tile if tile is not None else next_wslot()
        wv = wsl[:].rearrange("p k n -> p (k n)")[:, 0:2048].rearrange("p (c n) -> p c n", c=2)
        dma('gpsimd', wv, wout_d[l, c0 * 128:(c0 + 2) * 128, :].rearrange("(c p) n -> p c n", p=128), w=[wsl])
        return wsl

    def outproj(l, grp, tb, L, mixT, c0, wsl=None):
        if wsl is None:
            wsl = load_wout(l, c0)
        wv = wsl[:].rearrange("p k n -> p (k n)")[:, 0:2048].rearrange("p (c n) -> p c n", c=2)
        TL = min(512, L)
        for t0 in range(0, L, TL):
            for dc in range(NK):
                p = ps()
                for c in range(2):
                    mm(p[:, 0:TL], wv[:, c, dc * 128:(dc + 1) * 128], mixT[:, c, t0:t0 + TL], c == 0, c == 1, [wsl, mixT], [p], inc=(c == 1))
                stt(xT[:, dc, tb + t0:tb + t0 + TL], p[:, 0:TL], gateC[:, l, 1, grp, dc:dc + 1], xT[:, dc, tb + t0:tb + t0 + TL],
                    ALU.mult, ALU.add, [p, gateC, xT], [xT])

    def softmax_norm(pacc, n, g, mixT, c, t0, sinkcol, ph_tiles):
        rc = ph_tiles
        nr = slice(g * 64, (g + 1) * 64)
        dr = slice((1 - g) * 64, (2 - g) * 64)
        if sinkcol is not None:
            act(rc[dr, 0:n], pacc[dr, 0:n], AF.Ln, [pacc, esink], [rc], bias=esink[dr, sinkcol:sinkcol + 1], scale=1.0)
        else:
            act(rc[dr, 0:n], pacc[dr, 0:n], AF.Ln, [pacc], [rc])
        act(rc[dr, 0:n], rc[dr, 0:n], AF.Exp, [rc], [rc], scale=-1.0)
        tt(mixT[nr, c, t0:t0 + n], pacc[nr, 0:n], rc[dr, 0:n], ALU.mult, [pacc, rc], [mixT])

    def build_kdup(WA, W2):
        kview = WA[:, :, 256:384].rearrange("p k (c d) -> p k c d", c=2)
        w2k = W2[:, :, 0:256].rearrange("p k (c u d) -> p k c u d", c=2, u=2)
        for u in range(2):
            cp(w2k[:, :, :, u, :], kview, [WA], [W2], eng=('vector' if u else 'gpsimd'))

    def mixer_A(l, grp, tb, L, latent, seq, ph, W=None):
        lnexp_tables()
        if latent:
            wm = sb([128, 384], BF16, ph, "winmask")
            dma('gpsimd', wm[:], winmask_d, w=[wm])
        nb = L // 128
        TL = min(512, L)
        mixT = sb([128, 2, L], BF16, ph, "mixA")
        qT = sb([128, 2, L], BF16, ph, "qT")
        kT = sb([128, 2, L + (256 if latent else 0)], BF16, ph, "kTdup")
        vaug = sb([128, nb + (2 if latent else 0), 2, 192], BF16, ph, "vaugA")
        rc = sb([128, TL], F32, ph, "rcA")
        memset(vaug, vaug[:, :, :, 64:128], 1.0)
        if W is not None:
            WA, W2, wo = W
        else:
            WA = get_w(('A', l), [(0, 512)], l)
            W2 = next_wslot()
            build_kdup(WA, W2)
            wo = load_wout(l, 0)
        if latent:
            W3 = sb([128, NK, 256], BF16, ph, "W3A")
            qv = WA[:, :, 0:256].rearrange("p k (h two d) -> p k h two d", two=2, d=32)
            w2q = W2[:, :, 256:512].rearrange("p k (h two d) -> p k h two d", two=2, d=32)
            for half in range(2):
                cp(w2q[:, :, :, half, :], qv[:, :, :, 1 - half, :], [WA], [W2], eng=('vector' if half else 'gpsimd'))
            kv2 = WA[:, :, 256:384].rearrange("p k (c two d) -> p k c two d", two=2, d=32)
            w3v = W3[:].rearrange("p k (c u two d) -> p k c u two d", c=2, u=2, two=2)
            for u in range(2):
                for half in range(2):
                    cp(w3v[:, :, :, u, half, :], kv2[:, :, :, 1 - half, :], [WA], [W3], eng=('vector' if half else 'gpsimd'))
            rc_t = sb([128, TL], F32, ph, "ropec")
            rs_t = sb([128, TL], F32, ph, "ropes")
            t1 = sb([128, TL], F32, ph, "ropet1")
            t2 = rc
        for t0 in range(0, L, TL):
            if latent:
                dma('sync', rc_t[:, 0:TL], ropeA_c_d[:, t0:t0 + TL], w=[rc_t])
                dma('sync', rs_t[:, 0:TL], ropeA_s_d[:, t0:t0 + TL], w=[rs_t])
            for ci in range(4):
                p = ps()
                if ci < 2:
                    proj_fm(ph, WA, (ci * 128, ci * 128 + 128), 128, tb + t0, TL, p)
                else:
                    proj_fm(ph, W2, ((ci - 2) * 128, (ci - 2) * 128 + 128), 128, tb + t0, TL, p)
                dst = (qT[:, ci, t0:t0 + TL] if ci < 2 else kT[:, ci - 2, t0:t0 + TL])
                dstT = qT if ci < 2 else kT
                if latent:
                    p2 = ps()
                    if ci < 2:
                        proj_fm(ph, W2, (256 + ci * 128, 256 + ci * 128 + 128), 128, tb + t0, TL, p2)
                    else:
                        proj_fm(ph, W3, ((ci - 2) * 128, (ci - 2) * 128 + 128), 128, tb + t0, TL, p2)
                    tt(t1[:, 0:TL], p[:, 0:TL], rc_t[:, 0:TL], ALU.mult, [p, rc_t], [t1])
                    tt(t2[:, 0:TL], p2[:, 0:TL], rs_t[:, 0:TL], ALU.mult, [p2, rs_t], [t2])
                    tt(dst, t1[:, 0:TL], t2[:, 0:TL], ALU.add, [t1, t2], [dstT], eng='gpsimd')
                else:
                    cp(dst, p[:, 0:TL], [p], [dstT], eng='scalar')
        if not latent:
            kvst = sb([128, nb, 256], F32, ph, "kvst")
        for bi in range(nb):
            p = ps()
            for k in range(NK):
                mm(p[:, 0:256], hT[:, k, tb + bi * 128:tb + (bi + 1) * 128], WA[:, k, 256:512], k == 0, k == NK - 1, [hTk[k], WA], [p], inc=(k == NK - 1))
            for c in range(2):
                cp(vaug[:, bi, c, 0:64], p[:, 128 + c * 64:128 + (c + 1) * 64], [p], [vaug], eng='vector')
                cp(vaug[:, bi, c, 128:192], p[:, 128 + c * 64:128 + (c + 1) * 64], [p], [vaug], eng='vector')
            if not latent:
                cp(kvst[:, bi, :], p[:, 0:256], [p], [kvst], eng='vector')
        if not latent:
            for c in range(2):
                dma('sync', nk_d[seq, l, c].rearrange("(b p) d -> p b d", p=128), kvst[:, :, c * 64:(c + 1) * 64], r=[kvst])
                dma('sync', nv_d[seq, l, c].rearrange("(b p) d -> p b d", p=128), kvst[:, :, 128 + c * 64:128 + (c + 1) * 64], r=[kvst])
        nctx = 0
        if latent:
            nctx = 2
            kc = sb([128, 2, 2, 2, 64], F32, ph, "kctok")
            vc = sb([128, 2, 2, 64], F32, ph, "vctok")
            for bi in range(2):
                for dup in range(2):
                    dma('sync', kc[:, bi, :, dup, :], ck_d[l, :, bi * 128:(bi + 1) * 128, :].rearrange("c p d -> p c d"), w=[kc])
                dma('sync', vc[:, bi, :, :], cv_d[l, :, bi * 128:(bi + 1) * 128, :].rearrange("c p d -> p c d"), w=[vc])
            for bi in range(2):
                for c in range(2):
                    p = ps()
                    tr(p[:, 0:128], kc[:, bi, c, :, :].rearrange("p u d -> p (u d)"), [kc], [p])
                    cp(kT[:, c, L + bi * 128:L + (bi + 1) * 128], p[:, 0:128], [p], [kT], eng='scalar')
                    cp(vaug[:, nb + bi, c, 0:64], vc[:, bi, c, :], [vc], [vaug])
                    cp(vaug[:, nb + bi, c, 128:192], vc[:, bi, c, :], [vc], [vaug])
        scale = 0.125
        PT = [sb([128, TL], BF16, ph, "PT%d" % i) for i in range(8 if latent else 2)]
        pti = 0
        for c in range(2):
            for g in range(2):
                h = 2 * c + g
                rows = slice(g * 64, (g + 1) * 64)
                if latent:
                    for q0 in range(0, nb, 4):
                        pacc = ps(hold=True)
                        kbs = [j for j in range(q0 - 1, q0 + 5) if 0 <= j < nb]
                        info = {}
                        for j in kbs:
                            qa = max(j - 1, q0)
                            qb = min(j + 1, q0 + 3)
                            n = (qb - qa + 1) * 128
                            moff = (qa - (j - 1)) * 128
                            p = ps()
                            mm(p[:, 0:n], kT[rows, c, j * 128:(j + 1) * 128], qT[rows, c, qa * 128:qa * 128 + n], True, False, [kT, qT], [p], inc=False)
                            mm(p[:, 0:n], ident_b[:], wm[:, moff:moff + n], False, True, [ident_b, wm], [p])
                            P_ = PT[j - (q0 - 1)]
                            act(P_[:, 0:n], p[:, 0:n], AF.Exp, [p], [P_], scale=scale)
                            info[j] = (P_, qa)
                        for bi in range(2):
                            p = ps()
                            mm(p[:, 0:512], kT[rows, c, L + bi * 128:L + (bi + 1) * 128], qT[rows, c, q0 * 128:q0 * 128 + 512], True, True, [kT, qT], [p])
                            P_ = PT[6 + bi]
                            act(P_[:, 0:512], p[:, 0:512], AF.Exp, [p], [P_], scale=scale)
                        for qi in range(q0, q0 + 4):
                            o = (qi - q0) * 128
                            srcs = []
                            for j in (qi - 1, qi, qi + 1):
                                if j in info:
                                    P_, qa = info[j]
                                    srcs.append((vaug[:, j, c, g * 64:g * 64 + 128], P_[:, (qi - qa) * 128:(qi - qa + 1) * 128], P_))
                            for bi in range(2):
                                srcs.append((vaug[:, nb + bi, c, g * 64:g * 64 + 128], PT[6 + bi][:, o:o + 128], PT[6 + bi]))
                            for si_, (lh, rh, Pt_) in enumerate(srcs):
                                mm(pacc[:, o:o + 128], lh, rh, si_ == 0, si_ == len(srcs) - 1, [vaug, Pt_], [pacc], inc=(si_ == len(srcs) - 1))
                        softmax_norm(pacc, 512, g, mixT, c, q0 * 128, l * 4 + h, rc)
                        release(pacc)
                else:
                    pacc = ps(hold=True)
                    for j in range(nb):
                        p = ps()
                        mm(p[:, 0:L], kT[rows, c, j * 128:(j + 1) * 128], qT[rows, c, 0:L], True, True, [kT, qT], [p])
                        P_ = PT[pti % 2]
                        pti += 1
                        act(P_[:, 0:L], p[:, 0:L], AF.Exp, [p], [P_], scale=scale)
                        mm(pacc[:, 0:L], vaug[:, j, c, g * 64:g * 64 + 128], P_[:, 0:L], j == 0, j == nb - 1, [vaug, P_], [pacc], inc=(j == nb - 1))
                    softmax_norm(pacc, L, g, mixT, c, 0, l * 4 + h, rc)
                    release(pacc)
        outproj(l, grp, tb, L, mixT, 0, wo)
        if W is None and 'C' in MIX:
            prefetch(('C', l), [(1792, 2208)], l)

    def load_pw(l, stack):
        pw = sb([128, 2, 64], BF16, stack, "poolw")
        for b2 in range(2):
            dma('gpsimd', pw[b2 * 64:b2 * 64 + 64, :, :], poolw_d[l].rearrange("(a b) i o -> b i a o", b=2)[b2], w=[pw])
        return pw

    def mixer_D(l, grp, tb, L, latent, seq, ph, W=None):
        TL = min(512, L)
        mixT = sb([128, 2, L], BF16, ph, "mixD")
        if W is not None:
            wd_, pw, wo = W
        else:
            wd_ = get_w(('D', l), [(2208, 2464)], l)
            pw = load_pw(l, ph)
            wo = load_wout(l, 6)
        dpad = sb([128, 2, L + 32], F32, ph, "dpad")
        f2 = sb([128, 2, L + 32], F32, ph, "f2")
        f4 = sb([128, 2, L + 32], F32, ph, "f4")
        inv = sb([128, L], F32, ph, "invc")
        memset(dpad, dpad[:], 0.0)
        memset(f2, f2[:], 0.0)
        memset(f4, f4[:], 0.0)
        for t0 in range(0, L, TL):
            for c in range(2):
                p = ps()
                proj_fm(ph, wd_, (c * 128, c * 128 + 128), 128, tb + t0, TL, p)
                cp(dpad[:, c, 16 + t0:16 + t0 + TL], p[:, 0:TL], [p], [dpad], eng='scalar')
        n = L + 16
        diff = sb([128, 2, L], BF16, ph, "pdiff")
        for c in range(2):
            dma('sync', inv[:], (invS_d if latent else invP_d)[:, c, :], w=[inv])
            tt(f2[:, c, 0:n], dpad[:, c, 0:n], dpad[:, c, 1:n + 1], ALU.add, [dpad], [f2])
            tt(f4[:, c, 0:n], f2[:, c, 0:n], f2[:, c, 2:n + 2], ALU.add, [f2], [f4])
            if c == 0:
                srcs = [(f2, 2, slice(0, 64)), (f4, 4, slice(64, 128))]
            else:
                tt(f2[:, c, 0:n], f4[:, c, 0:n], f4[:, c, 4:n + 4], ALU.add, [f4], [f2])
                tt(f4[:, c, 0:n - 8], f2[:, c, 0:n - 8], f2[:, c, 8:n], ALU.add, [f2], [f4])
                srcs = [(f2, 8, slice(0, 64)), (f4, 16, slice(64, 128))]
            for (ft, w_, rs) in srcs:
                o = 16 - w_ // 2
                tt(inv[rs, 0:L], ft[rs, c, o:o + L], inv[rs, 0:L], ALU.mult, [ft, inv], [inv])
                tt(diff[rs, c, 0:L], inv[rs, 0:L], dpad[rs, c, 16:16 + L], ALU.subtract, [inv, dpad], [diff])
        for t0 in range(0, L, TL):
            for c in range(2):
                p = ps()
                for g in range(2):
                    rs = slice(g * 64, g * 64 + 64)
                    mm(p[rs, 0:TL], pw[rs, c, :], diff[rs, c, t0:t0 + TL], True, True, [pw, diff], [p], inc=(g == 1), tp=(g * 64, g * 64))
                act(mixT[:, c, t0:t0 + TL], p[:, 0:TL], AF.Identity, [p, vecTB], [mixT], scale=vecTB[:, 94 + l * 2 + c:95 + l * 2 + c])
        outproj(l, grp, tb, L, mixT, 6, wo)

    def load_wu(l, stack):
        wuq = sb([128, 2, 384], BF16, stack, "wuq")
        dma('gpsimd', wuq[:], wuq_d[l].rearrange("(k p) n -> p k n", p=128), w=[wuq])
        wukv = sb([128, 512], BF16, stack, "wukv")
        dma('gpsimd', wukv[:], wukv_d[l], w=[wukv])
        return wuq, wukv

    def mixer_C(l, grp, tb, L, latent, seq, ph, W=None):
        lnexp_tables()
        TL = min(512, L)
        nb = L // 128
        nkb = nb + (2 if latent else 0)
        LK = nkb * 128
        mixT = sb([128, 2, L], BF16, ph, "mixC")
        cqn = sb([128, 2, L], BF16, ph, "cqn")
        ckvT = sb([128, LK], BF16, ph, "ckvT")
        KhT = sb([128, LK], BF16, ph, "KhT")
        krs = sb([32, TL], BF16, ph, "krs")
        memset(KhT, KhT[64:128, :], 0.0)
        rstd = sb([128, TL], F32, ph, "rstdC")
        tmp = sb([128, TL], F32, ph, "tmpC")
        sq = sb([128, 2, TL], BF16, ph, "sqC")
        if W is not None:
            wc, wuq, wukv, wo = W
        else:
            wc = get_w(('C', l), [(1792, 2208)], l)
            wuq, wukv = load_wu(l, ph)
            wo = load_wout(l, 4)
        if latent:
            wkrs = sb([128, NK, 32], BF16, ph, "wkrs")
            cp(wkrs[:, :, 0:16], wc[:, :, 400:416], [wc], [wkrs], eng='gpsimd')
            cp(wkrs[:, :, 16:32], wc[:, :, 384:400], [wc], [wkrs], eng='vector')
            wuqs = sb([128, 2, 4, 96], BF16, ph, "wuqs")
            wuqv = wuq[:].rearrange("p k (h e) -> p k h e", e=96)
            cp(wuqs[:, :, :, 0:64], wuqv[:, :, :, 0:64], [wuq], [wuqs], eng='gpsimd')
            cp(wuqs[:, :, :, 64:80], wuqv[:, :, :, 80:96], [wuq], [wuqs], eng='vector')
            cp(wuqs[:, :, :, 80:96], wuqv[:, :, :, 64:80], [wuq], [wuqs], eng='vector')
            rcCs = [sb([128, TL], F32, ph, "ropeCc%d" % i) for i in range(2)]
            rsCs = [sb([128, TL], F32, ph, "ropeCs%d" % i) for i in range(2)]
            ropei = [0]

            def rope_tiles(t0_):
                i_ = ropei[0] % 2
                ropei[0] += 1
                dma('sync', rcCs[i_][:], ropeC_c_d[:, t0_:t0_ + TL], w=[rcCs[i_]])
                dma('sync', rsCs[i_][:], ropeC_s_d[:, t0_:t0_ + TL], w=[rsCs[i_]])
                return rcCs[i_], rsCs[i_]
            t1 = rstd
            t2 = tmp
        else:
            ckvF = sb([128, L], F32, ph, "ckvF")
            krF = sb([32, L], F32, ph, "krF")
        for t0 in range(0, L, TL):
            pq = [ps(), ps()]
            for c in range(2):
                proj_fm(ph, wc, (c * 128, c * 128 + 128), 128, tb + t0, TL, pq[c])
                act(sq[:, c, 0:TL], pq[c][:, 0:TL], AF.Square, [pq[c]], [sq])
            p = ps()
            for c in range(2):
                mm(p[:, 0:TL], ones256[:], sq[:, c, 0:TL], c == 0, c == 1, [ones256, sq], [p], inc=(c == 1))
            rstd_of(rstd[:, 0:TL], p[:, 0:TL], [p], [rstd])
            for c in range(2):
                tt(tmp[:, 0:TL], pq[c][:, 0:TL], rstd[:, 0:TL], ALU.mult, [pq[c], rstd], [tmp])
                act(cqn[:, c, t0:t0 + TL], tmp[:, 0:TL], AF.Identity, [tmp, vecTB], [cqn], scale=vecTB[:, 88 + l * 2 + c:89 + l * 2 + c])
            pk = ps()
            proj_fm(ph, wc, (256, 384), 128, tb + t0, TL, pk)
            act(sq[:, 0, 0:TL], pk[:, 0:TL], AF.Square, [pk], [sq])
            p = ps()
            mm(p[:, 0:TL], ones128[:], sq[:, 0, 0:TL], True, True, [ones128, sq], [p])
            rstd_of(rstd[:, 0:TL], p[:, 0:TL], [p], [rstd])
            tt(tmp[:, 0:TL], pk[:, 0:TL], rstd[:, 0:TL], ALU.mult, [pk, rstd], [tmp])
            if latent:
                act(ckvT[:, t0:t0 + TL], tmp[:, 0:TL], AF.Identity, [tmp, vecTB], [ckvT], scale=vecTB[:, 92 + l:93 + l])
            else:
                act(ckvF[:, t0:t0 + TL], tmp[:, 0:TL], AF.Identity, [tmp, vecTB], [ckvF], scale=vecTB[:, 92 + l:93 + l])
                cp(ckvT[:, t0:t0 + TL], ckvF[:, t0:t0 + TL], [ckvF], [ckvT])
            pr = ps()
            proj_fm(ph, wc, (384, 416), 32, tb + t0, TL, pr)
            if latent:
                pr2 = ps()
                proj_fm(ph, wkrs, (0, 32), 32, tb + t0, TL, pr2)
                rcC, rsC = rope_tiles(t0)
                tt(t1[0:32, 0:TL], pr[0:32, 0:TL], rcC[0:32, 0:TL], ALU.mult, [pr, rcC], [t1])
                tt(t2[0:32, 0:TL], pr2[0:32, 0:TL], rsC[0:32, 0:TL], ALU.mult, [pr2, rsC], [t2])
                tt(krs[:, 0:TL], t1[0:32, 0:TL], t2[0:32, 0:TL], ALU.add, [t1, t2], [krs])
                cp(KhT[64:96, t0:t0 + TL], krs[:, 0:TL], [krs], [KhT])
            else:
                cp(krF[:, t0:t0 + TL], pr[0:32, 0:TL], [pr], [krF], eng='scalar')
                cp(KhT[64:96, t0:t0 + TL], krF[:, t0:t0 + TL], [krF], [KhT])
        if latent:
            cst = sb([128, 2, 160], F32, ph, "cstg")
            dma('sync', cst[:, :, 0:128], cckv_d[l].rearrange("(b p) d -> p b d", p=128), w=[cst])
            dma('sync', cst[:, :, 128:160], ckr_d[l].rearrange("(b p) d -> p b d", p=128), w=[cst])
            for bi in range(2):
                p = ps()
                tr(p[:, 0:128], cst[:, bi, 0:128], [cst], [p])
                cp(ckvT[:, L + bi * 128:L + (bi + 1) * 128], p[:, 0:128], [p], [ckvT], eng='scalar')
                p = ps()
                tr(p[0:32, 0:128], cst[:, bi, 128:160], [cst], [p])
                cp(krs[:, 0:128], p[0:32, 0:128], [p], [krs], eng='scalar')
                cp(KhT[64:96, L + bi * 128:L + (bi + 1) * 128], krs[:, 0:128], [krs], [KhT])
        else:
            ost = sb([128, nb, 160], F32, ph, "ostC")
            for bi in range(nb):
                p = ps()
                tr(p[:, 0:128], ckvF[:, bi * 128:(bi + 1) * 128], [ckvF], [p])
                cp(ost[:, bi, 0:128], p[:, 0:128], [p], [ost])
                p = ps()
                tr(p[:, 0:32], krF[0:32, bi * 128:(bi + 1) * 128], [krF], [p], n=32)
                cp(ost[:, bi, 128:160], p[:, 0:32], [p], [ost])
            dma('sync', nckv_d[seq, l].rearrange("(b p) d -> p b d", p=128), ost[:, :, 0:128], r=[ost])
            dma('sync', nkr_d[seq, l].rearrange("(b p) d -> p b d", p=128), ost[:, :, 128:160], r=[ost])
        vaug = sb([128, nkb, 2, 192], BF16, ph, "vaugC")
        memset(vaug, vaug[:, :, :, 64:128], 1.0)
        wv4 = wukv[:].rearrange("p (h e) -> p h e", e=128)
        for j in range(nkb):
            p = ps()
            for h in range(4):
                mm(p[:, h * 64:(h + 1) * 64], ckvT[:, j * 128:(j + 1) * 128], wv4[:, h, 64:128], True, True, [ckvT, wukv], [p], inc=(h == 3))
            for pr_ in range(2):
                cp(vaug[:, j, pr_, 0:64], p[:, pr_ * 128:pr_ * 128 + 64], [p], [vaug], eng='vector')
                cp(vaug[:, j, pr_, 128:192], p[:, pr_ * 128 + 64:pr_ * 128 + 128], [p], [vaug], eng='vector')
        qhTs = [sb([128, L], BF16, ph, "qhT%d" % i) for i in range(2)]
        for q_ in qhTs:
            memset(q_, q_[64:128, :], 0.0)
        KhT2 = sb([128, LK], BF16, ph, "KhT2")
        memset(KhT2, KhT2[64:128, :], 0.0)
        cp(KhT2[64:96, :], KhT[64:96, :], [KhT], [KhT2])
        KhTs = [KhT, KhT2]
        rc = tmp
        PT = [sb([128, TL], BF16, ph, "PTC%d" % i) for i in range(3 if latent else 2)]
        pti = 0
        scale = 96 ** -0.5
        tiles_q = list(range(0, L, TL))
        kchunks = list(range(0, LK, 512))

        def prep_K(h, sl, k0):
            n = min(512, LK - k0)
            p = ps()
            mm(p[0:64, 0:n], wv4[:, h, 0:64], ckvT[:, k0:k0 + n], True, True, [wukv, ckvT], [p])
            cp(KhTs[sl][0:64, k0:k0 + n], p[0:64, 0:n], [p], [KhTs[sl]], eng='scalar')

        def prep_q(h, sl, t0):
            qhT = qhTs[sl]
            p = ps()
            for kc_ in range(2):
                mm(p[0:96, 0:TL], wuq[:, kc_, h * 96:(h + 1) * 96], cqn[:, kc_, t0:t0 + TL], kc_ == 0, kc_ == 1, [wuq, cqn], [p], inc=(kc_ == 1))
            if latent:
                p2 = ps()
                for kc_ in range(2):
                    mm(p2[0:96, 0:TL], wuqs[:, kc_, h, :], cqn[:, kc_, t0:t0 + TL], kc_ == 0, kc_ == 1, [wuqs, cqn], [p2], inc=(kc_ == 1))
                cp(qhT[0:64, t0:t0 + TL], p[0:64, 0:TL], [p], [qhT], eng='scalar')
                rcC, rsC = rope_tiles(t0)
                tt(t1[64:96, 0:TL], p[64:96, 0:TL], rcC[64:96, 0:TL], ALU.mult, [p, rcC], [t1])
                tt(t2[64:96, 0:TL], p2[64:96, 0:TL], rsC[64:96, 0:TL], ALU.mult, [p2, rsC], [t2])
                tt(qhT[64:96, t0:t0 + TL], t1[64:96, 0:TL], t2[64:96, 0:TL], ALU.add, [t1, t2], [qhT])
            else:
                cp(qhT[0:96, t0:t0 + TL], p[0:96, 0:TL], [p], [qhT], eng='scalar')

        def pieces(h):
            sl = h % 2
            nt_ = len(tiles_q)
            out = []
            for i in range(nt_):
                ks = [k0 for j_, k0 in enumerate(kchunks) if (j_ * nt_) // len(kchunks) == i]
                out.append((lambda ks=ks, i=i: ([prep_K(h, sl, k0) for k0 in ks], prep_q(h, sl, tiles_q[i]))))
            return out

        for f_ in pieces(0):
            f_()
        for h in range(4):
            c, g = h // 2, h % 2
            KhT_h, qhT = KhTs[h % 2], qhTs[h % 2]
            nxt = pieces(h + 1) if h + 1 < 4 else []
            for ti, t0 in enumerate(tiles_q):
                if ti < len(nxt):
                    nxt[ti]()
                pacc = ps(hold=True)
                pend = None
                for j in range(nkb):
                    p = ps()
                    mm(p[:, 0:TL], KhT_h[:, j * 128:(j + 1) * 128], qhT[:, t0:t0 + TL], True, True, [KhT_h, qhT], [p])
                    P_ = PT[pti % len(PT)]
                    pti += 1
                    act(P_[:, 0:TL], p[:, 0:TL], AF.Exp, [p], [P_], scale=scale)
                    if pend is not None:
                        mm(pacc[:, 0:TL], vaug[:, pend[0], c, g * 64:g * 64 + 128], pend[1][:, 0:TL], pend[0] == 0, False, [vaug, pend[1]], [pacc], inc=False)
                    pend = (j, P_)
                mm(pacc[:, 0:TL], vaug[:, pend[0], c, g * 64:g * 64 + 128], pend[1][:, 0:TL], pend[0] == 0, True, [vaug, pend[1]], [pacc], inc=True)
                softmax_norm(pacc, TL, g, mixT, c, t0, None, rc)
                release(pacc)
        outproj(l, grp, tb, L, mixT, 4, wo)
        if W is None and 'D' in MIX:
            prefetch(('D', l), [(2208, 2464)], l)

    def mixer_B(l, grp, tb, L, latent, nseq, ph):
        TL = 512
        nt = L // TL
        nb = L // 128
        cps = (L // nseq) // HC
        cpt = TL // HC
        cpb = 128 // HC
        mixT = sb([128, 2, L], BF16, ph, "mixB")
        vtok = sb([128, nb, 256], BF16, ph, "vtok")
        WX = get_w(('BX', l), [(512, 1024)], l)
        WY = get_w(('BY', l), [(1024, 1536)], l)
        wo = load_wout(l, 2)
        WG = sb([128, NK, 256], BF16, ph, "wgB")
        dma('gpsimd', WG[:], win_d[l, :, 1536:1792].rearrange("(k p) n -> p k n", p=128), w=[WG])
        for bi in range(nb):
            p = ps()
            for k in range(NK):
                mm(p[:, 0:256], hT[:, k, tb + bi * 128:tb + (bi + 1) * 128], WX[:, k, 256:512], k == 0, k == NK - 1, [hTk[k], WX], [p], inc=(k == NK - 1))
            cp(vtok[:, bi, :], p[:, 0:256], [p], [vtok], eng='scalar')
        hm = sb([128, 2, 128], BF16, ph, "hmask")
        dma('gpsimd', hm[:], hmask_d.rearrange("r s t -> s r t"), w=[hm])
        scm = sb([128, 512], F32, ph, "scanm")
        dma('sync', scm[:], scanmask_d, w=[scm])
        cm = sb([128, 4], F32, ph, "cmask")
        dma('sync', cm[:], cmask_d, w=[cm])
        oacc = sb([128, L], F32, ph, "oacc")
        Szero = sb([128, 64], F32, ph, "Szero")
        memset(Szero, Szero[:], 0.0)

        class X_:
            pass
        st = []
        for dr in range(2):
            X = X_()
            for nm in ("f_", "lf", "bb", "kk", "e1", "e2"):
                setattr(X, nm, sb([128, 512], F32, ph, "h%s%d" % (nm, dr)))
            X.qq = sb([128, 512], BF16, ph, "qq%d" % dr)
            X.kd = sb([128, 512], BF16, ph, "kd%d" % dr)
            X.kutok = sb([128, 4, 128], BF16, ph, "kutok%d" % dr)
            X.Sprev = sb([128, cpt, 64], BF16, ph, "Sprev%d" % dr)
            X.gam = sb([128, cpt], F32, ph, "gam%d" % dr)
            X.vexp = sb([128, 2, cpb, 64], BF16, ph, "vexp%d" % dr)
            X.At = [sb([128, 128], BF16, ph, "At%d_%d" % (dr, i)) for i in range(2)]
            X.Sall = [sb([128, cpt + 1, 64], F32, ph, "Sall%d" % dr)]
            st.append(X)

        for pr_ in range(2):
            touched = set()

            def stream(dr):
                X = st[dr]
                f_, lf, bb, kk, e1, e2 = X.f_, X.lf, X.bb, X.kk, X.e1, X.e2
                lb_ = lbv[:, l, dr, pr_:pr_ + 1]
                om_ = oml[:, l, dr, pr_:pr_ + 1]
                if latent:
                    dma('sync', X.Sall[0][:, 0, :], st_d[l, dr, 2 * pr_:2 * pr_ + 2].rearrange("h d v -> (h d) v"), w=[X.Sall[0]])
                tiles = list(range(nt)) if dr == 0 else list(range(nt - 1, -1, -1))
                for tidx, ti in enumerate(tiles):
                    SA = X.Sall[0]
                    if tidx > 0:
                        cp(SA[:, 0, :], SA[:, cpt, :], [SA], [SA])
                    jj = 0
                    t0 = ti * TL
                    pf = ps(hold=True)
                    proj_fm(ph, WY, (dr * 256 + pr_ * 128, dr * 256 + pr_ * 128 + 128), 128, tb + t0, TL, pf)
                    pq = ps(hold=True)
                    proj_fm(ph, WX, (pr_ * 128, pr_ * 128 + 128), 128, tb + t0, TL, pq)
                    yield
                    act(f_[:], pf[:], AF.Exp, [pf], [f_], scale=-1.0)
                    release(pf)
                    act(lf[:], f_[:], AF.Ln, [f_, oneT], [lf], bias=oneT[:, 0:1], scale=1.0)
                    act(f_[:], lf[:], AF.Exp, [lf], [f_], scale=-1.0)
                    yield
                    ts(f_[:], f_[:], om_, lb_, ALU.mult, ALU.add, [f_, oml, lbv], [f_])
                    act(lf[:], f_[:], AF.Ln, [f_], [lf])
                    ts(kk[:], f_[:], -1.0, 1.0, ALU.mult, ALU.add, [f_], [kk], eng='gpsimd')
                    yield
                    op('vector', lambda e: e.tensor_tensor_scan(out=bb[:], data0=scm[:], data1=lf[:], initial=0.0,
                                                                op0=ALU.mult, op1=ALU.add), r=[scm, lf], w=[bb])
                    b3 = bb[:].rearrange("p (n c) -> p n c", c=HC)
                    tot = b3[:, :, HC - 1:HC]
                    act(X.gam[:], b3[:, :, HC - 1], AF.Exp, [bb], [X.gam])
                    if dr == 1:
                        tt(e1[:].rearrange("p (n c) -> p n c", c=HC), tot.broadcast_to([128, cpt, HC]), b3, ALU.subtract, [bb], [e1])
                        tt(e2[:], bb[:], lf[:], ALU.subtract, [bb, lf], [e2])
                        yield
                        tt(bb[:], e1[:], lf[:], ALU.add, [e1, lf], [bb])
                    else:
                        tt(e2[:].rearrange("p (n c) -> p n c", c=HC), tot.broadcast_to([128, cpt, HC]), b3, ALU.subtract, [bb], [e2])
                    yield
                    act(e1[:], bb[:], AF.Exp, [bb], [e1])
                    tt(X.qq[:], pq[:], e1[:], ALU.mult, [pq, e1], [X.qq])
                    release(pq)
                    yield
                    act(f_[:], bb[:], AF.Exp, [bb], [f_], scale=-1.0)
                    tt(X.kd[:], kk[:], f_[:], ALU.mult, [kk, f_], [X.kd])
                    yield
                    act(e2[:], e2[:], AF.Exp, [e2], [e2])
                    tt(e2[:], kk[:], e2[:], ALU.mult, [kk, e2], [e2])
                    yield
                    for q in range(4):
                        p = ps()
                        tr(p[:, 0:128], e2[:, q * 128:(q + 1) * 128], [e2], [p])
                        cp(X.kutok[:, q, :], p[:, 0:128], [p], [X.kutok], eng='scalar')
                    yield
                    bqs = list(range(4)) if dr == 0 else list(range(3, -1, -1))
                    for bq in bqs:
                        bi = ti * 4 + bq
                        tt(X.vexp[:], vtok[:, bi, pr_ * 128:(pr_ + 1) * 128].rearrange("p (h v) -> p h v", h=2).unsqueeze(2).broadcast_to([128, 2, cpb, 64]),
                           cm[:, 0:cpb].unsqueeze(1).unsqueeze(3).broadcast_to([128, 2, cpb, 64]), ALU.mult, [vtok, cm], [X.vexp], eng='gpsimd')
                        pU = ps(hold=True)
                        for hh in range(2):
                            mm(pU[hh * 64:(hh + 1) * 64, 0:cpb * 64], X.kutok[:, bq, hh * 64:(hh + 1) * 64],
                               X.vexp[:, hh, :, :].rearrange("p n v -> p (n v)"), True, True, [X.kutok, X.vexp], [pU], inc=(hh == 1), tp=(0, hh * 64))
                        yield
                        chs = list(range(cpb)) if dr == 0 else list(range(cpb - 1, -1, -1))
                        for cj in chs:
                            nl = bq * cpb + cj
                            ng = ti * cpt + nl
                            first = (ng % cps == 0) if dr == 0 else (ng % cps == cps - 1)
                            last = (ng % cps == cps - 1) if dr == 0 else (ng % cps == 0)
                            if first and not latent:
                                cp(SA[:, jj, :], Szero[:], [Szero], [SA])
                            stt(SA[:, jj + 1, :], SA[:, jj, :], X.gam[:, nl:nl + 1], pU[:, cj * 64:(cj + 1) * 64], ALU.mult, ALU.add, [SA, X.gam, pU], [SA])
                            jj += 1
                            if last and not latent:
                                dma('sync', nst_d[ng // cps, l, dr, 2 * pr_:2 * pr_ + 2].rearrange("h d v -> (h d) v"), SA[:, jj, :], r=[SA])
                        yield
                        release(pU)
                    cp(X.Sprev[:], SA[:, 0:cpt, :], [SA], [X.Sprev], eng='scalar')
                    po = ps(hold=True)
                    for q in range(4):
                        bi = ti * 4 + q
                        cs = slice(q * 128, (q + 1) * 128)
                        for hh in range(2):
                            rs = slice(hh * 64, (hh + 1) * 64)
                            pA = ps()
                            mm(pA[:, 0:128], X.kd[rs, cs], X.qq[rs, cs], True, True, [X.kd, X.qq], [pA], tp=(hh * 64, 0))
                            A_ = X.At[hh]
                            tt(A_[:], pA[:, 0:128], hm[:, dr, :], ALU.mult, [pA, hm], [A_])
                            mm(po[rs, cs], vtok[:, bi, pr_ * 128 + hh * 64:pr_ * 128 + (hh + 1) * 64], A_[:], True, False,
                               [vtok, A_], [po], inc=False, tp=(0, hh * 64))
                            for cj in range(cpb):
                                nl = q * cpb + cj
                                lastm = (q == 3 and hh == 1 and cj == cpb - 1)
                                mm(po[rs, q * 128 + cj * HC:q * 128 + (cj + 1) * HC], X.Sprev[rs, (nl if dr == 0 else cpt - 1 - nl), :], X.qq[rs, q * 128 + cj * HC:q * 128 + (cj + 1) * HC],
                                   False, cj == cpb - 1, [X.Sprev, X.qq], [po], inc=(lastm or cj == cpb - 1), tp=(hh * 64, hh * 64))
                        yield
                    if ti not in touched:
                        touched.add(ti)
                        cp(oacc[:, t0:t0 + TL], po[:], [po], [oacc], eng='scalar')
                    else:
                        tt(oacc[:, t0:t0 + TL], oacc[:, t0:t0 + TL], po[:], ALU.add, [oacc, po], [oacc])
                    release(po)
                    yield

            gens = [stream(0), stream(1)]
            alive = [True, True]
            while any(alive):
                for gi in range(2):
                    if alive[gi]:
                        try:
                            next(gens[gi])
                        except StopIteration:
                            alive[gi] = False
            X = st[0]
            Y = st[1]
            for t0 in range(0, L, TL):
                Z = X if (t0 // TL) % 2 == 0 else Y
                act(Z.e1[:], oacc[:, t0:t0 + TL], AF.Square, [oacc], [Z.e1])
                cp(Z.kd[:], Z.e1[:], [Z.e1], [Z.kd])
                p = ps()
                mm(p[:], bd64[:], Z.kd[:], True, True, [bd64, Z.kd], [p])
                rstd_of(Z.e2[:], p[:], [p], [Z.e2])
                tt(Z.e1[:], oacc[:, t0:t0 + TL], Z.e2[:], ALU.mult, [oacc, Z.e2], [Z.e1])
                pg = ps()
                proj_fm(ph, WG, (pr_ * 128, pr_ * 128 + 128), 128, tb + t0, TL, pg)
                act(Z.f_[:], pg[:], AF.Exp, [pg], [Z.f_], scale=-1.0)
                act(Z.lf[:], Z.f_[:], AF.Ln, [Z.f_, oneT], [Z.lf], bias=oneT[:, 0:1], scale=1.0)
                act(Z.lf[:], Z.lf[:], AF.Exp, [Z.lf], [Z.lf], scale=-1.0)
                tt(Z.f_[:], pg[:], Z.lf[:], ALU.mult, [pg, Z.lf], [Z.f_])
                stt(mixT[:, pr_, t0:t0 + TL], Z.e1[:], hnT[:, l:l + 1], Z.f_[:], ALU.mult, ALU.mult, [Z.e1, hnT, Z.f_], [mixT])
        outproj(l, grp, tb, L, mixT, 2, wo)
        if latent and 'A' in MIX:
            prefetch(('A', l), [(0, 512)], l)

    def mixer_layer(T, l, grp, latent, nseq):
        L = T // nseq
        mod_consume()
        if l == 0:
            if mod_done[0] < 18:
                for _ in range(18 - mod_done[0]):
                    mod_step(1)
        else:
            for _ in range(36 - mod_done[0]):
                mod_step(1)
        S.mark("mixnorm l%d g%d" % (l, grp))
        with ExitStack() as ph:
            normmod(T, (lambda k: coefA[:, l, 1, grp, k:k + 1], [coefA]), (lambda k: shiftC[:, l, 1, grp, k:k + 1], [shiftC]),
                    lambda k, t0: hT[:, k, t0:t0 + 512], hTk, ph)
        S.barrier()
        if 'B' in MIX:
            S.mark("mixB l%d g%d s0" % (l, grp))
            with ExitStack() as ph:
                mixer_B(l, grp, 0, T, latent, nseq, ph)
            S.barrier()
        with ExitStack() as wph:
            Ws = {'A': None, 'C': None, 'D': None}
            if nseq > 1:
                def wtile(c0, c1, nm):
                    t = sb([128, NK, c1 - c0], BF16, wph, nm)
                    dma('gpsimd', t[:], win_d[l, :, c0:c1].rearrange("(k p) n -> p k n", p=128), w=[t])
                    return t

                def wo_tile(c0, nm):
                    return load_wout(l, c0)
                _wa = wtile(0, 512, "WAp")
                _w2 = sb([128, NK, 256], BF16, wph, "W2p")
                build_kdup(_wa, _w2)
                Ws['A'] = (_wa, _w2, wo_tile(0, "WoA"))
                Ws['C'] = (wtile(1792, 2208, "WCp"),) + load_wu(l, wph) + (wo_tile(4, "WoC"),)
                Ws['D'] = (wtile(2208, 2464, "WDp"), load_pw(l, wph), wo_tile(6, "WoD"))
            for name, fn in (('A', mixer_A), ('C', mixer_C), ('D', mixer_D)):
                if name not in MIX:
                    continue
                S.mark("mix%s l%d g%d s0" % (name, l, grp))
                with ExitStack() as ph:
                    for seq in range(nseq):
                        fn(l, grp, seq * L, L, latent, seq, ph, Ws[name])
                S.barrier()

    def run_pass(x_d, y_d, T, grp, latent, nseq):
        load_xT(x_d, T)
        for l in range(2):
            ffn(T, l, 0, grp, 0)
            mixer_layer(T, l, grp, latent, nseq)
            ffn(T, l, 2, grp, 1)
        final_out(T, y_d)

    if flags.get('sample', True):
        run_pass(xs_d, ys_d, 2048, 1, True, 1)
    if flags.get('prompt', True):
        run_pass(xp_d, yp_d, 1024, 0, False, 4)
    S.mark("end")
    S.finish()
    es.close()
    return nc, S


def _rope_tables(dim, nrows_tab, row_slices):
    n_freq = dim // 4
    half = dim // 2
    inv = (10000.0 ** (-np.arange(n_freq, dtype=np.float32) / n_freq)).astype(np.float32)
    t = np.arange(2048)
    row_id = (t // 64).astype(np.float32)
    col_id = (t % 64).astype(np.float32)
    ang = np.concatenate([row_id[:, None] * inv, col_id[:, None] * inv], axis=-1).astype(np.float32)
    cos = np.cos(ang).astype(np.float32)
    sin = np.sin(ang).astype(np.float32)
    C = np.zeros((128, 2048), np.float32)
    Sg = np.zeros((128, 2048), np.float32)
    for (r0, n) in row_slices:
        for r in range(n):
            d = r % dim
            j = d % half
            C[r0 + r] = cos[:, j]
            Sg[r0 + r] = sin[:, j] * (-1.0 if d < half else 1.0)
    return C, Sg


_CONST = {}


def _consts():
    if _CONST:
        return _CONST
    c = {}
    c['ident'] = np.eye(128, dtype=np.float32)
    c['ropeA_c'], c['ropeA_s'] = _rope_tables(64, 128, [(0, 128)])
    c['ropeC_c'], c['ropeC_s'] = _rope_tables(32, 128, [(0, 32), (64, 32)])
    b = np.arange(128)[:, None]
    a = np.arange(128)[None, :]
    m = np.zeros((128, 384), np.float32)
    m[:, 0:128] = np.where(b <= a, 0.0, -240000.0)
    m[:, 256:384] = np.where(a <= b, 0.0, -240000.0)
    c['winmask'] = m
    s = np.arange(128)[:, None]
    t = np.arange(128)[None, :]
    same = (s // HC) == (t // HC)
    c['hmask'] = np.stack([(same & (s <= t)), (same & (s >= t))]).astype(np.float32)
    sc = np.ones((128, 512), np.float32)
    sc[:, ::HC] = 0.0
    c['scanmask'] = sc
    c['cmask'] = (np.arange(128)[:, None] // HC == np.arange(4)[None, :]).astype(np.float32)
    pm = np.zeros((4, 128, 128), np.float32)
    for m in range(128):
        pm[0, m + 32 if (m % 64) < 32 else m - 32, m] = 1.0
        if m < 32 or 64 <= m < 96:
            pm[1, m + 16 if (m % 32) < 16 else m - 16, m] = 1.0
        else:
            pm[1, m, m] = 1.0
        pm[2, m % 64, m] = 1.0
        pm[3, 64 + m % 64, m] = 1.0
    c['permm'] = pm
    for nm, L in (('invcntS', 2048), ('invcntP', 256)):
        inv = np.zeros((128, 2, L), np.float32)
        pos = np.arange(L)
        for g, w in enumerate(POOL_WINDOWS):
            lo = np.clip(pos - w // 2, 0, L)
            hi = np.clip(pos - w // 2 + w, 0, L)
            inv[(g % 2) * 64:(g % 2) * 64 + 64, g // 2, :] = (1.0 / (hi - lo).astype(np.float32))[None, :]
        c[nm] = inv
    _CONST.update(c)
    return _CONST


_NC = {}


def kernel(x_prompt, x_sample, c, c_ctx, cache_attn_k, cache_attn_v, state_hgrn, cache_mla_ckv,
           cache_mla_krope, w_ada, b_ada, norm_sub, w_ffn_gate, w_ffn_up, w_ffn_down, w_in, w_out,
           attn_sink, hgrn_lb_logits, hgrn_out_norm, mla_q_norm, mla_kv_norm, mla_w_uq, mla_w_ukv,
           pool_w, pool_scale, final_norm, _flags=None):
    f32 = lambda a: np.ascontiguousarray(np.asarray(a, dtype=np.float32))
    key = repr(_flags)
    if key not in _NC:
        _NC[key] = build(_flags)[0]
    nc = _NC[key]
    cs = _consts()
    x_prompt, x_sample, c, c_ctx = f32(x_prompt), f32(x_sample), f32(c), f32(c_ctx)
    b_ada, norm_sub = f32(b_ada), f32(norm_sub)
    shared = {
        "hnorm": f32(hgrn_out_norm), "sink": f32(attn_sink), "w_ada": f32(w_ada), "w_gate": f32(w_ffn_gate), "w_up": f32(w_ffn_up),
        "w_down": f32(w_ffn_down), "w_in": f32(w_in), "w_out": f32(w_out), "w_uq": f32(mla_w_uq), "w_ukv": f32(mla_w_ukv),
        "pool_w": f32(pool_w),
    }
    shared.update(cs)
    vecA = np.concatenate([b_ada[0].reshape(72, 128), norm_sub.reshape(48, 128)], axis=0)
    in_maps = []
    for b in range(8):
        vecB = np.concatenate([b_ada[1].reshape(72, 128), c_ctx.reshape(8, 128), c[b].reshape(8, 128),
                               f32(mla_q_norm).reshape(4, 128), f32(mla_kv_norm).reshape(2, 128), f32(pool_scale).reshape(4, 128),
                               f32(hgrn_lb_logits).reshape(8, 128), f32(final_norm).reshape(8, 128)], axis=0)
        m = dict(shared)
        m.update({
            "xs": x_sample[b], "xp": x_prompt[4 * b:4 * b + 4].reshape(1024, 1024), "vecA": vecA, "vecB": np.ascontiguousarray(vecB),
            "cache_k": f32(cache_attn_k[b]), "cache_v": f32(cache_attn_v[b]), "state": f32(state_hgrn[b]),
            "cache_ckv": f32(cache_mla_ckv[b]), "cache_kr": f32(cache_mla_krope[b]),
        })
        in_maps.append(m)
    res = run_bass_kernel_spmd(nc, in_maps, core_ids=list(range(8)))
    R = res.results
    y_p = np.concatenate([r["y_p"].reshape(4, 256, 1024) for r in R], axis=0)
    y_s = np.stack([r["y_s"] for r in R], axis=0)
    nk = np.concatenate([r["new_k"] for r in R], axis=0)
    nv = np.concatenate([r["new_v"] for r in R], axis=0)
    nst = np.concatenate([r["new_st"] for r in R], axis=0)
    nckv = np.concatenate([r["new_ckv"] for r in R], axis=0)
    nkr = np.concatenate([r["new_kr"] for r in R], axis=0)
    return (y_p.astype(np.float32), y_s.astype(np.float32), nk.astype(np.float32), nv.astype(np.float32),
            nst.astype(np.float32), nckv.astype(np.float32), nkr.astype(np.float32))
```

```python
from contextlib import ExitStack
import numpy as np
import concourse.bass as bass
import concourse.mybir as mybir
from concourse.bass_utils import run_bass_kernel_spmd

F32 = mybir.dt.float32
BF16 = mybir.dt.bfloat16
AF = mybir.ActivationFunctionType
ALU = mybir.AluOpType
ENGS = ['tensor', 'vector', 'scalar', 'gpsimd', 'sync']

D = 1024
NK = 8
DFF = 2816
NF = 22
FG = 2
DIN = 2464
HC = 32
EPS = 1e-6
POOL_WINDOWS = (2, 4, 8, 16)


class Buf:
    def __init__(self, name):
        self.name = name
        self.lw = None
        self.rd = {}


class Sched:
    def __init__(self, nc, es):
        self.engh = {'tensor': nc.tensor, 'vector': nc.vector, 'scalar': nc.scalar, 'gpsimd': nc.gpsimd, 'sync': nc.sync}
        self.nc = nc
        self.es = es
        self.sems = {}
        self.val = {}
        self.seen = {e: {} for e in ENGS}
        self.pend_r = {e: [] for e in ENGS}
        self.pend_w = {e: [] for e in ENGS}
        for e in ENGS:
            self._mksem(e)
        self.ninst = 0
        self.nops = {e: 0 for e in ENGS}
        self.marks = []

    def mark(self, name):
        self.marks.append((name, dict(self.nops)))

    def _mksem(self, key):
        h = self.es.enter_context(self.nc.semaphore("s%d" % len(self.sems)))
        self.sems[key] = h
        self.val[key] = 0
        return h

    def _wait(self, eng, deps):
        best = {}
        for d in deps:
            if d is None:
                continue
            k, v = d
            if v > best.get(k, 0):
                best[k] = v
        for k, v in best.items():
            if self.seen[eng].get(k, 0) < v:
                self.seen[eng][k] = v
                self.engh[eng].wait_ge(self.sems[k], v)
                self.ninst += 1

    def _deps(self, r, w):
        deps = []
        for b in r:
            deps.append(b.lw)
        for b in w:
            deps.append(b.lw)
            deps.extend(b.rd.items())
        return deps

    def op(self, eng, fn, r=(), w=(), inc=True):
        self._wait(eng, self._deps(r, w))
        self.pend_r[eng].extend(r)
        self.pend_w[eng].extend(w)
        self.ninst += 1
        self.nops[eng] += 1
        if inc:
            self.val[eng] += 1
            v = self.val[eng]
            fn(self.engh[eng]).then_inc(self.sems[eng], 1)
            for b in self.pend_w[eng]:
                b.lw = (eng, v)
                b.rd = {}
            for b in self.pend_r[eng]:
                if b.lw == (eng, v):
                    continue
                b.rd[eng] = v
            self.pend_r[eng] = []
            self.pend_w[eng] = []
        else:
            fn(self.engh[eng])

    def dma(self, eng, fn, r=(), w=()):
        self._wait(eng, self._deps(r, w))
        owner = (list(w) + list(r))[0]
        key = ('dma', eng, owner.name)
        if key not in self.sems:
            self._mksem(key)
        self.val[key] += 16
        v = self.val[key]
        fn(self.engh[eng]).then_inc(self.sems[key], 16)
        self.ninst += 1
        for b in w:
            b.lw = (key, v)
            b.rd = {}
        for b in r:
            b.rd[key] = v

    def barrier(self):
        deps = [(k, v) for k, v in self.val.items() if v > 0]
        for e in ENGS:
            self._wait(e, deps)

    def finish(self):
        self.barrier()


class TT:
    def __init__(self, t, name):
        self.t = t
        self.b = Buf(name)

    def __getitem__(self, k):
        return self.t[k]


def build(flags=None):
    flags = flags or {}
    MIX = flags.get('mix', 'ABCD')
    nc = bass.Bass("TRN2", target_bir_lowering=False)
    es = ExitStack()
    S = Sched(nc, es)

    def din(name, shape):
        return nc.dram_tensor(name, list(shape), F32, kind="ExternalInput").ap()

    def dout(name, shape):
        return nc.dram_tensor(name, list(shape), F32, kind="ExternalOutput").ap()

    xs_d = din("xs", [2048, D])
    xp_d = din("xp", [1024, D])
    vecA_d = din("vecA", [120, 128])
    vecB_d = din("vecB", [114, 128])
    hnorm_d = din("hnorm", [2, 64])
    sink_d = din("sink", [2, 4])
    ck_d = din("cache_k", [2, 2, 256, 64])
    cv_d = din("cache_v", [2, 2, 256, 64])
    st_d = din("state", [2, 2, 4, 64, 64])
    cckv_d = din("cache_ckv", [2, 256, 128])
    ckr_d = din("cache_kr", [2, 256, 32])
    wada_d = din("w_ada", [2, D, 9 * D])
    wg_d = din("w_gate", [2, 2, D, DFF])
    wu_d = din("w_up", [2, 2, D, DFF])
    wd_d = din("w_down", [2, 2, DFF, D])
    win_d = din("w_in", [2, D, DIN])
    wout_d = din("w_out", [2, D, D])
    wuq_d = din("w_uq", [2, 256, 384])
    wukv_d = din("w_ukv", [2, 128, 512])
    poolw_d = din("pool_w", [2, 4, 64, 64])
    ident_d = din("ident", [128, 128])
    ropeA_c_d = din("ropeA_c", [128, 2048])
    ropeA_s_d = din("ropeA_s", [128, 2048])
    ropeC_c_d = din("ropeC_c", [128, 2048])
    ropeC_s_d = din("ropeC_s", [128, 2048])
    winmask_d = din("winmask", [128, 384])
    hmask_d = din("hmask", [2, 128, 128])
    scanmask_d = din("scanmask", [128, 512])
    cmask_d = din("cmask", [128, 4])
    invS_d = din("invcntS", [128, 2, 2048])
    invP_d = din("invcntP", [128, 2, 256])
    perm_d = din("permm", [4, 128, 128])

    ys_d = dout("y_s", [2048, D])
    yp_d = dout("y_p", [1024, D])
    nk_d = dout("new_k", [4, 2, 2, 256, 64])
    nv_d = dout("new_v", [4, 2, 2, 256, 64])
    nst_d = dout("new_st", [4, 2, 2, 4, 64, 64])
    nckv_d = dout("new_ckv", [4, 2, 256, 128])
    nkr_d = dout("new_kr", [4, 2, 256, 32])

    cnt = [0]
    live_names = {}

    def sb(shape, dt, stack=None, name=None):
        cnt[0] += 1
        nm = "%s_%d" % (name or "t", cnt[0])
        t = (stack or es).enter_context(nc.sbuf_tensor(nm, list(shape), dt))
        key = (id(stack or es), name or nm)
        n_ = live_names.get(key, 0)
        live_names[key] = n_ + 1
        return TT(t, (name or nm) + ("#%d" % n_ if n_ else ""))

    def op(eng, fn, r=(), w=(), inc=True):
        S.op(eng, fn, r=[x.b for x in r], w=[x.b for x in w], inc=inc)

    def dma(eng, out, in_, r=(), w=(), slow=False):
        if slow:
            S.dma(eng, lambda e: e.dma_start(out=out, in_=in_, allow_slow_non_contiguous=True),
                  r=[x.b for x in r], w=[x.b for x in w])
        else:
            S.dma(eng, lambda e: e.dma_start(out=out, in_=in_), r=[x.b for x in r], w=[x.b for x in w])

    def mm(out, lhsT, rhs, start, stop, r, w, inc=True, tp=None):
        if tp is None:
            op('tensor', lambda e: e.matmul(out, lhsT=lhsT, rhs=rhs, start=start, stop=stop), r=r, w=w, inc=inc)
        else:
            op('tensor', lambda e: e.matmul(out, lhsT=lhsT, rhs=rhs, start=start, stop=stop, tile_position=tp),
               r=r, w=w, inc=inc)

    def tr(out, in_, r, w, n=128):
        op('tensor', lambda e: e.transpose(out=out, in_=in_, identity=ident_f[0:n, 0:n]), r=list(r) + [ident_f], w=w)

    def act(out, in_, func, r, w, bias=None, scale=None):
        kw = {}
        if bias is not None:
            kw['bias'] = bias
        if scale is not None:
            kw['scale'] = scale
        op('scalar', lambda e: e.activation(out=out, in_=in_, func=func, **kw), r=r, w=w)

    def tt(out, in0, in1, alu, r, w, eng='vector'):
        op(eng, lambda e: e.tensor_tensor(out=out, in0=in0, in1=in1, op=alu), r=r, w=w)

    def ts(out, in0, s1, s2, op0, op1, r, w, eng='vector'):
        if op1 is None:
            op(eng, lambda e: e.tensor_scalar(out=out, in0=in0, scalar1=s1, scalar2=None, op0=op0), r=r, w=w)
        else:
            op(eng, lambda e: e.tensor_scalar(out=out, in0=in0, scalar1=s1, scalar2=s2, op0=op0, op1=op1), r=r, w=w)

    def rstd_of(out, in_, r, w):
        act(out, in_, AF.Ln, list(r) + [epsT], w, bias=epsT[:, 0:1], scale=1.0)
        act(out, out, AF.Exp, w, w, scale=-0.5)

    def lnexp_tables():
        pass

    def stt(out, in0, scalar, in1, op0, op1, r, w, eng='vector'):
        op(eng, lambda e: e.scalar_tensor_tensor(out=out, in0=in0, scalar=scalar, in1=in1, op0=op0, op1=op1), r=r, w=w)

    def cp(out, in_, r, w, eng='vector'):
        if eng == 'scalar':
            act(out, in_, AF.Copy, r, w)
        else:
            op(eng, lambda e: e.tensor_copy(out=out, in_=in_), r=r, w=w)

    def memset(t, ap, val, eng='vector'):
        op(eng, lambda e: e.memset(ap, val), r=(), w=[t])

    PSB = [TT(es.enter_context(nc.psum_tensor("ps%d" % i, [128, 512], F32)), "ps%d" % i) for i in range(8)]
    xT = sb([128, NK, 2048], F32, name="xT")
    hT = sb([128, NK, 2048], BF16, name="hT")
    hTk = []
    for _k in range(NK):
        _v = TT(hT.t, "hT%d" % _k)
        hTk.append(_v)
    ident_f = sb([128, 128], F32, name="identf")
    ident_b = sb([128, 128], BF16, name="identb")
    onesD = sb([128, 128], BF16, name="onesD")
    ones256 = sb([128, 128], BF16, name="ones256")
    ones128 = sb([128, 128], BF16, name="ones128")
    bd64 = sb([128, 128], BF16, name="bd64")
    epsT = sb([128, 1], F32, name="eps")
    oneT = sb([128, 1], F32, name="one")
    vecTA = sb([128, 120], F32, name="vecTA")
    vecTB = sb([128, 114], F32, name="vecTB")
    modT = sb([128, 2, 72, 2], F32, name="modT")
    coefA = sb([128, 2, 3, 2, NK], F32, name="coefA")
    gateC = sb([128, 2, 3, 2, NK], F32, name="gateC")
    shiftC = sb([128, 2, 3, 2, NK], F32, name="shiftC")
    scT = sb([128, NK, 2], BF16, name="scT")
    hnT = sb([128, 2], F32, name="hnT")
    esink = sb([128, 8], F32, name="esink")
    lbv = sb([128, 2, 2, 2], F32, name="lbv")
    oml = sb([128, 2, 2, 2], F32, name="oml")
    wslots = [sb([128, NK, 512], BF16, name="wslot%d" % i) for i in range(3)]
    wsl_i = [0]

    def next_wslot():
        w = wslots[wsl_i[0] % 3]
        wsl_i[0] += 1
        return w

    psi = [0]
    held = set()

    def ps(hold=False):
        while True:
            i = psi[0] % 8
            psi[0] += 1
            if i not in held:
                break
        if hold:
            held.add(i)
        return PSB[i]

    def release(p):
        held.discard(PSB.index(p))

    dma('sync', ident_f[:], ident_d, w=[ident_f])
    dma('gpsimd', ident_b[:], ident_d, w=[ident_b])
    memset(onesD, onesD[:], 1.0 / 1024)
    memset(ones256, ones256[:], 1.0 / 256)
    memset(ones128, ones128[:], 1.0 / 128)
    memset(bd64, bd64[:], 0.0)
    memset(bd64, bd64[0:64, 0:64], 1.0 / 64)
    memset(bd64, bd64[64:128, 64:128], 1.0 / 64)
    memset(epsT, epsT[:], EPS)
    memset(oneT, oneT[:], 1.0)
    with ExitStack() as ph:
        stg = sb([128, 128], F32, ph, "stg")
        for (src, n, dst) in ((vecA_d, 120, vecTA), (vecB_d, 114, vecTB)):
            dma('sync', stg[0:n, :], src, w=[stg])
            p = ps()
            tr(p[:, 0:n], stg[0:n, :], [stg], [p], n=n)
            cp(dst[:, 0:n], p[:, 0:n], [p], [dst])
        dma('sync', hnT[0:64, :], hnorm_d.rearrange("l v -> v l"), w=[hnT], slow=True)
        dma('sync', hnT[64:128, :], hnorm_d.rearrange("l v -> v l"), w=[hnT], slow=True)
        sk = sb([128, 8], F32, ph, "sk")
        dma('sync', sk[:], sink_d.rearrange("l h -> (l h)").partition_broadcast(128), w=[sk], slow=True)
        act(esink[:], sk[:], AF.Exp, [sk], [esink])
        lbl = vecTB[:, 98:106].rearrange("p (l r) -> p l r", l=2)
        ex = sb([128, 2, 4], F32, ph, "ex")
        act(ex[:], lbl, AF.Exp, [vecTB], [ex])
        sm = sb([128, 4], F32, ph, "sm")
        tt(sm[:], ex[:, 0, :], ex[:, 1, :], ALU.add, [ex], [sm])
        op('vector', lambda e: e.reciprocal(out=sm[:], in_=sm[:]), r=[sm], w=[sm])
        lbf = lbv[:].rearrange("p l r q -> p l (r q)")
        memset(lbv, lbf[:, 0, :], 0.0)
        tt(lbf[:, 1, :], ex[:, 1, :], sm[:], ALU.mult, [ex, sm], [lbv])
        ts(oml[:].rearrange("p l r q -> p (l r q)"), lbv[:].rearrange("p l r q -> p (l r q)"), -1.0, 1.0, ALU.mult, ALU.add, [lbv], [oml])
        act(scT[:].rearrange("p k g -> p g k"), vecTB[:, 72:88].rearrange("p (g k) -> p g k", g=2), AF.Silu, [vecTB], [scT])
        pass
    S.barrier()
    mblk = sb([2, 512], F32, name="mblk")
    mod_pending = [(l_, cb_) for l_ in range(2) for cb_ in range(18)]
    mod_done = [0]

    def mod_coefs(l, j):
        for g in range(2):
            ng = vecTA[:, 72 + (l * 3 + j) * 8: 72 + (l * 3 + j) * 8 + 8]
            sc_ = modT[:, l, (3 * j + 1) * 8:(3 * j + 2) * 8, g]
            stt(coefA[:, l, j, g, :], sc_, 1.0, ng, ALU.add, ALU.mult, [modT, vecTA], [coefA])
            cp(shiftC[:, l, j, g, :], modT[:, l, (3 * j) * 8:(3 * j + 1) * 8, g], [modT], [shiftC])
            ts(gateC[:, l, j, g, :], modT[:, l, (3 * j + 2) * 8:(3 * j + 3) * 8, g], 0.5 if j != 1 else 1.0, None,
               ALU.mult, None, [modT], [gateC])

    mod_inflight = []

    def mod_issue(n):
        for _ in range(n):
            if not mod_pending:
                return
            l, cb = mod_pending.pop(0)
            wsl = next_wslot()
            dma('gpsimd', wsl[:], wada_d[l, :, cb * 512:(cb + 1) * 512].rearrange("(k p) n -> p k n", p=128), w=[wsl])
            mod_inflight.append((l, cb, wsl))

    def mod_step(n):
        mod_issue(n)
        mod_consume()

    def mod_consume(keep=0):
        while len(mod_inflight) > keep:
            l, cb, wsl = mod_inflight.pop(0)
            p = ps()
            for k in range(NK):
                mm(p[0:2, :], scT[:, k, :], wsl[:, k, :], k == 0, k == NK - 1, [scT, wsl], [p], inc=(k == NK - 1))
            cp(mblk[:], p[0:2, :], [p], [mblk], eng='scalar')
            p2 = ps()
            for c4 in range(4):
                tr(p2[:, c4 * 2:c4 * 2 + 2], mblk[0:2, c4 * 128:(c4 + 1) * 128], [mblk], [p2], n=2)
            bsrc = (vecTA[:, 0:72] if l == 0 else vecTB[:, 0:72])
            bt = vecTA if l == 0 else vecTB
            tt(modT[:, l, cb * 4:(cb + 1) * 4, :], p2[:, 0:8].rearrange("p (c g) -> p c g", g=2),
               bsrc[:, cb * 4:(cb + 1) * 4].unsqueeze(2).broadcast_to([128, 4, 2]), ALU.add, [p2, bt], [modT])
            mod_done[0] += 1
            if cb % 6 == 5:
                mod_coefs(l, cb // 6)

    mod_issue(2)

    def mod_prologue():
        for _ in range(4):
            mod_consume(keep=1)
            mod_issue(1)
        mod_consume()
    S.barrier()

    def load_xT(x_d, T):
        S.mark("load")
        with ExitStack() as ph:
            stg = [sb([128, D], F32, ph, "xstg%d" % i) for i in range(2)]
            for bi in range(T // 128):
                s_ = stg[bi % 2]
                dma('sync', s_[:], x_d[bi * 128:(bi + 1) * 128, :], w=[s_])
                for half in range(2):
                    p = ps()
                    for q in range(4):
                        k = half * 4 + q
                        tr(p[:, q * 128:(q + 1) * 128], s_[:, k * 128:(k + 1) * 128], [s_], [p])
                    cp(xT[:, half * 4:half * 4 + 4, bi * 128:(bi + 1) * 128], p[:].rearrange("p (q t) -> p q t", q=4), [p], [xT],
                       eng='vector')
        S.barrier()

    def normmod(T, A, B, out, out_b, ph):
        sqs = [[sb([128, 512], BF16, ph, "sq%d_%d" % (i, k)) for k in range(NK)] for i in range(2)]
        rstds = [sb([128, 512], F32, ph, "rstd%d" % i) for i in range(2)]
        tmps = [sb([128, 512], F32, ph, "nm_tmp%d" % i) for i in range(4)]
        tiles = list(range(0, T, 512))
        lnexp_tables()

        pss = {}

        def s1a(i):
            t0 = tiles[i]
            sq = sqs[i % 2]
            for k in range(NK):
                e_ = ('gpsimd', 'scalar', 'vector', 'scalar', 'gpsimd', 'vector', 'scalar', 'vector')[k]
                if e_ == 'scalar':
                    act(sq[k][:], xT[:, k, t0:t0 + 512], AF.Square, [xT], [sq[k]])
                else:
                    tt(sq[k][:], xT[:, k, t0:t0 + 512], xT[:, k, t0:t0 + 512], ALU.mult, [xT], [sq[k]], eng=e_)
            p = ps(hold=True)
            pss[i] = p
            for k in range(NK):
                mm(p[:], onesD[:], sq[k][:], k == 0, k == NK - 1, [onesD, sq[k]], [p], inc=(k == NK - 1))

        def s1b(i):
            p = pss.pop(i)
            rstd_of(rstds[i % 2][:], p[:], [p], [rstds[i % 2]])
            release(p)

        def s2(i):
            t0 = tiles[i]
            rstd = rstds[i % 2]
            for k in range(NK):
                tm = tmps[k % 4]
                tt(tm[:], xT[:, k, t0:t0 + 512], rstd[:], ALU.mult, [xT, rstd], [tm])
                if B is not None:
                    if k % 4 == 3:
                        ts(out(k, t0), tm[:], A[0](k), B[0](k), ALU.mult, ALU.add, [tm] + A[1] + B[1], [out_b[k]], eng='gpsimd')
                    else:
                        act(out(k, t0), tm[:], AF.Identity, [tm] + A[1] + B[1], [out_b[k]], bias=B[0](k), scale=A[0](k))
                else:
                    act(out(k, t0), tm[:], AF.Identity, [tm] + A[1], [out_b[k]], scale=A[0](k))

        s1a(0)
        s1b(0)
        for step in range(len(tiles)):
            if step + 1 < len(tiles):
                s1a(step + 1)
            s2(step)
            if step + 1 < len(tiles):
                s1b(step + 1)

    def ffn(T, l, j, grp, fi):
        S.mark("ffn l%d j%d g%d" % (l, j, grp))
        if mod_done[0] == 0:
            mod_prologue()
        with ExitStack() as ph:
            wgu = [sb([128, NK, 2, FG * 128], BF16, ph, "wgu%d" % i) for i in range(2)]
            wdn = [sb([128, FG, D], BF16, ph, "wdn%d" % i) for i in range(2)]
            acts = [sb([128, FG, 512], BF16, ph, "act%d" % i) for i in range(2)]
            sgs = [sb([128, 512], BF16, ph, "sg%d" % i) for i in range(2)]
            ngrp = NF // FG
            ntile = T // 512

            def load(fg):
                f0 = fg * FG * 128
                dma('gpsimd', wgu[fg % 2][:, :, 0, :], wg_d[l, fi, :, f0:f0 + FG * 128].rearrange("(k p) n -> p k n", p=128), w=[wgu[fg % 2]])
                dma('gpsimd', wgu[fg % 2][:, :, 1, :], wu_d[l, fi, :, f0:f0 + FG * 128].rearrange("(k p) n -> p k n", p=128), w=[wgu[fg % 2]])
                dma('gpsimd', wdn[fg % 2][:], wd_d[l, fi, f0:f0 + FG * 128, :].rearrange("(c p) n -> p c n", p=128), w=[wdn[fg % 2]])

            load(0)
            normmod(T, (lambda k: coefA[:, l, j, grp, k:k + 1], [coefA]), (lambda k: shiftC[:, l, j, grp, k:k + 1], [shiftC]),
                    lambda k, t0: hT[:, k, t0:t0 + 512], hTk, ph)
            sgi = 0
            for fg in range(ngrp):
                if fg + 1 < ngrp:
                    load(fg + 1)
                mod_consume()
                if mod_done[0] + len(mod_inflight) < (18 if (l == 0 and j == 0) else 36):
                    mod_issue(2)
                W = wgu[fg % 2]
                Wd = wdn[fg % 2]

                def gateup(ti):
                    nonlocal sgi
                    A_ = acts[ti % 2]
                    t0 = ti * 512
                    for fc in range(FG):
                        pg = ps()
                        pu = ps()
                        for k in range(NK):
                            mm(pg[:], W[:, k, 0, fc * 128:(fc + 1) * 128], hT[:, k, t0:t0 + 512], k == 0, k == NK - 1, [W, hTk[k]], [pg], inc=(k == NK - 1))
                        for k in range(NK):
                            mm(pu[:], W[:, k, 1, fc * 128:(fc + 1) * 128], hT[:, k, t0:t0 + 512], k == 0, k == NK - 1, [W, hTk[k]], [pu], inc=(k == NK - 1))
                        sg = sgs[sgi % 2]
                        sgi += 1
                        act(sg[:], pg[:], AF.Silu, [pg], [sg])
                        tt(A_[:, fc, :], pu[:], sg[:], ALU.mult, [pu, sg], [A_])

                def down(ti):
                    A_ = acts[ti % 2]
                    t0 = ti * 512
                    for dc in range(NK):
                        pd = ps()
                        for fc in range(FG):
                            mm(pd[:], Wd[:, fc, dc * 128:(dc + 1) * 128], A_[:, fc, :], fc == 0, fc == FG - 1, [Wd, A_], [pd], inc=(fc == FG - 1))
                        stt(xT[:, dc, t0:t0 + 512], pd[:], gateC[:, l, j, grp, dc:dc + 1], xT[:, dc, t0:t0 + 512], ALU.mult, ALU.add,
                            [pd, gateC, xT], [xT])

                for step in range(ntile + 1):
                    if step < ntile:
                        gateup(step)
                    if step >= 1:
                        down(step - 1)
            mod_consume()
            if j == 0 and 'B' in MIX:
                prefetch(('BX', l), [(512, 1024)], l)
                prefetch(('BY', l), [(1024, 1536)], l)
        S.barrier()

    def final_out(T, y_d):
        S.mark("final")
        with ExitStack() as ph:
            zT = [sb([128, NK, 512], F32, ph, "zT%d" % i) for i in range(2)]
            ost = [sb([128, D], F32, ph, "ost%d" % i) for i in range(2)]
            sqs = [[sb([128, 512], BF16, ph, "sq%d_%d" % (i, k)) for k in range(NK)] for i in range(2)]
            rstds = [sb([128, 512], F32, ph, "rstd%d" % i) for i in range(2)]
            tmps = [sb([128, 512], F32, ph, "nm_tmp%d" % i) for i in range(4)]
            tiles = list(range(0, T, 512))
            lnexp_tables()
            oi = [0]

            pss = {}

            def s1a(i):
                t0 = tiles[i]
                sq = sqs[i % 2]
                for k in range(NK):
                    e_ = ('gpsimd', 'scalar', 'vector', 'scalar', 'gpsimd', 'vector', 'scalar', 'vector')[k]
                    if e_ == 'scalar':
                        act(sq[k][:], xT[:, k, t0:t0 + 512], AF.Square, [xT], [sq[k]])
                    else:
                        tt(sq[k][:], xT[:, k, t0:t0 + 512], xT[:, k, t0:t0 + 512], ALU.mult, [xT], [sq[k]], eng=e_)
                p = ps(hold=True)
                pss[i] = p
                for k in range(NK):
                    mm(p[:], onesD[:], sq[k][:], k == 0, k == NK - 1, [onesD, sq[k]], [p], inc=(k == NK - 1))

            def s1b(i):
                p = pss.pop(i)
                rstd_of(rstds[i % 2][:], p[:], [p], [rstds[i % 2]])
                release(p)

            def s2(i):
                t0 = tiles[i]
                rstd = rstds[i % 2]
                z = zT[i % 2]
                for k in range(NK):
                    tm = tmps[k % 4]
                    tt(tm[:], xT[:, k, t0:t0 + 512], rstd[:], ALU.mult, [xT, rstd], [tm])
                    act(z[:, k, :], tm[:], AF.Identity, [tm, vecTB], [z], scale=vecTB[:, 106 + k:107 + k])
                for bi in range(4):
                    o_ = ost[oi[0] % 2]
                    oi[0] += 1
                    for half in range(2):
                        p = ps()
                        for q in range(4):
                            k = half * 4 + q
                            tr(p[:, q * 128:(q + 1) * 128], z[:, k, bi * 128:(bi + 1) * 128], [z], [p])
                        cp(o_[:, half * 512:(half + 1) * 512], p[:], [p], [o_], eng='vector')
                    dma('sync', y_d[t0 + bi * 128:t0 + (bi + 1) * 128, :], o_[:], r=[o_])

            s1a(0)
            s1b(0)
            for step in range(len(tiles)):
                if step + 1 < len(tiles):
                    s1a(step + 1)
                s2(step)
                if step + 1 < len(tiles):
                    s1b(step + 1)
        S.barrier()

    def proj_fm(ph_r, W, wcols, M, t0, n, out_ps, rows0=0):
        for k in range(NK):
            mm(out_ps[rows0:rows0 + M, 0:n], W[:, k, wcols[0]:wcols[1]], hT[:, k, t0:t0 + n], k == 0, k == NK - 1, [W, hTk[k]], [out_ps],
               inc=(k == NK - 1))

    def load_w(cols_list, l, src=None):
        wsl = next_wslot()
        o = 0
        for (c0, c1) in cols_list:
            dma('gpsimd', wsl[:, :, o:o + (c1 - c0)], win_d[l, :, c0:c1].rearrange("(k p) n -> p k n", p=128), w=[wsl])
            o += c1 - c0
        return wsl

    stash = {}

    def prefetch(key, cols, l):
        stash[key] = load_w(cols, l)

    def get_w(key, cols, l):
        if key in stash:
            return stash.pop(key)
        return load_w(cols, l)

    def load_wout(l, c0, tile=None):
        wsl = tile if tile is not None else next_wslot()
        wv = wsl[:].rearrange("p k n -> p (k n)")[:, 0:2048].rearrange("p (c n) -> p c n", c=2)
        dma('gpsimd', wv, wout_d[l, c0 * 128:(c0 + 2) * 128, :].rearrange("(c p) n -> p c n", p=128), w=[wsl])
        return wsl

    def outproj(l, grp, tb, L, mixT, c0, wsl=None):
        if wsl is None:
            wsl = load_wout(l, c0)
        wv = wsl[:].rearrange("p k n -> p (k n)")[:, 0:2048].rearrange("p (c n) -> p c n", c=2)
        TL = min(512, L)
        for t0 in range(0, L, TL):
            for dc in range(NK):
                p = ps()
                for c in range(2):
                    mm(p[:, 0:TL], wv[:, c, dc * 128:(dc + 1) * 128], mixT[:, c, t0:t0 + TL], c == 0, c == 1, [wsl, mixT], [p], inc=(c == 1))
                stt(xT[:, dc, tb + t0:tb + t0 + TL], p[:, 0:TL], gateC[:, l, 1, grp, dc:dc + 1], xT[:, dc, tb + t0:tb + t0 + TL],
                    ALU.mult, ALU.add, [p, gateC, xT], [xT])

    def softmax_norm(pacc, n, g, mixT, c, t0, sinkcol, ph_tiles):
        rc = ph_tiles
        nr = slice(g * 64, (g + 1) * 64)
        dr = slice((1 - g) * 64, (2 - g) * 64)
        if sinkcol is not None:
            act(rc[dr, 0:n], pacc[dr, 0:n], AF.Ln, [pacc, esink], [rc], bias=esink[dr, sinkcol:sinkcol + 1], scale=1.0)
        else:
            act(rc[dr, 0:n], pacc[dr, 0:n], AF.Ln, [pacc], [rc])
        act(rc[dr, 0:n], rc[dr, 0:n], AF.Exp, [rc], [rc], scale=-1.0)
        tt(mixT[nr, c, t0:t0 + n], pacc[nr, 0:n], rc[dr, 0:n], ALU.mult, [pacc, rc], [mixT])

    def build_kdup(WA, W2):
        kview = WA[:, :, 256:384].rearrange("p k (c d) -> p k c d", c=2)
        w2k = W2[:, :, 0:256].rearrange("p k (c u d) -> p k c u d", c=2, u=2)
        for u in range(2):
            cp(w2k[:, :, :, u, :], kview, [WA], [W2], eng=('vector' if u else 'gpsimd'))

    def mixer_A(l, grp, tb, L, latent, seq, ph, W=None):
        lnexp_tables()
        if latent:
            wm = sb([128, 384], BF16, ph, "winmask")
            dma('gpsimd', wm[:], winmask_d, w=[wm])
        nb = L // 128
        TL = min(512, L)
        mixT = sb([128, 2, L], BF16, ph, "mixA")
        qT = sb([128, 2, L], BF16, ph, "qT")
        kT = sb([128, 2, L + (256 if latent else 0)], BF16, ph, "kTdup")
        vaug = sb([128, nb + (2 if latent else 0), 2, 192], BF16, ph, "vaugA")
        rc = sb([128, TL], F32, ph, "rcA")
        memset(vaug, vaug[:, :, :, 64:128], 1.0)
        if W is not None:
            WA, W2, wo = W
        else:
            WA = get_w(('A', l), [(0, 512)], l)
            W2 = next_wslot()
            build_kdup(WA, W2)
            wo = load_wout(l, 0)
        if latent:
            W3 = sb([128, NK, 256], BF16, ph, "W3A")
            qv = WA[:, :, 0:256].rearrange("p k (h two d) -> p k h two d", two=2, d=32)
            w2q = W2[:, :, 256:512].rearrange("p k (h two d) -> p k h two d", two=2, d=32)
            for half in range(2):
                cp(w2q[:, :, :, half, :], qv[:, :, :, 1 - half, :], [WA], [W2], eng=('vector' if half else 'gpsimd'))
            kv2 = WA[:, :, 256:384].rearrange("p k (c two d) -> p k c two d", two=2, d=32)
            w3v = W3[:].rearrange("p k (c u two d) -> p k c u two d", c=2, u=2, two=2)
            for u in range(2):
                for half in range(2):
                    cp(w3v[:, :, :, u, half, :], kv2[:, :, :, 1 - half, :], [WA], [W3], eng=('vector' if half else 'gpsimd'))
            rc_t = sb([128, TL], F32, ph, "ropec")
            rs_t = sb([128, TL], F32, ph, "ropes")
            t1 = sb([128, TL], F32, ph, "ropet1")
            t2 = rc
        for t0 in range(0, L, TL):
            if latent:
                dma('sync', rc_t[:, 0:TL], ropeA_c_d[:, t0:t0 + TL], w=[rc_t])
                dma('sync', rs_t[:, 0:TL], ropeA_s_d[:, t0:t0 + TL], w=[rs_t])
            for ci in range(4):
                p = ps()
                if ci < 2:
                    proj_fm(ph, WA, (ci * 128, ci * 128 + 128), 128, tb + t0, TL, p)
                else:
                    proj_fm(ph, W2, ((ci - 2) * 128, (ci - 2) * 128 + 128), 128, tb + t0, TL, p)
                dst = (qT[:, ci, t0:t0 + TL] if ci < 2 else kT[:, ci - 2, t0:t0 + TL])
                dstT = qT if ci < 2 else kT
                if latent:
                    p2 = ps()
                    if ci < 2:
                        proj_fm(ph, W2, (256 + ci * 128, 256 + ci * 128 + 128), 128, tb + t0, TL, p2)
                    else:
                        proj_fm(ph, W3, ((ci - 2) * 128, (ci - 2) * 128 + 128), 128, tb + t0, TL, p2)
                    tt(t1[:, 0:TL], p[:, 0:TL], rc_t[:, 0:TL], ALU.mult, [p, rc_t], [t1])
                    tt(t2[:, 0:TL], p2[:, 0:TL], rs_t[:, 0:TL], ALU.mult, [p2, rs_t], [t2])
                    tt(dst, t1[:, 0:TL], t2[:, 0:TL], ALU.add, [t1, t2], [dstT], eng='gpsimd')
                else:
                    cp(dst, p[:, 0:TL], [p], [dstT], eng='scalar')
        if not latent:
            kvst = sb([128, nb, 256], F32, ph, "kvst")
        for bi in range(nb):
            p = ps()
            for k in range(NK):
                mm(p[:, 0:256], hT[:, k, tb + bi * 128:tb + (bi + 1) * 128], WA[:, k, 256:512], k == 0, k == NK - 1, [hTk[k], WA], [p], inc=(k == NK - 1))
            for c in range(2):
                cp(vaug[:, bi, c, 0:64], p[:, 128 + c * 64:128 + (c + 1) * 64], [p], [vaug], eng='vector')
                cp(vaug[:, bi, c, 128:192], p[:, 128 + c * 64:128 + (c + 1) * 64], [p], [vaug], eng='vector')
            if not latent:
                cp(kvst[:, bi, :], p[:, 0:256], [p], [kvst], eng='vector')
        if not latent:
            for c in range(2):
                dma('sync', nk_d[seq, l, c].rearrange("(b p) d -> p b d", p=128), kvst[:, :, c * 64:(c + 1) * 64], r=[kvst])
                dma('sync', nv_d[seq, l, c].rearrange("(b p) d -> p b d", p=128), kvst[:, :, 128 + c * 64:128 + (c + 1) * 64], r=[kvst])
        nctx = 0
        if latent:
            nctx = 2
            kc = sb([128, 2, 2, 2, 64], F32, ph, "kctok")
            vc = sb([128, 2, 2, 64], F32, ph, "vctok")
            for bi in range(2):
                for dup in range(2):
                    dma('sync', kc[:, bi, :, dup, :], ck_d[l, :, bi * 128:(bi + 1) * 128, :].rearrange("c p d -> p c d"), w=[kc])
                dma('sync', vc[:, bi, :, :], cv_d[l, :, bi * 128:(bi + 1) * 128, :].rearrange("c p d -> p c d"), w=[vc])
            for bi in range(2):
                for c in range(2):
                    p = ps()
                    tr(p[:, 0:128], kc[:, bi, c, :, :].rearrange("p u d -> p (u d)"), [kc], [p])
                    cp(kT[:, c, L + bi * 128:L + (bi + 1) * 128], p[:, 0:128], [p], [kT], eng='scalar')
                    cp(vaug[:, nb + bi, c, 0:64], vc[:, bi, c, :], [vc], [vaug])
                    cp(vaug[:, nb + bi, c, 128:192], vc[:, bi, c, :], [vc], [vaug])
        scale = 0.125
        PT = [sb([128, TL], BF16, ph, "PT%d" % i) for i in range(8 if latent else 2)]
        pti = 0
        for c in range(2):
            for g in range(2):
                h = 2 * c + g
                rows = slice(g * 64, (g + 1) * 64)
                if latent:
                    for q0 in range(0, nb, 4):
                        pacc = ps(hold=True)
                        kbs = [j for j in range(q0 - 1, q0 + 5) if 0 <= j < nb]
                        info = {}
                        for j in kbs:
                            qa = max(j - 1, q0)
                            qb = min(j + 1, q0 + 3)
                            n = (qb - qa + 1) * 128
                            moff = (qa - (j - 1)) * 128
                            p = ps()
                            mm(p[:, 0:n], kT[rows, c, j * 128:(j + 1) * 128], qT[rows, c, qa * 128:qa * 128 + n], True, False, [kT, qT], [p], inc=False)
                            mm(p[:, 0:n], ident_b[:], wm[:, moff:moff + n], False, True, [ident_b, wm], [p])
                            P_ = PT[j - (q0 - 1)]
                            act(P_[:, 0:n], p[:, 0:n], AF.Exp, [p], [P_], scale=scale)
                            info[j] = (P_, qa)
                        for bi in range(2):
                            p = ps()
                            mm(p[:, 0:512], kT[rows, c, L + bi * 128:L + (bi + 1) * 128], qT[rows, c, q0 * 128:q0 * 128 + 512], True, True, [kT, qT], [p])
                            P_ = PT[6 + bi]
                            act(P_[:, 0:512], p[:, 0:512], AF.Exp, [p], [P_], scale=scale)
                        for qi in range(q0, q0 + 4):
                            o = (qi - q0) * 128
                            srcs = []
                            for j in (qi - 1, qi, qi + 1):
                                if j in info:
                                    P_, qa = info[j]
                                    srcs.append((vaug[:, j, c, g * 64:g * 64 + 128], P_[:, (qi - qa) * 128:(qi - qa + 1) * 128], P_))
                            for bi in range(2):
                                srcs.append((vaug[:, nb + bi, c, g * 64:g * 64 + 128], PT[6 + bi][:, o:o + 128], PT[6 + bi]))
                            for si_, (lh, rh, Pt_) in enumerate(srcs):
                                mm(pacc[:, o:o + 128], lh, rh, si_ == 0, si_ == len(srcs) - 1, [vaug, Pt_], [pacc], inc=(si_ == len(srcs) - 1))
                        softmax_norm(pacc, 512, g, mixT, c, q0 * 128, l * 4 + h, rc)
                        release(pacc)
                else:
                    pacc = ps(hold=True)
                    for j in range(nb):
                        p = ps()
                        mm(p[:, 0:L], kT[rows, c, j * 128:(j + 1) * 128], qT[rows, c, 0:L], True, True, [kT, qT], [p])
                        P_ = PT[pti % 2]
                        pti += 1
                        act(P_[:, 0:L], p[:, 0:L], AF.Exp, [p], [P_], scale=scale)
                        mm(pacc[:, 0:L], vaug[:, j, c, g * 64:g * 64 + 128], P_[:, 0:L], j == 0, j == nb - 1, [vaug, P_], [pacc], inc=(j == nb - 1))
                    softmax_norm(pacc, L, g, mixT, c, 0, l * 4 + h, rc)
                    release(pacc)
        outproj(l, grp, tb, L, mixT, 0, wo)
        if W is None and 'C' in MIX:
            prefetch(('C', l), [(1792, 2208)], l)

    def load_pw(l, stack):
        pw = sb([128, 2, 64], BF16, stack, "poolw")
        for b2 in range(2):
            dma('gpsimd', pw[b2 * 64:b2 * 64 + 64, :, :], poolw_d[l].rearrange("(a b) i o -> b i a o", b=2)[b2], w=[pw])
        return pw

    def mixer_D(l, grp, tb, L, latent, seq, ph, W=None):
        TL = min(512, L)
        mixT = sb([128, 2, L], BF16, ph, "mixD")
        if W is not None:
            wd_, pw, wo = W
        else:
            wd_ = get_w(('D', l), [(2208, 2464)], l)
            pw = load_pw(l, ph)
            wo = load_wout(l, 6)
        dpad = sb([128, 2, L + 32], F32, ph, "dpad")
        f2 = sb([128, 2, L + 32], F32, ph, "f2")
        f4 = sb([128, 2, L + 32], F32, ph, "f4")
        inv = sb([128, L], F32, ph, "invc")
        memset(dpad, dpad[:], 0.0)
        memset(f2, f2[:], 0.0)
        memset(f4, f4[:], 0.0)
        for t0 in range(0, L, TL):
            for c in range(2):
                p = ps()
                proj_fm(ph, wd_, (c * 128, c * 128 + 128), 128, tb + t0, TL, p)
                cp(dpad[:, c, 16 + t0:16 + t0 + TL], p[:, 0:TL], [p], [dpad], eng='scalar')
        n = L + 16
        diff = sb([128, 2, L], BF16, ph, "pdiff")
        for c in range(2):
            dma('sync', inv[:], (invS_d if latent else invP_d)[:, c, :], w=[inv])
            tt(f2[:, c, 0:n], dpad[:, c, 0:n], dpad[:, c, 1:n + 1], ALU.add, [dpad], [f2])
            tt(f4[:, c, 0:n], f2[:, c, 0:n], f2[:, c, 2:n + 2], ALU.add, [f2], [f4])
            if c == 0:
                srcs = [(f2, 2, slice(0, 64)), (f4, 4, slice(64, 128))]
            else:
                tt(f2[:, c, 0:n], f4[:, c, 0:n], f4[:, c, 4:n + 4], ALU.add, [f4], [f2])
                tt(f4[:, c, 0:n - 8], f2[:, c, 0:n - 8], f2[:, c, 8:n], ALU.add, [f2], [f4])
                srcs = [(f2, 8, slice(0, 64)), (f4, 16, slice(64, 128))]
            for (ft, w_, rs) in srcs:
                o = 16 - w_ // 2
                tt(inv[rs, 0:L], ft[rs, c, o:o + L], inv[rs, 0:L], ALU.mult, [ft, inv], [inv])
                tt(diff[rs, c, 0:L], inv[rs, 0:L], dpad[rs, c, 16:16 + L], ALU.subtract, [inv, dpad], [diff])
        for t0 in range(0, L, TL):
            for c in range(2):
                p = ps()
                for g in range(2):
                    rs = slice(g * 64, g * 64 + 64)
                    mm(p[rs, 0:TL], pw[rs, c, :], diff[rs, c, t0:t0 + TL], True, True, [pw, diff], [p], inc=(g == 1), tp=(g * 64, g * 64))
                act(mixT[:, c, t0:t0 + TL], p[:, 0:TL], AF.Identity, [p, vecTB], [mixT], scale=vecTB[:, 94 + l * 2 + c:95 + l * 2 + c])
        outproj(l, grp, tb, L, mixT, 6, wo)

    def load_wu(l, stack):
        wuq = sb([128, 2, 384], BF16, stack, "wuq")
        dma('gpsimd', wuq[:], wuq_d[l].rearrange("(k p) n -> p k n", p=128), w=[wuq])
        wukv = sb([128, 512], BF16, stack, "wukv")
        dma('gpsimd', wukv[:], wukv_d[l], w=[wukv])
        return wuq, wukv

    def mixer_C(l, grp, tb, L, latent, seq, ph, W=None):
        lnexp_tables()
        TL = min(512, L)
        nb = L // 128
        nkb = nb + (2 if latent else 0)
        LK = nkb * 128
        mixT = sb([128, 2, L], BF16, ph, "mixC")
        cqn = sb([128, 2, L], BF16, ph, "cqn")
        ckvT = sb([128, LK], BF16, ph, "ckvT")
        KhT = sb([128, LK], BF16, ph, "KhT")
        krs = sb([32, TL], BF16, ph, "krs") if latent else None
        memset(KhT, KhT[64:128, :], 0.0)
        rstd = sb([128, TL], F32, ph, "rstdC")
        tmp = sb([128, TL], F32, ph, "tmpC")
        sq = sb([128, 3 if latent else 2, TL], BF16, ph, "sqC")
        kidx = 2 if latent else 0
        rstd2 = sb([128, TL], F32, ph, "rstd2C")
        tmp2 = sb([128, TL], F32, ph, "tmp2C") if latent else tmp
        if latent:
            t1k = rstd2
            t2k = tmp2
        if W is not None:
            wc, wuq, wukv, wo = W
        else:
            wc = get_w(('C', l), [(1792, 2208)], l)
            wuq, wukv = load_wu(l, ph)
            wo = load_wout(l, 4)
        if latent:
            wkrs = sb([128, NK, 32], BF16, ph, "wkrs")
            cp(wkrs[:, :, 0:16], wc[:, :, 400:416], [wc], [wkrs], eng='gpsimd')
            cp(wkrs[:, :, 16:32], wc[:, :, 384:400], [wc], [wkrs], eng='vector')
            wuqs = sb([128, 2, 4, 96], BF16, ph, "wuqs")
            wuqv = wuq[:].rearrange("p k (h e) -> p k h e", e=96)
            cp(wuqs[:, :, :, 0:64], wuqv[:, :, :, 0:64], [wuq], [wuqs], eng='gpsimd')
            cp(wuqs[:, :, :, 64:80], wuqv[:, :, :, 80:96], [wuq], [wuqs], eng='vector')
            cp(wuqs[:, :, :, 80:96], wuqv[:, :, :, 64:80], [wuq], [wuqs], eng='vector')
            rcCs = [sb([128, TL], F32, ph, "ropeCc%d" % i) for i in range(2)]
            rsCs = [sb([128, TL], F32, ph, "ropeCs%d" % i) for i in range(2)]
            ropei = [0]

            def rope_tiles(t0_):
                i_ = ropei[0] % 2
                ropei[0] += 1
                dma('sync', rcCs[i_][:], ropeC_c_d[:, t0_:t0_ + TL], w=[rcCs[i_]])
                dma('sync', rsCs[i_][:], ropeC_s_d[:, t0_:t0_ + TL], w=[rsCs[i_]])
                return rcCs[i_], rsCs[i_]
            t1 = rstd
            t2 = tmp
        else:
            ckvF = sb([128, L], F32, ph, "ckvF")
            krF = sb([32, L], F32, ph, "krF")
        for t0 in range(0, L, TL):
            pq = [ps(hold=True), ps(hold=True)]
            for c in range(2):
                proj_fm(ph, wc, (c * 128, c * 128 + 128), 128, tb + t0, TL, pq[c])
            pk = ps(hold=True)
            proj_fm(ph, wc, (256, 384), 128, tb + t0, TL, pk)
            pr = ps(hold=True)
            proj_fm(ph, wc, (384, 416), 32, tb + t0, TL, pr)
            if latent:
                pr2 = ps(hold=True)
                proj_fm(ph, wkrs, (0, 32), 32, tb + t0, TL, pr2)
            for c in range(2):
                act(sq[:, c, 0:TL], pq[c][:, 0:TL], AF.Square, [pq[c]], [sq])
            if latent:
                rcC, rsC = rope_tiles(t0)
                tt(t1k[0:32, 0:TL], pr[0:32, 0:TL], rcC[0:32, 0:TL], ALU.mult, [pr, rcC], [t1k])
                tt(t2k[0:32, 0:TL], pr2[0:32, 0:TL], rsC[0:32, 0:TL], ALU.mult, [pr2, rsC], [t2k])
                release(pr)
                release(pr2)
                tt(krs[:, 0:TL], t1k[0:32, 0:TL], t2k[0:32, 0:TL], ALU.add, [t1k, t2k], [krs])
                cp(KhT[64:96, t0:t0 + TL], krs[:, 0:TL], [krs], [KhT])
            else:
                cp(krF[:, t0:t0 + TL], pr[0:32, 0:TL], [pr], [krF], eng='scalar')
                release(pr)
                cp(KhT[64:96, t0:t0 + TL], krF[:, t0:t0 + TL], [krF], [KhT])
            pa = ps()
            for c in range(2):
                mm(pa[:, 0:TL], ones256[:], sq[:, c, 0:TL], c == 0, c == 1, [ones256, sq], [pa], inc=(c == 1))
            act(sq[:, kidx, 0:TL], pk[:, 0:TL], AF.Square, [pk], [sq])
            pb = ps()
            mm(pb[:, 0:TL], ones128[:], sq[:, kidx, 0:TL], True, True, [ones128, sq], [pb])
            rstd_of(rstd[:, 0:TL], pa[:, 0:TL], [pa], [rstd])
            rstd_of(rstd2[:, 0:TL], pb[:, 0:TL], [pb], [rstd2])
            for c in range(2):
                tm_ = (tmp, tmp2)[c]
                tt(tm_[:, 0:TL], pq[c][:, 0:TL], rstd[:, 0:TL], ALU.mult, [pq[c], rstd], [tm_])
                release(pq[c])
                act(cqn[:, c, t0:t0 + TL], tm_[:, 0:TL], AF.Identity, [tm_, vecTB], [cqn], scale=vecTB[:, 88 + l * 2 + c:89 + l * 2 + c])
            tt(tmp[:, 0:TL], pk[:, 0:TL], rstd2[:, 0:TL], ALU.mult, [pk, rstd2], [tmp])
            release(pk)
            if latent:
                act(ckvT[:, t0:t0 + TL], tmp[:, 0:TL], AF.Identity, [tmp, vecTB], [ckvT], scale=vecTB[:, 92 + l:93 + l])
            else:
                act(ckvF[:, t0:t0 + TL], tmp[:, 0:TL], AF.Identity, [tmp, vecTB], [ckvF], scale=vecTB[:, 92 + l:93 + l])
                cp(ckvT[:, t0:t0 + TL], ckvF[:, t0:t0 + TL], [ckvF], [ckvT])
        if latent:
            cst = sb([128, 2, 160], F32, ph, "cstg")
            dma('sync', cst[:, :, 0:128], cckv_d[l].rearrange("(b p) d -> p b d", p=128), w=[cst])
            dma('sync', cst[:, :, 128:160], ckr_d[l].rearrange("(b p) d -> p b d", p=128), w=[cst])
            for bi in range(2):
                p = ps()
                tr(p[:, 0:128], cst[:, bi, 0:128], [cst], [p])
                cp(ckvT[:, L + bi * 128:L + (bi + 1) * 128], p[:, 0:128], [p], [ckvT], eng='scalar')
                p = ps()
                tr(p[0:32, 0:128], cst[:, bi, 128:160], [cst], [p])
                cp(krs[:, 0:128], p[0:32, 0:128], [p], [krs], eng='scalar')
                cp(KhT[64:96, L + bi * 128:L + (bi + 1) * 128], krs[:, 0:128], [krs], [KhT])
        else:
            ost = sb([128, nb, 160], F32, ph, "ostC")
            for bi in range(nb):
                p = ps()
                tr(p[:, 0:128], ckvF[:, bi * 128:(bi + 1) * 128], [ckvF], [p])
                cp(ost[:, bi, 0:128], p[:, 0:128], [p], [ost])
                p = ps()
                tr(p[:, 0:32], krF[0:32, bi * 128:(bi + 1) * 128], [krF], [p], n=32)
                cp(ost[:, bi, 128:160], p[:, 0:32], [p], [ost])
            dma('sync', nckv_d[seq, l].rearrange("(b p) d -> p b d", p=128), ost[:, :, 0:128], r=[ost])
            dma('sync', nkr_d[seq, l].rearrange("(b p) d -> p b d", p=128), ost[:, :, 128:160], r=[ost])
        vaug = sb([128, nkb, 2, 192], BF16, ph, "vaugC")
        memset(vaug, vaug[:, :, :, 64:128], 1.0)
        wv4 = wukv[:].rearrange("p (h e) -> p h e", e=128)
        for j in range(nkb):
            p = ps()
            for h in range(4):
                mm(p[:, h * 64:(h + 1) * 64], ckvT[:, j * 128:(j + 1) * 128], wv4[:, h, 64:128], True, True, [ckvT, wukv], [p], inc=(h == 3))
            for pr_ in range(2):
                cp(vaug[:, j, pr_, 0:64], p[:, pr_ * 128:pr_ * 128 + 64], [p], [vaug], eng='vector')
                cp(vaug[:, j, pr_, 128:192], p[:, pr_ * 128 + 64:pr_ * 128 + 128], [p], [vaug], eng='vector')
        if latent:
            qhTs = [sb([128, L], BF16, ph, "qhT%d" % i) for i in range(2)]
            for q_ in qhTs:
                memset(q_, q_[64:128, :], 0.0)
            KhT2 = sb([128, LK], BF16, ph, "KhT2")
            memset(KhT2, KhT2[64:128, :], 0.0)
            cp(KhT2[64:96, :], KhT[64:96, :], [KhT], [KhT2])
            KhTs = [KhT, KhT2]
        else:
            q0_ = sb([128, L], BF16, ph, "qhT0")
            memset(q0_, q0_[64:128, :], 0.0)
            qhTs = [q0_, q0_]
            KhTs = [KhT, KhT]
        rc = tmp
        PT = [sb([128, TL], BF16, ph, "PTC%d" % i) for i in range(3 if latent else 2)]
        pti = 0
        scale = 96 ** -0.5
        tiles_q = list(range(0, L, TL))
        kchunks = list(range(0, LK, 512))

        def prep_K(h, sl, k0):
            n = min(512, LK - k0)
            p = ps()
            mm(p[0:64, 0:n], wv4[:, h, 0:64], ckvT[:, k0:k0 + n], True, True, [wukv, ckvT], [p])
            cp(KhTs[sl][0:64, k0:k0 + n], p[0:64, 0:n], [p], [KhTs[sl]], eng='scalar')

        def prep_q(h, sl, t0):
            qhT = qhTs[sl]
            p = ps()
            for kc_ in range(2):
                mm(p[0:96, 0:TL], wuq[:, kc_, h * 96:(h + 1) * 96], cqn[:, kc_, t0:t0 + TL], kc_ == 0, kc_ == 1, [wuq, cqn], [p], inc=(kc_ == 1))
            if latent:
                p2 = ps()
                for kc_ in range(2):
                    mm(p2[0:96, 0:TL], wuqs[:, kc_, h, :], cqn[:, kc_, t0:t0 + TL], kc_ == 0, kc_ == 1, [wuqs, cqn], [p2], inc=(kc_ == 1))
                cp(qhT[0:64, t0:t0 + TL], p[0:64, 0:TL], [p], [qhT], eng='scalar')
                rcC, rsC = rope_tiles(t0)
                tt(t1[64:96, 0:TL], p[64:96, 0:TL], rcC[64:96, 0:TL], ALU.mult, [p, rcC], [t1])
                tt(t2[64:96, 0:TL], p2[64:96, 0:TL], rsC[64:96, 0:TL], ALU.mult, [p2, rsC], [t2])
                tt(qhT[64:96, t0:t0 + TL], t1[64:96, 0:TL], t2[64:96, 0:TL], ALU.add, [t1, t2], [qhT])
            else:
                cp(qhT[0:96, t0:t0 + TL], p[0:96, 0:TL], [p], [qhT], eng='scalar')

        def pieces(h):
            sl = h % 2
            nt_ = len(tiles_q)
            out = []
            for i in range(nt_):
                ks = [k0 for j_, k0 in enumerate(kchunks) if (j_ * nt_) // len(kchunks) == i]
                out.append((lambda ks=ks, i=i: ([prep_K(h, sl, k0) for k0 in ks], prep_q(h, sl, tiles_q[i]))))
            return out

        for f_ in pieces(0):
            f_()
        for h in range(4):
            c, g = h // 2, h % 2
            KhT_h, qhT = KhTs[h % 2], qhTs[h % 2]
            nxt = pieces(h + 1) if h + 1 < 4 else []
            for ti, t0 in enumerate(tiles_q):
                if latent and ti < len(nxt):
                    nxt[ti]()
                pacc = ps(hold=True)
                pend = None
                for j in range(nkb):
                    p = ps()
                    mm(p[:, 0:TL], KhT_h[:, j * 128:(j + 1) * 128], qhT[:, t0:t0 + TL], True, True, [KhT_h, qhT], [p])
                    P_ = PT[pti % len(PT)]
                    pti += 1
                    act(P_[:, 0:TL], p[:, 0:TL], AF.Exp, [p], [P_], scale=scale)
                    if pend is not None:
                        mm(pacc[:, 0:TL], vaug[:, pend[0], c, g * 64:g * 64 + 128], pend[1][:, 0:TL], pend[0] == 0, False, [vaug, pend[1]], [pacc], inc=False)
                    pend = (j, P_)
                mm(pacc[:, 0:TL], vaug[:, pend[0], c, g * 64:g * 64 + 128], pend[1][:, 0:TL], pend[0] == 0, True, [vaug, pend[1]], [pacc], inc=True)
                softmax_norm(pacc, TL, g, mixT, c, t0, None, rc)
                release(pacc)
            if not latent:
                for f_ in nxt:
                    f_()
        outproj(l, grp, tb, L, mixT, 4, wo)
        if W is None and 'D' in MIX:
            prefetch(('D', l), [(2208, 2464)], l)

    def mixer_B(l, grp, tb, L, latent, nseq, ph):
        TL = 512
        nt = L // TL
        nb = L // 128
        cps = (L // nseq) // HC
        cpt = TL // HC
        cpb = 128 // HC
        mixT = sb([128, 2, L], BF16, ph, "mixB")
        vtok = sb([128, nb, 256], BF16, ph, "vtok")
        WX = get_w(('BX', l), [(512, 1024)], l)
        WY = get_w(('BY', l), [(1024, 1536)], l)
        wo = load_wout(l, 2)
        WG = sb([128, NK, 256], BF16, ph, "wgB")
        dma('gpsimd', WG[:], win_d[l, :, 1536:1792].rearrange("(k p) n -> p k n", p=128), w=[WG])
        for bi in range(nb):
            p = ps()
            for k in range(NK):
                mm(p[:, 0:256], hT[:, k, tb + bi * 128:tb + (bi + 1) * 128], WX[:, k, 256:512], k == 0, k == NK - 1, [hTk[k], WX], [p], inc=(k == NK - 1))
            cp(vtok[:, bi, :], p[:, 0:256], [p], [vtok], eng='scalar')
        hm = sb([128, 2, 128], BF16, ph, "hmask")
        dma('gpsimd', hm[:], hmask_d.rearrange("r s t -> s r t"), w=[hm])
        scm = sb([128, 512], F32, ph, "scanm")
        dma('sync', scm[:], scanmask_d, w=[scm])
        cm = sb([128, 4], F32, ph, "cmask")
        dma('sync', cm[:], cmask_d, w=[cm])
        oacc = sb([128, L], F32, ph, "oacc")
        Szero = sb([128, 64], F32, ph, "Szero")
        memset(Szero, Szero[:], 0.0)

        class X_:
            pass
        st = []
        for dr in range(2):
            X = X_()
            for nm in ("f_", "lf", "bb", "kk", "e1", "e2"):
                setattr(X, nm, sb([128, 512], F32, ph, "h%s%d" % (nm, dr)))
            X.qq = sb([128, 512], BF16, ph, "qq%d" % dr)
            X.kd = sb([128, 512], BF16, ph, "kd%d" % dr)
            X.kutok = sb([128, 4, 128], BF16, ph, "kutok%d" % dr)
            X.Sprev = sb([128, cpt, 64], BF16, ph, "Sprev%d" % dr)
            X.gam = sb([128, cpt], F32, ph, "gam%d" % dr)
            X.vexp = sb([128, 2, cpb, 64], BF16, ph, "vexp%d" % dr)
            X.At = [sb([128, 128], BF16, ph, "At%d_%d" % (dr, i)) for i in range(2)]
            X.Sall = [sb([128, cpt + 1, 64], F32, ph, "Sall%d" % dr)]
            st.append(X)

        for pr_ in range(2):
            touched = set()

            def stream(dr):
                X = st[dr]
                f_, lf, bb, kk, e1, e2 = X.f_, X.lf, X.bb, X.kk, X.e1, X.e2
                lb_ = lbv[:, l, dr, pr_:pr_ + 1]
                om_ = oml[:, l, dr, pr_:pr_ + 1]
                if latent:
                    dma('sync', X.Sall[0][:, 0, :], st_d[l, dr, 2 * pr_:2 * pr_ + 2].rearrange("h d v -> (h d) v"), w=[X.Sall[0]])
                tiles = list(range(nt)) if dr == 0 else list(range(nt - 1, -1, -1))
                for tidx, ti in enumerate(tiles):
                    SA = X.Sall[0]
                    if tidx > 0:
                        cp(SA[:, 0, :], SA[:, cpt, :], [SA], [SA])
                    jj = 0
                    t0 = ti * TL
                    pf = ps(hold=True)
                    proj_fm(ph, WY, (dr * 256 + pr_ * 128, dr * 256 + pr_ * 128 + 128), 128, tb + t0, TL, pf)
                    pq = ps(hold=True)
                    proj_fm(ph, WX, (pr_ * 128, pr_ * 128 + 128), 128, tb + t0, TL, pq)
                    yield
                    act(f_[:], pf[:], AF.Exp, [pf], [f_], scale=-1.0)
                    release(pf)
                    act(lf[:], f_[:], AF.Ln, [f_, oneT], [lf], bias=oneT[:, 0:1], scale=1.0)
                    act(f_[:], lf[:], AF.Exp, [lf], [f_], scale=-1.0)
                    yield
                    ts(f_[:], f_[:], om_, lb_, ALU.mult, ALU.add, [f_, oml, lbv], [f_])
                    act(lf[:], f_[:], AF.Ln, [f_], [lf])
                    ts(kk[:], f_[:], -1.0, 1.0, ALU.mult, ALU.add, [f_], [kk], eng='gpsimd')
                    yield
                    op('vector', lambda e: e.tensor_tensor_scan(out=bb[:], data0=scm[:], data1=lf[:], initial=0.0,
                                                                op0=ALU.mult, op1=ALU.add), r=[scm, lf], w=[bb])
                    b3 = bb[:].rearrange("p (n c) -> p n c", c=HC)
                    tot = b3[:, :, HC - 1:HC]
                    act(X.gam[:], b3[:, :, HC - 1], AF.Exp, [bb], [X.gam])
                    if dr == 1:
                        tt(e1[:].rearrange("p (n c) -> p n c", c=HC), tot.broadcast_to([128, cpt, HC]), b3, ALU.subtract, [bb], [e1])
                        tt(e2[:], bb[:], lf[:], ALU.subtract, [bb, lf], [e2])
                        yield
                        tt(bb[:], e1[:], lf[:], ALU.add, [e1, lf], [bb])
                    else:
                        tt(e2[:].rearrange("p (n c) -> p n c", c=HC), tot.broadcast_to([128, cpt, HC]), b3, ALU.subtract, [bb], [e2])
                    yield
                    act(e1[:], bb[:], AF.Exp, [bb], [e1])
                    tt(X.qq[:], pq[:], e1[:], ALU.mult, [pq, e1], [X.qq])
                    release(pq)
                    yield
                    act(f_[:], bb[:], AF.Exp, [bb], [f_], scale=-1.0)
                    tt(X.kd[:], kk[:], f_[:], ALU.mult, [kk, f_], [X.kd])
                    yield
                    act(e2[:], e2[:], AF.Exp, [e2], [e2])
                    tt(e2[:], kk[:], e2[:], ALU.mult, [kk, e2], [e2])
                    yield
                    for q in range(4):
                        p = ps()
                        tr(p[:, 0:128], e2[:, q * 128:(q + 1) * 128], [e2], [p])
                        cp(X.kutok[:, q, :], p[:, 0:128], [p], [X.kutok], eng='scalar')
                    yield
                    bqs = list(range(4)) if dr == 0 else list(range(3, -1, -1))
                    for bq in bqs:
                        bi = ti * 4 + bq
                        tt(X.vexp[:], vtok[:, bi, pr_ * 128:(pr_ + 1) * 128].rearrange("p (h v) -> p h v", h=2).unsqueeze(2).broadcast_to([128, 2, cpb, 64]),
                           cm[:, 0:cpb].unsqueeze(1).unsqueeze(3).broadcast_to([128, 2, cpb, 64]), ALU.mult, [vtok, cm], [X.vexp], eng='gpsimd')
                        pU = ps(hold=True)
                        for hh in range(2):
                            mm(pU[hh * 64:(hh + 1) * 64, 0:cpb * 64], X.kutok[:, bq, hh * 64:(hh + 1) * 64],
                               X.vexp[:, hh, :, :].rearrange("p n v -> p (n v)"), True, True, [X.kutok, X.vexp], [pU], inc=(hh == 1), tp=(0, hh * 64))
                        yield
                        chs = list(range(cpb)) if dr == 0 else list(range(cpb - 1, -1, -1))
                        for cj in chs:
                            nl = bq * cpb + cj
                            ng = ti * cpt + nl
                            first = (ng % cps == 0) if dr == 0 else (ng % cps == cps - 1)
                            last = (ng % cps == cps - 1) if dr == 0 else (ng % cps == 0)
                            if first and not latent:
                                cp(SA[:, jj, :], Szero[:], [Szero], [SA])
                            stt(SA[:, jj + 1, :], SA[:, jj, :], X.gam[:, nl:nl + 1], pU[:, cj * 64:(cj + 1) * 64], ALU.mult, ALU.add, [SA, X.gam, pU], [SA])
                            jj += 1
                            if last and not latent:
                                dma('sync', nst_d[ng // cps, l, dr, 2 * pr_:2 * pr_ + 2].rearrange("h d v -> (h d) v"), SA[:, jj, :], r=[SA])
                        yield
                        release(pU)
                    cp(X.Sprev[:], SA[:, 0:cpt, :], [SA], [X.Sprev], eng='scalar')
                    po = ps(hold=True)
                    for q in range(4):
                        bi = ti * 4 + q
                        cs = slice(q * 128, (q + 1) * 128)
                        for hh in range(2):
                            rs = slice(hh * 64, (hh + 1) * 64)
                            pA = ps()
                            mm(pA[:, 0:128], X.kd[rs, cs], X.qq[rs, cs], True, True, [X.kd, X.qq], [pA], tp=(hh * 64, 0))
                            A_ = X.At[hh]
                            tt(A_[:], pA[:, 0:128], hm[:, dr, :], ALU.mult, [pA, hm], [A_])
                            mm(po[rs, cs], vtok[:, bi, pr_ * 128 + hh * 64:pr_ * 128 + (hh + 1) * 64], A_[:], True, False,
                               [vtok, A_], [po], inc=False, tp=(0, hh * 64))
                            for cj in range(cpb):
                                nl = q * cpb + cj
                                lastm = (q == 3 and hh == 1 and cj == cpb - 1)
                                mm(po[rs, q * 128 + cj * HC:q * 128 + (cj + 1) * HC], X.Sprev[rs, (nl if dr == 0 else cpt - 1 - nl), :], X.qq[rs, q * 128 + cj * HC:q * 128 + (cj + 1) * HC],
                                   False, cj == cpb - 1, [X.Sprev, X.qq], [po], inc=(lastm or cj == cpb - 1), tp=(hh * 64, hh * 64))
                        yield
                    if ti not in touched:
                        touched.add(ti)
                        cp(oacc[:, t0:t0 + TL], po[:], [po], [oacc], eng='scalar')
                    else:
                        tt(oacc[:, t0:t0 + TL], oacc[:, t0:t0 + TL], po[:], ALU.add, [oacc, po], [oacc])
                    release(po)
                    yield

            gens = [stream(0), stream(1)]
            alive = [True, True]
            while any(alive):
                for gi in range(2):
                    if alive[gi]:
                        try:
                            next(gens[gi])
                        except StopIteration:
                            alive[gi] = False
            X = st[0]
            Y = st[1]
            for t0 in range(0, L, TL):
                Z = X if (t0 // TL) % 2 == 0 else Y
                act(Z.e1[:], oacc[:, t0:t0 + TL], AF.Square, [oacc], [Z.e1])
                cp(Z.kd[:], Z.e1[:], [Z.e1], [Z.kd])
                p = ps()
                mm(p[:], bd64[:], Z.kd[:], True, True, [bd64, Z.kd], [p])
                rstd_of(Z.e2[:], p[:], [p], [Z.e2])
                tt(Z.e1[:], oacc[:, t0:t0 + TL], Z.e2[:], ALU.mult, [oacc, Z.e2], [Z.e1])
                pg = ps()
                proj_fm(ph, WG, (pr_ * 128, pr_ * 128 + 128), 128, tb + t0, TL, pg)
                act(Z.f_[:], pg[:], AF.Exp, [pg], [Z.f_], scale=-1.0)
                act(Z.lf[:], Z.f_[:], AF.Ln, [Z.f_, oneT], [Z.lf], bias=oneT[:, 0:1], scale=1.0)
                act(Z.lf[:], Z.lf[:], AF.Exp, [Z.lf], [Z.lf], scale=-1.0)
                tt(Z.f_[:], pg[:], Z.lf[:], ALU.mult, [pg, Z.lf], [Z.f_])
                stt(mixT[:, pr_, t0:t0 + TL], Z.e1[:], hnT[:, l:l + 1], Z.f_[:], ALU.mult, ALU.mult, [Z.e1, hnT, Z.f_], [mixT])
        outproj(l, grp, tb, L, mixT, 2, wo)
        if latent and 'A' in MIX:
            prefetch(('A', l), [(0, 512)], l)

    def mixer_layer(T, l, grp, latent, nseq):
        L = T // nseq
        mod_consume()
        if l == 0:
            if mod_done[0] < 18:
                for _ in range(18 - mod_done[0]):
                    mod_step(1)
        else:
            for _ in range(36 - mod_done[0]):
                mod_step(1)
        S.mark("mixnorm l%d g%d" % (l, grp))
        with ExitStack() as ph:
            normmod(T, (lambda k: coefA[:, l, 1, grp, k:k + 1], [coefA]), (lambda k: shiftC[:, l, 1, grp, k:k + 1], [shiftC]),
                    lambda k, t0: hT[:, k, t0:t0 + 512], hTk, ph)
        S.barrier()
        if 'B' in MIX:
            S.mark("mixB l%d g%d s0" % (l, grp))
            with ExitStack() as ph:
                mixer_B(l, grp, 0, T, latent, nseq, ph)
            S.barrier()
        with ExitStack() as wph:
            Ws = {'A': None, 'C': None, 'D': None}
            if nseq > 1:
                def wtile(c0, c1, nm):
                    t = sb([128, NK, c1 - c0], BF16, wph, nm)
                    dma('gpsimd', t[:], win_d[l, :, c0:c1].rearrange("(k p) n -> p k n", p=128), w=[t])
                    return t

                def wo_tile(c0, nm):
                    return load_wout(l, c0)
                _wa = wtile(0, 512, "WAp")
                _w2 = sb([128, NK, 256], BF16, wph, "W2p")
                build_kdup(_wa, _w2)
                Ws['A'] = (_wa, _w2, wo_tile(0, "WoA"))
                Ws['C'] = (wtile(1792, 2208, "WCp"),) + load_wu(l, wph) + (wo_tile(4, "WoC"),)
                Ws['D'] = (wtile(2208, 2464, "WDp"), load_pw(l, wph), wo_tile(6, "WoD"))
            for name, fn in (('A', mixer_A), ('C', mixer_C), ('D', mixer_D)):
                if name not in MIX:
                    continue
                S.mark("mix%s l%d g%d s0" % (name, l, grp))
                with ExitStack() as ph:
                    for seq in range(nseq):
                        fn(l, grp, seq * L, L, latent, seq, ph, Ws[name])
                S.barrier()

    def run_pass(x_d, y_d, T, grp, latent, nseq):
        load_xT(x_d, T)
        for l in range(2):
            ffn(T, l, 0, grp, 0)
            mixer_layer(T, l, grp, latent, nseq)
            ffn(T, l, 2, grp, 1)
        final_out(T, y_d)

    if flags.get('sample', True):
        run_pass(xs_d, ys_d, 2048, 1, True, 1)
    if flags.get('prompt', True):
        run_pass(xp_d, yp_d, 1024, 0, False, 4)
    S.mark("end")
    S.finish()
    es.close()
    return nc, S


def _rope_tables(dim, nrows_tab, row_slices):
    n_freq = dim // 4
    half = dim // 2
    inv = (10000.0 ** (-np.arange(n_freq, dtype=np.float32) / n_freq)).astype(np.float32)
    t = np.arange(2048)
    row_id = (t // 64).astype(np.float32)
    col_id = (t % 64).astype(np.float32)
    ang = np.concatenate([row_id[:, None] * inv, col_id[:, None] * inv], axis=-1).astype(np.float32)
    cos = np.cos(ang).astype(np.float32)
    sin = np.sin(ang).astype(np.float32)
    C = np.zeros((128, 2048), np.float32)
    Sg = np.zeros((128, 2048), np.float32)
    for (r0, n) in row_slices:
        for r in range(n):
            d = r % dim
            j = d % half
            C[r0 + r] = cos[:, j]
            Sg[r0 + r] = sin[:, j] * (-1.0 if d < half else 1.0)
    return C, Sg


_CONST = {}


def _consts():
    if _CONST:
        return _CONST
    c = {}
    c['ident'] = np.eye(128, dtype=np.float32)
    c['ropeA_c'], c['ropeA_s'] = _rope_tables(64, 128, [(0, 128)])
    c['ropeC_c'], c['ropeC_s'] = _rope_tables(32, 128, [(0, 32), (64, 32)])
    b = np.arange(128)[:, None]
    a = np.arange(128)[None, :]
    m = np.zeros((128, 384), np.float32)
    m[:, 0:128] = np.where(b <= a, 0.0, -240000.0)
    m[:, 256:384] = np.where(a <= b, 0.0, -240000.0)
    c['winmask'] = m
    s = np.arange(128)[:, None]
    t = np.arange(128)[None, :]
    same = (s // HC) == (t // HC)
    c['hmask'] = np.stack([(same & (s <= t)), (same & (s >= t))]).astype(np.float32)
    sc = np.ones((128, 512), np.float32)
    sc[:, ::HC] = 0.0
    c['scanmask'] = sc
    c['cmask'] = (np.arange(128)[:, None] // HC == np.arange(4)[None, :]).astype(np.float32)
    pm = np.zeros((4, 128, 128), np.float32)
    for m in range(128):
        pm[0, m + 32 if (m % 64) < 32 else m - 32, m] = 1.0
        if m < 32 or 64 <= m < 96:
            pm[1, m + 16 if (m % 32) < 16 else m - 16, m] = 1.0
        else:
            pm[1, m, m] = 1.0
        pm[2, m % 64, m] = 1.0
        pm[3, 64 + m % 64, m] = 1.0
    c['permm'] = pm
    for nm, L in (('invcntS', 2048), ('invcntP', 256)):
        inv = np.zeros((128, 2, L), np.float32)
        pos = np.arange(L)
        for g, w in enumerate(POOL_WINDOWS):
            lo = np.clip(pos - w // 2, 0, L)
            hi = np.clip(pos - w // 2 + w, 0, L)
            inv[(g % 2) * 64:(g % 2) * 64 + 64, g // 2, :] = (1.0 / (hi - lo).astype(np.float32))[None, :]
        c[nm] = inv
    _CONST.update(c)
    return _CONST


_NC = {}


def kernel(x_prompt, x_sample, c, c_ctx, cache_attn_k, cache_attn_v, state_hgrn, cache_mla_ckv,
           cache_mla_krope, w_ada, b_ada, norm_sub, w_ffn_gate, w_ffn_up, w_ffn_down, w_in, w_out,
           attn_sink, hgrn_lb_logits, hgrn_out_norm, mla_q_norm, mla_kv_norm, mla_w_uq, mla_w_ukv,
           pool_w, pool_scale, final_norm, _flags=None):
    f32 = lambda a: np.ascontiguousarray(np.asarray(a, dtype=np.float32))
    key = repr(_flags)
    if key not in _NC:
        _NC[key] = build(_flags)[0]
    nc = _NC[key]
    cs = _consts()
    x_prompt, x_sample, c, c_ctx = f32(x_prompt), f32(x_sample), f32(c), f32(c_ctx)
    b_ada, norm_sub = f32(b_ada), f32(norm_sub)
    shared = {
        "hnorm": f32(hgrn_out_norm), "sink": f32(attn_sink), "w_ada": f32(w_ada), "w_gate": f32(w_ffn_gate), "w_up": f32(w_ffn_up),
        "w_down": f32(w_ffn_down), "w_in": f32(w_in), "w_out": f32(w_out), "w_uq": f32(mla_w_uq), "w_ukv": f32(mla_w_ukv),
        "pool_w": f32(pool_w),
    }
    shared.update(cs)
    vecA = np.concatenate([b_ada[0].reshape(72, 128), norm_sub.reshape(48, 128)], axis=0)
    in_maps = []
    for b in range(8):
        vecB = np.concatenate([b_ada[1].reshape(72, 128), c_ctx.reshape(8, 128), c[b].reshape(8, 128),
                               f32(mla_q_norm).reshape(4, 128), f32(mla_kv_norm).reshape(2, 128), f32(pool_scale).reshape(4, 128),
                               f32(hgrn_lb_logits).reshape(8, 128), f32(final_norm).reshape(8, 128)], axis=0)
        m = dict(shared)
        m.update({
            "xs": x_sample[b], "xp": x_prompt[4 * b:4 * b + 4].reshape(1024, 1024), "vecA": vecA, "vecB": np.ascontiguousarray(vecB),
            "cache_k": f32(cache_attn_k[b]), "cache_v": f32(cache_attn_v[b]), "state": f32(state_hgrn[b]),
            "cache_ckv": f32(cache_mla_ckv[b]), "cache_kr": f32(cache_mla_krope[b]),
        })
        in_maps.append(m)
    res = run_bass_kernel_spmd(nc, in_maps, core_ids=list(range(8)))
    R = res.results
    y_p = np.concatenate([r["y_p"].reshape(4, 256, 1024) for r in R], axis=0)
    y_s = np.stack([r["y_s"] for r in R], axis=0)
    nk = np.concatenate([r["new_k"] for r in R], axis=0)
    nv = np.concatenate([r["new_v"] for r in R], axis=0)
    nst = np.concatenate([r["new_st"] for r in R], axis=0)
    nckv = np.concatenate([r["new_ckv"] for r in R], axis=0)
    nkr = np.concatenate([r["new_kr"] for r in R], axis=0)
    return (y_p.astype(np.float32), y_s.astype(np.float32), nk.astype(np.float32), nv.astype(np.float32),
            nst.astype(np.float32), nckv.astype(np.float32), nkr.astype(np.float32))
```
